# Optimizing a Trainium2 kernel written in Bass

```python
import jax, jax.numpy as jnp
from jax import lax
import numpy as np

D_MODEL = 1024
BATCH = 16
SEQ = 2048
DEPTH = 1

CHUNK = 64
Q_BLOCK = 128
PLE_DIM = 256
EPS = 1e-6

MLA_HEADS = 8
QK_NOPE = 64
QK_ROPE = 32
V_HEAD = 64
Q_LORA = 256
KV_LORA = 128
MLA_WIDTH = MLA_HEADS * V_HEAD
ROPE_THETA = 10000.0

LRU_WIDTH = D_MODEL - MLA_WIDTH
LRU_BLOCKS = 8
LRU_BLOCK = LRU_WIDTH // LRU_BLOCKS
LRU_CONV = 4
LRU_C = 8.0

D_FF = 2816
FFN_CONV = 3

IN_SPLITS = (Q_LORA, Q_LORA + KV_LORA, Q_LORA + KV_LORA + QK_ROPE,
             Q_LORA + KV_LORA + QK_ROPE + LRU_WIDTH)
IN_COLS = Q_LORA + KV_LORA + QK_ROPE + 2 * LRU_WIDTH

kernel_name = 'hybrid_mla_rglru_block'


def _rms_norm(x, g):
    xf = x.astype(jnp.float32)
    y = xf * lax.rsqrt(jnp.mean(xf * xf, axis=-1, keepdims=True) + EPS)
    return (y * g.astype(jnp.float32)).astype(x.dtype)


def _causal_dwconv(x, w, b):
    k, c = w.shape
    y = lax.conv_general_dilated(
        x, w.astype(x.dtype).reshape(k, 1, c), window_strides=(1,),
        padding=[(k - 1, 0)], dimension_numbers=('NWC', 'WIO', 'NWC'),
        feature_group_count=c)
    return y + b.astype(x.dtype)


def _rope(x, positions):
    half = x.shape[-1] // 2
    inv_freq = ROPE_THETA ** (-jnp.arange(half, dtype=jnp.float32) / half)
    ang = positions.astype(jnp.float32)[..., None] * inv_freq
    ang = ang.reshape(ang.shape[:2] + (1,) * (x.ndim - 3) + (half,))
    cos, sin = jnp.cos(ang), jnp.sin(ang)
    xf = x.astype(jnp.float32)
    x1, x2 = xf[..., :half], xf[..., half:]
    return jnp.concatenate([x1 * cos - x2 * sin, x2 * cos + x1 * sin], axis=-1).astype(x.dtype)


def _mla(cq, ckv, kr, positions, g_cq, w_uq, g_ckv, w_ukv, g_qn, g_kn):
    B, S, _ = cq.shape
    q = (_rms_norm(cq, g_cq) @ w_uq).reshape(B, S, MLA_HEADS, QK_NOPE + QK_ROPE)
    kv = (_rms_norm(ckv, g_ckv) @ w_ukv).reshape(B, S, MLA_HEADS, QK_NOPE + V_HEAD)
    k_nope, v = kv[..., :QK_NOPE], kv[..., QK_NOPE:]
    k_rope = jnp.broadcast_to(kr[:, :, None, :], (B, S, MLA_HEADS, QK_ROPE))
    k = jnp.concatenate([k_nope, k_rope], axis=-1)
    q = _rms_norm(q, g_qn)
    k = _rms_norm(k, g_kn)
    q = jnp.concatenate([q[..., :QK_NOPE], _rope(q[..., QK_NOPE:], positions)], axis=-1)
    k = jnp.concatenate([k[..., :QK_NOPE], _rope(k[..., QK_NOPE:], positions)], axis=-1)
    scale = (QK_NOPE + QK_ROPE) ** -0.5
    qf = q.astype(jnp.float32) * scale
    kf = k.astype(jnp.float32)
    vf = v.astype(jnp.float32)
    outs = []
    for qb in range(S // Q_BLOCK):
        q0 = qb * Q_BLOCK
        kend = q0 + Q_BLOCK
        s = jnp.einsum('bqhd,bkhd->bhqk', qf[:, q0:kend], kf[:, :kend])
        q_chunk = (q0 + jnp.arange(Q_BLOCK)) // CHUNK
        k_chunk = jnp.arange(kend) // CHUNK
        mask = k_chunk[None, :] <= q_chunk[:, None]
        s = jnp.where(mask[None, None], s, -1e30)
        pr = jax.nn.softmax(s, axis=-1)
        outs.append(jnp.einsum('bhqk,bkhd->bqhd', pr, vf[:, :kend]))
    o = jnp.concatenate(outs, axis=1).reshape(B, S, MLA_WIDTH)
    return o.astype(cq.dtype)


def _lru_combine(left, right):
    a1, b1 = left
    a2, b2 = right
    return a1 * a2, a2 * b1 + b2


def _rglru_block(xl, gl, w_conv, b_conv, w_r, b_r, w_i, b_i, lam):
    B, S, W = xl.shape
    xc = _causal_dwconv(xl, w_conv, b_conv)
    xf = xc.astype(jnp.float32)
    xb = xf.reshape(B, S, LRU_BLOCKS, LRU_BLOCK)
    r = jax.nn.sigmoid(jnp.einsum('bsnc,ncd->bsnd', xb, w_r.astype(jnp.float32)).reshape(B, S, W)
                       + b_r.astype(jnp.float32))
    i = jax.nn.sigmoid(jnp.einsum('bsnc,ncd->bsnd', xb, w_i.astype(jnp.float32)).reshape(B, S, W)
                       + b_i.astype(jnp.float32))
    log_a = -LRU_C * r * jax.nn.softplus(-lam.astype(jnp.float32))
    a = jnp.exp(log_a)
    bterm = jnp.sqrt(-jnp.expm1(2.0 * log_a)) * (i * xf)
    _, h = lax.associative_scan(_lru_combine, (a, bterm), axis=1)
    y = h * jax.nn.gelu(gl.astype(jnp.float32), approximate=True)
    return y.astype(xl.dtype)


def setup_inputs(seed: int = 0) -> dict:
    key = jax.random.key(seed)
    ks = iter(jax.random.split(key, 40))
    f32 = jnp.float32

    def nrm(shape, scale):
        return jax.random.normal(next(ks), shape, f32) * scale

    def gain(shape):
        return 1.0 + 0.02 * jax.random.normal(next(ks), shape, f32)

    x = jax.random.normal(next(ks), (BATCH, SEQ, D_MODEL), f32)
    p = jax.random.normal(next(ks), (DEPTH, BATCH, SEQ, PLE_DIM), f32)
    offset = jax.random.randint(next(ks), (BATCH, 1), 0, 4096, dtype=jnp.int32)
    positions = offset + jnp.arange(SEQ, dtype=jnp.int32)[None, :]

    u = jax.random.uniform(next(ks), (DEPTH, LRU_WIDTH), f32, minval=0.9, maxval=0.999)
    s_lam = u ** (1.0 / LRU_C)
    lru_lambda = jnp.log(s_lam) - jnp.log1p(-s_lam)

    return {
        'x': x,
        'p': p,
        'positions': positions,
        'g_mix': gain((DEPTH, D_MODEL)),
        'w_in': nrm((DEPTH, D_MODEL, IN_COLS), D_MODEL ** -0.5),
        'g_cq': gain((DEPTH, Q_LORA)),
        'w_uq': nrm((DEPTH, Q_LORA, MLA_HEADS * (QK_NOPE + QK_ROPE)), Q_LORA ** -0.5),
        'g_ckv': gain((DEPTH, KV_LORA)),
        'w_ukv': nrm((DEPTH, KV_LORA, MLA_HEADS * (QK_NOPE + V_HEAD)), KV_LORA ** -0.5),
        'g_qn': gain((DEPTH, QK_NOPE + QK_ROPE)),
        'g_kn': gain((DEPTH, QK_NOPE + QK_ROPE)),
        'w_lru_conv': nrm((DEPTH, LRU_CONV, LRU_WIDTH), LRU_CONV ** -0.5),
        'b_lru_conv': nrm((DEPTH, LRU_WIDTH), 0.02),
        'w_lru_r': nrm((DEPTH, LRU_BLOCKS, LRU_BLOCK, LRU_BLOCK), LRU_BLOCK ** -0.5),
        'b_lru_r': nrm((DEPTH, LRU_WIDTH), 0.02),
        'w_lru_i': nrm((DEPTH, LRU_BLOCKS, LRU_BLOCK, LRU_BLOCK), LRU_BLOCK ** -0.5),
        'b_lru_i': nrm((DEPTH, LRU_WIDTH), 0.02),
        'lru_lambda': lru_lambda,
        'g_attn_out': gain((DEPTH, MLA_WIDTH)),
        'g_lru_out': gain((DEPTH, LRU_WIDTH)),
        'w_out': nrm((DEPTH, D_MODEL, D_MODEL), D_MODEL ** -0.5),
        'g_ffn': gain((DEPTH, D_MODEL)),
        'w_up': nrm((DEPTH, D_MODEL, 2 * D_FF), D_MODEL ** -0.5),
        'w_ffn_conv': nrm((DEPTH, FFN_CONV, 2 * D_FF), FFN_CONV ** -0.5),
        'b_ffn_conv': nrm((DEPTH, 2 * D_FF), 0.02),
        'w_down': nrm((DEPTH, D_FF, D_MODEL), D_FF ** -0.5),
        'g_ple': gain((DEPTH, D_MODEL)),
        'w_ple_gate': nrm((DEPTH, D_MODEL, D_MODEL), D_MODEL ** -0.5),
        'b_ple_gate': nrm((DEPTH, D_MODEL), 0.02),
        'w_ple': nrm((DEPTH, PLE_DIM, D_MODEL), PLE_DIM ** -0.5),
    }


def reference(x, p, positions, g_mix, w_in, g_cq, w_uq, g_ckv, w_ukv, g_qn, g_kn,
              w_lru_conv, b_lru_conv, w_lru_r, b_lru_r, w_lru_i, b_lru_i, lru_lambda,
              g_attn_out, g_lru_out, w_out, g_ffn, w_up, w_ffn_conv, b_ffn_conv, w_down,
              g_ple, w_ple_gate, b_ple_gate, w_ple):
    for l in range(DEPTH):
        hn = _rms_norm(x, g_mix[l])
        proj = hn @ w_in[l]
        cq, ckv, kr, xl, gl = jnp.split(proj, IN_SPLITS, axis=-1)
        attn = _mla(cq, ckv, kr, positions, g_cq[l], w_uq[l], g_ckv[l], w_ukv[l],
                    g_qn[l], g_kn[l])
        lru = _rglru_block(xl, gl, w_lru_conv[l], b_lru_conv[l], w_lru_r[l], b_lru_r[l],
                           w_lru_i[l], b_lru_i[l], lru_lambda[l])
        mixed = jnp.concatenate([_rms_norm(attn, g_attn_out[l]),
                                 _rms_norm(lru, g_lru_out[l])], axis=-1)
        x = x + mixed @ w_out[l]
        hf = _rms_norm(x, g_ffn[l])
        up = _causal_dwconv(hf @ w_up[l], w_ffn_conv[l], b_ffn_conv[l])
        gte, val = up[..., :D_FF], up[..., D_FF:]
        x = x + (jax.nn.gelu(gte, approximate=True) * val) @ w_down[l]
        gate = jax.nn.sigmoid(_rms_norm(x, g_ple[l]) @ w_ple_gate[l] + b_ple_gate[l])
        x = x + gate * (p[l] @ w_ple[l])
    return x
```

```python
import math
import numpy as np
import ml_dtypes
from contextlib import ExitStack
import concourse.bass as bass
import concourse.mybir as mybir
from concourse.bass_utils import run_bass_kernel_spmd

F32 = mybir.dt.float32
BF16 = mybir.dt.bfloat16
I32 = mybir.dt.int32
AF = mybir.ActivationFunctionType
ALU = mybir.AluOpType
AX = mybir.AxisListType

NCORES = 8
D = 1024
S = 2048
NT = S // 128
TOK = 2 * S
H = 8
EPS = 1e-6
DFF = 2816
NCH = DFF // 128
WIN = 256

C_GMIX, C_GFFN, C_GPLE, C_GCQ, C_GCKV, C_GOUT = 0, 8, 16, 24, 26, 27
C_LCW, C_LCB, C_BR, C_BI, C_LAM = 35, 51, 55, 59, 63
C_FCW, C_FCB = 67, 67 + 132
NCOL = C_FCB + 44
R_GQ, R_GK, R_IF = 0, 96, 192
NROW = 208


class Tok:
    __slots__ = ("q", "sem", "val")

    def __init__(self, q, sem, val):
        self.q, self.sem, self.val = q, sem, val


class Buf:
    def __init__(self, name=""):
        self.name = name
        self.w = None
        self.r = {}


class TB:
    def __init__(self, t, name=""):
        self.t = t
        self.b = Buf(name)


class Q:
    def __init__(self, name, sem):
        self.name, self.sem = name, sem
        self.n = 0
        self.ops = []
        self.waited = {}
        self.pend_r = []
        self.pend_w = []


class DSem:
    def __init__(self, h):
        self.h = h
        self.val = 0


def _b(x):
    return x.b if isinstance(x, TB) else x


class Kern:
    def __init__(self, nc, es):
        self.nc = nc
        self.es = es
        self.q = {}
        for n in ("pe", "dve", "act", "pool", "sp"):
            self.q[n] = Q(n, es.enter_context(nc.semaphore("q_" + n)))
        self.nsem = 5

    def dsem(self, name):
        self.nsem += 1
        return DSem(self.es.enter_context(self.nc.semaphore(name)))

    def sb(self, es, name, shape, dt):
        return TB(es.enter_context(self.nc.sbuf_tensor("s_" + name, list(shape), dt)), name)

    def _waits(self, q, r, w):
        needs = {}

        def need(t):
            if t is None:
                return
            k = id(t.sem)
            if k not in needs or needs[k][1] < t.val:
                needs[k] = (t.sem, t.val)

        for b in r:
            need(_b(b).w)
        for b in w:
            b = _b(b)
            if b.w is not None and b.w.q is not q:
                need(b.w)
            for t in b.r.values():
                if t.q is not q:
                    need(t)
        waits = []
        for k, (sem, val) in needs.items():
            if q.waited.get(k, 0) < val:
                q.waited[k] = val
                waits.append((sem, val))
        return waits

    @staticmethod
    def _stamp(tok, r, w):
        for b in r:
            b = _b(b)
            k = id(tok.sem)
            o = b.r.get(k)
            if o is None or o.val < tok.val:
                b.r[k] = tok
        for b in w:
            b = _b(b)
            b.w = tok
            b.r = {}

    def op(self, qn, f, r=(), w=(), sig=True):
        q = self.q[qn]
        waits = self._waits(q, r, w)
        if sig:
            q.n += 1
            tok = Tok(q, q.sem, q.n)
            q.ops.append((waits, f, (q.sem, 1)))
            self._stamp(tok, list(r) + q.pend_r, list(w) + q.pend_w)
            q.pend_r, q.pend_w = [], []
            return tok
        q.ops.append((waits, f, None))
        q.pend_r += list(r)
        q.pend_w += list(w)
        return None

    def dma(self, qn, out, in_, sem, r=(), w=()):
        q = self.q[qn]
        assert not q.pend_r and not q.pend_w
        waits = self._waits(q, r, w)
        sem.val += 16
        tok = Tok(None, sem.h, sem.val)
        q.ops.append((waits, lambda e: e.dma_start(out=out, in_=in_), (sem.h, 16)))
        self._stamp(tok, r, w)
        return tok

    def barrier(self, extra=()):
        for q in self.q.values():
            waits = []
            for oq in self.q.values():
                if oq is q or oq.n == 0:
                    continue
                k = id(oq.sem)
                if q.waited.get(k, 0) < oq.n:
                    q.waited[k] = oq.n
                    waits.append((oq.sem, oq.n))
            for t in extra:
                k = id(t.sem)
                if q.waited.get(k, 0) < t.val:
                    q.waited[k] = t.val
                    waits.append((t.sem, t.val))
            if waits:
                q.ops.append((waits, None, None))

    def flush(self, final_toks=()):
        nc = self.nc
        if final_toks:
            q = self.q["sp"]
            needs = {}
            for t in final_toks:
                k = id(t.sem)
                if k not in needs or needs[k][1] < t.val:
                    needs[k] = (t.sem, t.val)
            q.ops.append((list(needs.values()), None, None))
        with nc.Block() as blk:
            for qn, deco in (("sp", blk.sync), ("act", blk.scalar), ("pool", blk.gpsimd),
                             ("dve", blk.vector), ("pe", blk.tensor)):
                qq = self.q[qn]
                assert not qq.pend_r and not qq.pend_w, qn
                ops = qq.ops
                qq.ops = []

                def body(e, ops=ops):
                    for waits, f, inc in ops:
                        for sem, val in waits:
                            e.wait_ge(sem, val)
                        if f is None:
                            continue
                        ins = f(e)
                        if inc is not None:
                            ins.then_inc(inc[0], inc[1])

                deco(body)

    def mm(self, out, lhsT, rhs, start, stop, r=(), w=(), sig=False):
        return self.op("pe", lambda e: e.matmul(out, lhsT, rhs, start=start, stop=stop), r, w, sig)

    def tr(self, out, in_, ident, r=(), w=(), sig=False):
        return self.op("pe", lambda e: e.transpose(out, in_, ident), r, w, sig)

    def act(self, out, in_, func, r=(), w=(), bias=None, scale=None, accum=None, qn="act"):
        kw = {}
        if bias is not None:
            kw["bias"] = bias
        if scale is not None:
            kw["scale"] = scale
        if accum is not None:
            kw["accum_out"] = accum
        return self.op(qn, lambda e: e.activation(out=out, in_=in_, func=func, **kw), r, w)

    def tt(self, qn, out, in0, in1, op, r=(), w=()):
        return self.op(qn, lambda e: e.tensor_tensor(out=out, in0=in0, in1=in1, op=op), r, w)

    def ts(self, qn, out, in0, s1, s2, op0, op1, r=(), w=()):
        return self.op(qn, lambda e: e.tensor_scalar(out=out, in0=in0, scalar1=s1, scalar2=s2,
                                                     op0=op0, op1=op1), r, w)

    def stt(self, out, in0, scalar, in1, op0, op1, r=(), w=()):
        return self.op("dve", lambda e: e.scalar_tensor_tensor(out=out, in0=in0, scalar=scalar, in1=in1,
                                                               op0=op0, op1=op1), r, w)

    def copy(self, qn, out, in_, r=(), w=()):
        if qn == "act":
            return self.op(qn, lambda e: e.copy(out=out, in_=in_), r, w)
        return self.op(qn, lambda e: e.tensor_copy(out=out, in_=in_), r, w)

    def memset(self, qn, ap, val, w=()):
        return self.op(qn, lambda e: e.memset(ap, val), (), w)


def build_nc():
    nc = bass.Bass("TRN2", target_bir_lowering=False)

    def din(name, shape, dt=F32):
        return nc.dram_tensor(name, list(shape), dt, kind="ExternalInput").ap()

    x_d = din("x", [TOK, D])
    p_d = din("p", [TOK, 256])
    pos_d = din("posT", [128, 2 * NT], I32)
    w_in_d = din("w_in", [D, 1440])
    w_uq_d = din("w_uq", [256, 768])
    w_ukv_d = din("w_ukv", [128, 1024])
    w_out_d = din("w_out", [D, D])
    w_up_d = din("w_up", [D, 2 * DFF])
    w_down_d = din("w_down", [DFF, D])
    w_pg_d = din("w_pg", [D, D])
    w_ple_d = din("w_ple", [256, D])
    w_r_d = din("w_r", [512, 64])
    w_i_d = din("w_i", [512, 64])
    cols_d = din("cols", [128, NCOL])
    rows_d = din("rows", [1, NROW])
    bple_d = din("bple", [1, D])
    ident_d = din("ident", [128, 128], BF16)
    out_d = nc.dram_tensor("out", [TOK, D], F32, kind="ExternalOutput").ap()
    x1_d = nc.dram_tensor("x1s", [TOK, D], F32, kind="Internal").ap()
    x2_d = nc.dram_tensor("x2s", [TOK, D], F32, kind="Internal").ap()

    with ExitStack() as g:
        K = Kern(nc, g)
        banks = [TB(g.enter_context(nc.psum_tensor("ps%d" % i, [128, 512], F32)), "ps%d" % i) for i in range(8)]

        class Ring:
            def __init__(self, items):
                self.items = items
                self.i = 0

            def next(self):
                it = self.items[self.i % len(self.items)]
                self.i += 1
                return it

        cols = K.sb(g, "cols", [128, NCOL], F32)
        rows = K.sb(g, "rows", [128, NROW], F32)
        ident = K.sb(g, "ident", [128, 128], BF16)
        nhalf = K.sb(g, "nhalf", [128, 16], F32)
        cst_sem = K.dsem("cst")
        K.dma("sp", cols.t[:], cols_d[:, :], cst_sem, w=[cols])
        K.dma("sp", rows.t[:], rows_d.partition_broadcast(128)[:, 0, :], cst_sem, w=[rows])
        K.dma("sp", ident.t[:], ident_d[:, :], cst_sem, w=[ident])
        ftok = Tok(None, cst_sem.h, cst_sem.val)
        for tb in (cols, rows, ident):
            tb.b.w = ftok
        K.memset("pool", nhalf.t[:], -0.5, w=[nhalf])

        def col(c0, n=1):
            return cols.t[:, c0:c0 + n]

        def rstd(out_tb, out_ap, ss_tb, ss_ap, nfeat, n):
            K.ts("pool", ss_ap, ss_ap, 1.0 / nfeat, EPS, ALU.mult, ALU.add, r=[ss_tb], w=[ss_tb])
            K.tt("pool", out_ap, ss_ap, nhalf.t[:, 0:n], ALU.pow, r=[ss_tb, nhalf], w=[out_tb])

        with ExitStack() as es:
            Win = K.sb(es, "Win", [128, 8, 1440], BF16)
            Wuq = K.sb(es, "Wuq", [128, 2, 768], BF16)
            Wukv = K.sb(es, "Wukv", [128, 1024], BF16)
            Wout = K.sb(es, "Wout", [128, 8, 1024], BF16)
            Wr = K.sb(es, "Wr", [128, 4, 128], BF16)
            Wi = K.sb(es, "Wi", [128, 4, 128], BF16)
            stage = [K.sb(es, "stg%d" % i, [128, 512], F32) for i in range(2)]
            stage_sem = [K.dsem("stg%d" % i) for i in range(2)]
            stg_i = [0]
            lxc = K.sb(es, "lxc", [128, 512], F32)
            lxb = K.sb(es, "lxb", [128, 512], BF16)
            lr_ = K.sb(es, "lr", [128, 512], F32)
            li_ = K.sb(es, "li", [128, 512], F32)
            lm_ = K.sb(es, "lm", [128, 512], F32)

            def load_cast(dst_tb, dst_ap, src_ap, n, scale_col=None):
                i = stg_i[0] % 2
                stg_i[0] += 1
                st_ = stage[i]
                K.dma("sp", st_.t[:, 0:n], src_ap, stage_sem[i], w=[st_])
                if scale_col is None:
                    K.copy("act", dst_ap, st_.t[:, 0:n], r=[st_], w=[dst_tb])
                else:
                    K.act(dst_ap, st_.t[:, 0:n], AF.Copy, r=[st_, cols], w=[dst_tb], scale=scale_col)

            w_in_v = w_in_d.rearrange("(k p) n -> p k n", p=128)
            for k in range(8):
                for h0 in (0, 480, 960):
                    load_cast(Win, Win.t[:, k, h0:h0 + 480], w_in_v[:, k, h0:h0 + 480], 480, col(C_GMIX + k))
            w_uq_v = w_uq_d.rearrange("(k p) n -> p k n", p=128)
            for k in range(2):
                for h0 in (0, 384):
                    load_cast(Wuq, Wuq.t[:, k, h0:h0 + 384], w_uq_v[:, k, h0:h0 + 384], 384, col(C_GCQ + k))
            for h0 in (0, 512):
                load_cast(Wukv, Wukv.t[:, h0:h0 + 512], w_ukv_d[:, h0:h0 + 512], 512, col(C_GCKV))
            w_out_v = w_out_d.rearrange("(k p) n -> p k n", p=128)
            for k in range(8):
                for h0 in (0, 512):
                    load_cast(Wout, Wout.t[:, k, h0:h0 + 512], w_out_v[:, k, h0:h0 + 512], 512, col(C_GOUT + k))
            bd_sem = K.dsem("bd")
            for gi, (wd, stg_, dst) in enumerate(((w_r_d, lr_, Wr), (w_i_d, li_, Wi))):
                K.memset("pool", stg_.t[:], 0.0, w=[stg_])
                sv = stg_.t[:].rearrange("p (c n) -> p c n", c=4)
                for blk in range(8):
                    c, hf_ = blk // 2, blk % 2
                    K.dma("sp", sv[hf_ * 64:(hf_ + 1) * 64, c, hf_ * 64:(hf_ + 1) * 64],
                          wd[blk * 64:(blk + 1) * 64, :], bd_sem, w=[stg_])
            for stg_ in (lr_, li_):
                stg_.b.w = Tok(None, bd_sem.h, bd_sem.val)
            K.copy("pool", Wr.t[:], lr_.t[:].rearrange("p (c n) -> p c n", c=4), r=[lr_], w=[Wr])
            K.copy("pool", Wi.t[:], li_.t[:].rearrange("p (c n) -> p c n", c=4), r=[li_], w=[Wi])

            gq = K.sb(es, "gq", [128, 96], F32)
            K.ts("pool", gq.t[:], rows.t[:, R_GQ:R_GQ + 96], 96.0 ** -0.5, 0.0, ALU.mult, ALU.add, r=[rows], w=[gq])
            clam = K.sb(es, "clam", [128, 4], F32)
            K.act(clam.t[:], col(C_LAM, 4), AF.Exp, r=[cols], w=[clam], scale=-1.0)
            K.act(clam.t[:], clam.t[:], AF.Ln, r=[clam], w=[clam], bias=1.0)
            K.ts("pool", clam.t[:], clam.t[:], -8.0, 0.0, ALU.mult, ALU.add, r=[clam], w=[clam])
            ones2 = K.sb(es, "ones2", [128, 2], F32)
            K.memset("pool", ones2.t[:], 1.0, w=[ones2])

            KT = K.sb(es, "KT", [128, 8, S], BF16)
            VA = K.sb(es, "VA", [128, NT, 8, 65], BF16)
            KTb = [Buf("KT%d" % i) for i in range(NT)]
            VAb = [Buf("VA%d" % i) for i in range(NT)]
            K.memset("pool", VA.t[:], 1.0, w=VAb)
            posi = K.sb(es, "posi", [128, 2 * NT], I32)
            pos_sem = K.dsem("pos")
            K.dma("sp", posi.t[:], pos_d[:, :], pos_sem, w=[posi])
            posf = K.sb(es, "posf", [128, 2 * NT], F32)
            K.copy("dve", posf.t[:], posi.t[:], r=[posi], w=[posf])
            NA = 2 * NT
            angi = K.sb(es, "angi", [128, NA, 16], I32)
            rsin = K.sb(es, "rsin", [128, NA, 16], F32)
            rcos = K.sb(es, "rcos", [128, NA, 16], F32)
            x1o = K.sb(es, "x1o", [128, D], F32)
            ang_t, angn_t, angm_t = lxc, lm_, x1o
            angv = lxc.t[:].rearrange("p (a f) -> p a f", f=16)
            angnv = lm_.t[:].rearrange("p (a f) -> p a f", f=16)
            angmv = x1o.t[:, 0:512].rearrange("p (a f) -> p a f", f=16)
            TWO_PI = 2.0 * math.pi
            C1 = 6.28125
            C2 = TWO_PI - C1
            PI_IN = 3.1415925

            def wrap():
                K.ts("dve", angmv, angv, math.pi, -TWO_PI, ALU.is_gt, ALU.mult, r=[ang_t], w=[angm_t])
                K.tt("dve", angv, angv, angmv, ALU.add, r=[ang_t, angm_t], w=[ang_t])
                K.ts("dve", angmv, angv, -math.pi, TWO_PI, ALU.is_lt, ALU.mult, r=[ang_t], w=[angm_t])
                K.tt("dve", angv, angv, angmv, ALU.add, r=[ang_t, angm_t], w=[ang_t])
                K.ts("dve", angv, angv, PI_IN, -PI_IN, ALU.min, ALU.max, r=[ang_t], w=[ang_t])

            pf = posf.t[:, :].unsqueeze(2).to_broadcast([128, NA, 16])
            fr = rows.t[:, R_IF:R_IF + 16].unsqueeze(1).to_broadcast([128, NA, 16])
            K.tt("dve", angv, pf, fr, ALU.mult, r=[posf, rows], w=[ang_t])
            K.ts("dve", angnv, angv, 1.0 / TWO_PI, None, ALU.mult, ALU.bypass, r=[ang_t], w=[angn_t])
            K.copy("dve", angi.t[:], angnv, r=[angn_t], w=[angi])
            K.copy("dve", angnv, angi.t[:], r=[angi], w=[angn_t])
            K.stt(angv, angnv, -C1, angv, ALU.mult, ALU.add, r=[angn_t, ang_t], w=[ang_t])
            K.stt(angv, angnv, -C2, angv, ALU.mult, ALU.add, r=[angn_t, ang_t], w=[ang_t])
            wrap()
            K.act(rsin.t[:], angv, AF.Sin, r=[ang_t], w=[rsin])
            K.ts("dve", angv, angv, math.pi / 2, None, ALU.add, ALU.bypass, r=[ang_t], w=[ang_t])
            wrap()
            K.act(rcos.t[:], angv, AF.Sin, r=[ang_t], w=[rcos])

            xr = [K.sb(es, "x%d" % i, [128, D], F32) for i in range(3)]
            xr_sem = [K.dsem("xs%d" % i) for i in range(3)]
            xr_i = [0]

            def load_x(src_ap):
                i = xr_i[0] % 3
                xr_i[0] += 1
                tb = xr[i]
                K.dma("sp", tb.t[:], src_ap, xr_sem[i], w=[tb])
                return tb

            junk = K.sb(es, "junk", [128, D], BF16)
            hn = Ring([K.sb(es, "hn%d" % i, [128, D], BF16) for i in range(2)])
            hnT = [K.sb(es, "hnT%d" % i, [128, 8, 512], BF16) for i in range(2)]
            ss4 = K.sb(es, "ss4", [128, 8], F32)
            mla = K.sb(es, "mla", [128, 416], F32)
            ssm = K.sb(es, "ssm", [128, 4], F32)
            cqn = K.sb(es, "cqn", [128, 384], BF16)
            cqT = K.sb(es, "cqT", [128, 3, 128], BF16)
            qf = K.sb(es, "qf", [128, 8, 96], F32)
            sq = K.sb(es, "sq", [128, 8, 96], F32)
            ssh = K.sb(es, "ssh", [128, 16], F32)
            Qb = K.sb(es, "Qb", [128, 8, 96], BF16)
            Kb = K.sb(es, "Kb", [128, 8, 96], BF16)
            QT = [K.sb(es, "QT%d" % i, [128, 8, 128], BF16) for i in range(2)]
            rk = K.sb(es, "rk", [128, 32], F32)
            rk2 = K.sb(es, "rk2", [128, 32], F32)
            rt = [K.sb(es, "rt%d" % i, [128, 8, 16], F32) for i in range(2)]
            PT = Ring([K.sb(es, "PT%d" % i, [128, 4, 128], BF16) for i in range(4)])
            rec = K.sb(es, "rec", [128, 8], F32)
            attn = K.sb(es, "attn", [128, 8, 64], F32)
            ssa = K.sb(es, "ssa", [128, 2], F32)
            attb = K.sb(es, "attb", [128, 512], BF16)
            attT = [K.sb(es, "attT%d" % i, [128, 4, 128], BF16) for i in range(8)]
            xlb = K.sb(es, "xlb", [128, 4, 515], F32)
            hst = K.sb(es, "hst", [128, 4], F32)
            ybf = [K.sb(es, "ybf%d" % i, [128, 4, 512], BF16) for i in range(2)]
            ssl = [K.sb(es, "ssl%d" % i, [128, 4, 2], F32) for i in range(2)]
            x1_sem = K.dsem("x1o")

            gen = Ring(banks[4:8])
            sbk = Ring(banks[0:2])
            obk = banks[2:4]
            NG = 2 * NT
            NSG = NG // 4

            def N_unit(G, jj):
                r0 = G * 512 + jj * 128
                col_ = (G % 2) * 4 + jj
                xt = load_x(x_d[r0:r0 + 128, :])
                h_ = hn.next()
                K.act(junk.t[:], xt.t[:], AF.Square, r=[xt], w=[junk, ss4], accum=ss4.t[:, col_:col_ + 1])
                rstd(ss4, ss4.t[:, col_:col_ + 1], ss4, ss4.t[:, col_:col_ + 1], D, 1)
                K.act(h_.t[:], xt.t[:], AF.Copy, r=[xt, ss4], w=[h_], scale=ss4.t[:, col_:col_ + 1])
                bk = gen.next()
                bv = bk.t[:].bitcast(BF16)
                for k in range(8):
                    K.tr(bv[:, k * 128:(k + 1) * 128], h_.t[:, k * 128:(k + 1) * 128], ident.t[:],
                         r=[h_, ident], w=[bk], sig=(k == 7))
                hT = hnT[G % 2]
                K.copy("dve", hT.t[:, :, jj * 128:(jj + 1) * 128],
                       bv.rearrange("p (k t) -> p k t", k=8), r=[bk], w=[hT])

            pstate = {}

            def P_stage(g, i):
                G, j = g // 4, g % 4
                t = g % NT
                hT = hnT[G % 2]
                if i == 0:
                    bk = gen.next()
                    for k in range(8):
                        K.mm(bk.t[:, 0:416], hT.t[:, k, j * 128:(j + 1) * 128], Win.t[:, k, 0:416],
                             k == 0, k == 7, r=[hT, Win], w=[bk], sig=(k == 7))
                    K.copy("act", mla.t[:], bk.t[:, 0:416], r=[bk], w=[mla])
                    K.act(junk.t[:, 0:256], mla.t[:, 0:256], AF.Square, r=[mla], w=[junk, ssm], accum=ssm.t[:, 0:1])
                    K.act(junk.t[:, 0:128], mla.t[:, 256:384], AF.Square, r=[mla], w=[junk, ssm], accum=ssm.t[:, 1:2])
                    K.act(junk.t[:, 0:32], mla.t[:, 384:416], AF.Square, r=[mla], w=[junk, ssm], accum=ssm.t[:, 2:3])
                    rstd(ssm, ssm.t[:, 0:1], ssm, ssm.t[:, 0:1], 256, 1)
                    rstd(ssm, ssm.t[:, 1:2], ssm, ssm.t[:, 1:2], 128, 1)
                    K.act(cqn.t[:, 0:256], mla.t[:, 0:256], AF.Copy, r=[mla, ssm], w=[cqn], scale=ssm.t[:, 0:1])
                    K.act(cqn.t[:, 256:384], mla.t[:, 256:384], AF.Copy, r=[mla, ssm], w=[cqn], scale=ssm.t[:, 1:2])
                elif i == 1:
                    bk = gen.next()
                    bv = bk.t[:].bitcast(BF16)
                    for k in range(3):
                        K.tr(bv[:, k * 128:(k + 1) * 128], cqn.t[:, k * 128:(k + 1) * 128], ident.t[:],
                             r=[cqn, ident], w=[bk], sig=(k == 2))
                    K.copy("dve", cqT.t[:], bv[:, 0:384].rearrange("p (k t) -> p k t", k=3), r=[bk], w=[cqT])
                elif i == 2:
                    bq0, bq1 = gen.next(), gen.next()
                    for k in range(2):
                        K.mm(bq0.t[:, 0:384], cqT.t[:, k, :], Wuq.t[:, k, 0:384], k == 0, k == 1,
                             r=[cqT, Wuq], w=[bq0], sig=(k == 1))
                    for k in range(2):
                        K.mm(bq1.t[:, 0:384], cqT.t[:, k, :], Wuq.t[:, k, 384:768], k == 0, k == 1,
                             r=[cqT, Wuq], w=[bq1], sig=(k == 1))
                    qf2 = qf.t[:].rearrange("p h d -> p (h d)")
                    K.copy("act", qf2[:, 0:384], bq0.t[:, 0:384], r=[bq0], w=[qf])
                    K.copy("act", qf2[:, 384:768], bq1.t[:, 0:384], r=[bq1], w=[qf])
                    K.tt("pool", sq.t[:], qf.t[:], qf.t[:], ALU.mult, r=[qf], w=[sq])
                    K.op("dve", lambda e: e.tensor_reduce(out=ssh.t[:, 0:8], in_=sq.t[:], axis=AX.X, op=ALU.add),
                         r=[sq], w=[ssh])
                    bk0, bk1 = gen.next(), gen.next()
                    K.mm(bk0.t[:, :], cqT.t[:, 2, :], Wukv.t[:, 0:512], True, True, r=[cqT, Wukv], w=[bk0], sig=True)
                    K.mm(bk1.t[:, :], cqT.t[:, 2, :], Wukv.t[:, 512:1024], True, True, r=[cqT, Wukv], w=[bk1], sig=True)
                    pstate["kv"] = (bk0, bk1)
                    for hb, bkk in ((0, bk0), (1, bk1)):
                        kvv = bkk.t[:].rearrange("p (h d) -> p h d", h=4)
                        K.act(sq.t[:, hb * 4:(hb + 1) * 4, 0:64], kvv[:, :, 0:64], AF.Square, r=[bkk], w=[sq])
                        K.copy("act", VA.t[:, t, hb * 4:(hb + 1) * 4, 0:64], kvv[:, :, 64:128], r=[bkk], w=[VAb[t]])
                    K.op("dve", lambda e: e.tensor_reduce(out=ssh.t[:, 8:16], in_=sq.t[:, :, 0:64], axis=AX.X, op=ALU.add),
                         r=[sq], w=[ssh])
                    K.ts("dve", ssh.t[:, 8:16], ssh.t[:, 8:16], ssm.t[:, 2:3], None, ALU.add, ALU.bypass,
                         r=[ssh, ssm], w=[ssh])
                    rstd(ssh, ssh.t[:], ssh, ssh.t[:], 96, 16)
                elif i == 3:
                    bk0, bk1 = pstate["kv"]
                    K.tt("dve", qf.t[:], qf.t[:], ssh.t[:, 0:8].unsqueeze(2).to_broadcast([128, 8, 96]), ALU.mult,
                         r=[qf, ssh], w=[qf])
                    K.tt("dve", qf.t[:], qf.t[:], gq.t[:, :].unsqueeze(1).to_broadcast([128, 8, 96]), ALU.mult,
                         r=[qf, gq], w=[qf])
                    cosb = rcos.t[:, g, :].unsqueeze(1).to_broadcast([128, 8, 16])
                    sinb = rsin.t[:, g, :].unsqueeze(1).to_broadcast([128, 8, 16])
                    K.copy("act", Qb.t[:, :, 0:64], qf.t[:, :, 0:64], r=[qf], w=[Qb])
                    K.tt("pool", rt[0].t[:], qf.t[:, :, 64:80], cosb, ALU.mult, r=[qf, rcos], w=[rt[0]])
                    K.tt("pool", rt[1].t[:], qf.t[:, :, 80:96], sinb, ALU.mult, r=[qf, rsin], w=[rt[1]])
                    K.tt("pool", Qb.t[:, :, 64:80], rt[0].t[:], rt[1].t[:], ALU.subtract, r=[rt[0], rt[1]], w=[Qb])
                    K.tt("pool", rt[0].t[:], qf.t[:, :, 80:96], cosb, ALU.mult, r=[qf, rcos], w=[rt[0]])
                    K.tt("pool", rt[1].t[:], qf.t[:, :, 64:80], sinb, ALU.mult, r=[qf, rsin], w=[rt[1]])
                    K.tt("pool", Qb.t[:, :, 80:96], rt[0].t[:], rt[1].t[:], ALU.add, r=[rt[0], rt[1]], w=[Qb])
                    for hb, bkk in ((0, bk0), (1, bk1)):
                        kvv = bkk.t[:].rearrange("p (h d) -> p h d", h=4)
                        K.tt("dve", sq.t[:, hb * 4:(hb + 1) * 4, 0:64], kvv[:, :, 0:64],
                             ssh.t[:, 8 + hb * 4:12 + hb * 4].unsqueeze(2).to_broadcast([128, 4, 64]), ALU.mult,
                             r=[bkk, ssh], w=[sq])
                    K.tt("dve", Kb.t[:, :, 0:64], sq.t[:, :, 0:64],
                         rows.t[:, R_GK:R_GK + 64].unsqueeze(1).to_broadcast([128, 8, 64]), ALU.mult,
                         r=[sq, rows], w=[Kb])
                    K.tt("pool", rk.t[:], mla.t[:, 384:416], rows.t[:, R_GK + 64:R_GK + 96], ALU.mult, r=[mla, rows], w=[rk])
                    c16, s16 = rcos.t[:, g, :], rsin.t[:, g, :]
                    K.tt("pool", rt[0].t[:, 0, :], rk.t[:, 0:16], c16, ALU.mult, r=[rk, rcos], w=[rt[0]])
                    K.tt("pool", rt[1].t[:, 0, :], rk.t[:, 16:32], s16, ALU.mult, r=[rk, rsin], w=[rt[1]])
                    K.tt("pool", rk2.t[:, 0:16], rt[0].t[:, 0, :], rt[1].t[:, 0, :], ALU.subtract, r=[rt[0], rt[1]], w=[rk2])
                    K.tt("pool", rt[0].t[:, 0, :], rk.t[:, 16:32], c16, ALU.mult, r=[rk, rcos], w=[rt[0]])
                    K.tt("pool", rt[1].t[:, 0, :], rk.t[:, 0:16], s16, ALU.mult, r=[rk, rsin], w=[rt[1]])
                    K.tt("pool", rk2.t[:, 16:32], rt[0].t[:, 0, :], rt[1].t[:, 0, :], ALU.add, r=[rt[0], rt[1]], w=[rk2])
                    K.tt("dve", Kb.t[:, :, 64:96], rk2.t[:, :].unsqueeze(1).to_broadcast([128, 8, 32]),
                         ssh.t[:, 8:16].unsqueeze(2).to_broadcast([128, 8, 32]), ALU.mult, r=[rk2, ssh], w=[Kb])
                elif i == 4:
                    qt = QT[g % 2]
                    for src, dst_b, dst_ap in ((Qb, qt, qt.t[0:96, :, :]),
                                               (Kb, KTb[t], KT.t[0:96, :, t * 128:(t + 1) * 128])):
                        bk = gen.next()
                        bv = bk.t[:].bitcast(BF16)
                        for h in range(8):
                            K.tr(bv[0:96, h * 128:(h + 1) * 128], src.t[:, h, :], ident.t[:],
                                 r=[src, ident], w=[bk], sig=(h == 7))
                        K.copy("dve", dst_ap, bv[0:96, :].rearrange("p (h t) -> p h t", h=8), r=[bk], w=[dst_b])

            def A_head(g, h):
                t = g % NT
                qt = QT[g % 2]
                ob = obk[h // 4]
                oap = ob.t[:, (h % 4) * 65:(h % 4) * 65 + 65]
                for g0 in range(0, t + 1, 4):
                    n = min(4, t + 1 - g0)
                    sb_ = sbk.next()
                    for i in range(n):
                        kt = g0 + i
                        K.mm(sb_.t[:, i * 128:(i + 1) * 128], KT.t[0:96, h, kt * 128:(kt + 1) * 128],
                             qt.t[0:96, h, :], True, True, r=[KTb[kt], qt], w=[sb_], sig=(i == n - 1))
                    pt = PT.next()
                    K.act(pt.t[:, 0:n, :].rearrange("p a b -> p (a b)"), sb_.t[:, 0:n * 128], AF.Exp,
                          r=[sb_], w=[pt])
                    if g0 + n - 1 == t:
                        K.memset("pool", pt.t[64:128, n - 1, 0:64], 0.0, w=[pt])
                    for i in range(n):
                        kt = g0 + i
                        K.mm(oap, pt.t[:, i, :], VA.t[:, kt, h, :], kt == 0, kt == t,
                             r=[pt, VAb[kt]], w=[ob], sig=(i == n - 1))

            def F_tile(g):
                for hb in range(2):
                    ov = obk[hb].t[:, 0:260].rearrange("p (h d) -> p h d", h=4)
                    K.op("dve", lambda e, ov=ov, hb=hb: e.reciprocal(out=rec.t[:, hb * 4:(hb + 1) * 4], in_=ov[:, :, 64]),
                         r=[obk[hb]], w=[rec])
                    K.tt("dve", attn.t[:, hb * 4:(hb + 1) * 4, :], ov[:, :, 0:64],
                         rec.t[:, hb * 4:(hb + 1) * 4].unsqueeze(2).to_broadcast([128, 4, 64]), ALU.mult,
                         r=[obk[hb], rec], w=[attn])
                a2 = attn.t[:].rearrange("p h d -> p (h d)")
                K.act(junk.t[:, 0:512], a2, AF.Square, r=[attn], w=[junk, ssa], accum=ssa.t[:, 0:1])
                rstd(ssa, ssa.t[:, 0:1], ssa, ssa.t[:, 0:1], 512, 1)
                K.act(attb.t[:], a2, AF.Copy, r=[attn, ssa], w=[attb], scale=ssa.t[:, 0:1])
                at = attT[g % 8]
                bk = gen.next()
                bv = bk.t[:].bitcast(BF16)
                for k in range(4):
                    K.tr(bv[:, k * 128:(k + 1) * 128], attb.t[:, k * 128:(k + 1) * 128], ident.t[:],
                         r=[attb, ident], w=[bk], sig=(k == 3))
                K.copy("dve", at.t[:], bv[:, 0:512].rearrange("p (k t) -> p k t", k=4), r=[bk], w=[at])

            def L_chunk(G, c):
                hT = hnT[G % 2]
                yb = ybf[G % 2]
                sl = ssl[G % 2]
                if G % (NT // 4) == 0 and c == 0:
                    K.memset("pool", xlb.t[:, :, 0:3], 0.0, w=[xlb])
                    K.memset("pool", hst.t[:], 0.0, w=[hst])
                bx, bg = gen.next(), gen.next()
                for k in range(8):
                    K.mm(bx.t[:, :], Win.t[:, k, 416 + c * 128:416 + (c + 1) * 128], hT.t[:, k, :],
                         k == 0, k == 7, r=[Win, hT], w=[bx], sig=(k == 7))
                K.copy("act", xlb.t[:, c, 3:515], bx.t[:, :], r=[bx], w=[xlb])
                for k in range(8):
                    K.mm(bg.t[:, :], Win.t[:, k, 928 + c * 128:928 + (c + 1) * 128], hT.t[:, k, :],
                         k == 0, k == 7, r=[Win, hT], w=[bg], sig=(k == 7))
                K.act(li_.t[:], bg.t[:, :], AF.Gelu_apprx_tanh, r=[bg], w=[li_])
                K.ts("dve", lxc.t[:], xlb.t[:, c, 3:515], col(C_LCW + c * 4 + 3), col(C_LCB + c),
                     ALU.mult, ALU.add, r=[xlb, cols], w=[lxc])
                for jj in range(3):
                    K.stt(lxc.t[:], xlb.t[:, c, jj:jj + 512], col(C_LCW + c * 4 + jj), lxc.t[:],
                          ALU.mult, ALU.add, r=[xlb, cols, lxc], w=[lxc])
                K.copy("pool", xlb.t[:, c, 0:3], xlb.t[:, c, 512:515], r=[xlb], w=[xlb])
                K.copy("pool", lxb.t[:], lxc.t[:], r=[lxc], w=[lxb])
                br, bi = gen.next(), gen.next()
                K.mm(br.t[:, :], Wr.t[:, c, :], lxb.t[:], True, True, r=[Wr, lxb], w=[br], sig=True)
                K.mm(bi.t[:, :], Wi.t[:, c, :], lxb.t[:], True, True, r=[Wi, lxb], w=[bi], sig=True)
                K.act(lr_.t[:], br.t[:, :], AF.Sigmoid, r=[br, cols], w=[lr_], bias=col(C_BR + c))
                K.act(lm_.t[:], bi.t[:, :], AF.Sigmoid, r=[bi, cols], w=[lm_], bias=col(C_BI + c))
                K.act(lr_.t[:], lr_.t[:], AF.Exp, r=[lr_, clam], w=[lr_], scale=clam.t[:, c:c + 1])
                K.tt("dve", lm_.t[:], lm_.t[:], lxc.t[:], ALU.mult, r=[lm_, lxc], w=[lm_])
                K.tt("pool", lxc.t[:], lr_.t[:], lr_.t[:], ALU.mult, r=[lr_], w=[lxc])
                K.ts("pool", lxc.t[:], lxc.t[:], -1.0, 1.0, ALU.mult, ALU.add, r=[lxc], w=[lxc])
                K.tt("pool", lxc.t[:], lxc.t[:], nhalf5.t[:], ALU.pow, r=[lxc, nhalf5], w=[lxc])
                K.tt("dve", lm_.t[:], lm_.t[:], lxc.t[:], ALU.mult, r=[lm_, lxc], w=[lm_])
                K.op("dve", lambda e, c=c: e.tensor_tensor_scan(out=lxc.t[:], data0=lr_.t[:], data1=lm_.t[:],
                                                                 initial=hst.t[:, c:c + 1], op0=ALU.mult, op1=ALU.add),
                     r=[lr_, lm_, hst], w=[lxc])
                K.copy("pool", hst.t[:, c:c + 1], lxc.t[:, 511:512], r=[lxc], w=[hst])
                K.tt("dve", lxc.t[:], lxc.t[:], li_.t[:], ALU.mult, r=[lxc, li_], w=[lxc])
                K.copy("pool", yb.t[:, c, :], lxc.t[:], r=[lxc], w=[yb])
                K.act(lm_.t[:], lxc.t[:], AF.Square, r=[lxc], w=[lm_])
                bs_ = gen.next()
                for j in range(4):
                    K.mm(bs_.t[:, (c * 4 + j) * 2:(c * 4 + j) * 2 + 2], lm_.t[:, j * 128:(j + 1) * 128], ones2.t[:],
                         True, True, r=[lm_, ones2], w=[bs_], sig=(j == 3))
                slv = sl.t[:].rearrange("p j o -> p (j o)")
                if c == 0:
                    K.copy("dve", slv, bs_.t[:, 0:8], r=[bs_], w=[sl])
                else:
                    K.tt("dve", slv, slv, bs_.t[:, c * 8:c * 8 + 8], ALU.add, r=[bs_, sl], w=[sl])
                if c == 3:
                    rstd(sl, slv, sl, slv, 512, 8)

            def O_tile(G, j):
                g = G * 4 + j
                r0 = g * 128
                at = attT[g % 8]
                yb = ybf[G % 2]
                sl = ssl[G % 2]
                xt = load_x(x_d[r0:r0 + 128, :])
                for hh in range(2):
                    ba = gen.next()
                    for k in range(4):
                        K.mm(ba.t[:, :], at.t[:, k, :], Wout.t[:, k, hh * 512:(hh + 1) * 512], k == 0, k == 3,
                             r=[at, Wout], w=[ba], sig=(k == 3))
                    K.tt("dve", x1o.t[:, hh * 512:(hh + 1) * 512], ba.t[:, :], xt.t[:, hh * 512:(hh + 1) * 512], ALU.add,
                         r=[ba, xt], w=[x1o])
                    bl = gen.next()
                    for k in range(4):
                        K.mm(bl.t[:, :], yb.t[:, k, j * 128:(j + 1) * 128], Wout.t[:, 4 + k, hh * 512:(hh + 1) * 512],
                             k == 0, k == 3, r=[yb, Wout], w=[bl], sig=(k == 3))
                    K.stt(x1o.t[:, hh * 512:(hh + 1) * 512], bl.t[:, :], sl.t[:, j, 0:1], x1o.t[:, hh * 512:(hh + 1) * 512],
                          ALU.mult, ALU.add, r=[bl, sl, x1o], w=[x1o])
                K.dma("sp", x1_d[r0:r0 + 128, :], x1o.t[:], x1_sem, r=[x1o])

            nhalf5 = K.sb(es, "nhalf5", [128, 512], F32)
            K.memset("pool", nhalf5.t[:], 0.5, w=[nhalf5])

            for jj in range(4):
                N_unit(0, jj)
            for i in range(5):
                P_stage(0, i)
            for g in range(NG):
                G, j = g // 4, g % 4
                for h in range(8):
                    A_head(g, h)
                    if g + 1 < NG and (g + 1) % NT != 0:
                        if h == 0:
                            P_stage(g + 1, 0)
                        elif h == 1:
                            P_stage(g + 1, 1)
                        elif h == 2:
                            P_stage(g + 1, 2)
                        elif h == 3:
                            P_stage(g + 1, 3)
                        elif h == 6:
                            P_stage(g + 1, 4)
                    if h == 4 and G + 1 < NSG:
                        if j == 1:
                            N_unit(G + 1, 0)
                            N_unit(G + 1, 1)
                        elif j == 2:
                            N_unit(G + 1, 2)
                            N_unit(G + 1, 3)
                    if h == 5:
                        L_chunk(G, j)
                    if h == 7 and G >= 1:
                        O_tile(G - 1, j)
                F_tile(g)
                if g + 1 < NG and (g + 1) % NT == 0:
                    for i in range(5):
                        P_stage(g + 1, i)
            for j in range(4):
                O_tile(NSG - 1, j)
            x1_done = Tok(None, x1_sem.h, x1_sem.val)
            K.barrier([x1_done])
            K.flush()

        with ExitStack() as es:
            Wup = K.sb(es, "Wup", [128, 8, 2 * DFF], BF16)
            Wdn = K.sb(es, "Wdn", [128, NCH, D], BF16)
            stage = [K.sb(es, "fstg%d" % i, [128, 1024], F32) for i in range(2)]
            stage_sem = [K.dsem("fstg%d" % i) for i in range(2)]
            wup_b = [Buf("wup%d" % c) for c in range(2 * NCH)]
            wdn_b = [Buf("wdn%d" % c) for c in range(NCH)]
            w_up_v = w_up_d.rearrange("(k p) n -> p k n", p=128)
            w_dn_v = w_down_d.rearrange("(c p) n -> p c n", p=128)
            si = [0]
            first = [True]

            def fl(dst_b, dst_ap, src_ap, n, scale_col, eng):
                i = si[0] % 2
                si[0] += 1
                st_ = stage[i]
                K.dma("sp", st_.t[:, 0:n], src_ap, stage_sem[i], w=[st_])
                if scale_col is not None:
                    K.ts("pool", dst_ap, st_.t[:, 0:n], scale_col, 0.0, ALU.mult, ALU.add, r=[st_, cols], w=[dst_b])
                else:
                    K.copy(eng, dst_ap, st_.t[:, 0:n], r=[st_], w=[dst_b])

            for c in range(NCH):
                for half in range(2):
                    cc = half * NCH + c
                    c0 = half * DFF + c * 128
                    i = si[0] % 2
                    si[0] += 1
                    st_ = stage[i]
                    K.dma("sp", st_.t[:, :].rearrange("p (k n) -> p k n", k=8), w_up_v[:, :, c0:c0 + 128], stage_sem[i], w=[st_])
                    K.tt("pool", Wup.t[:, :, c0:c0 + 128], st_.t[:, :].rearrange("p (k n) -> p k n", k=8),
                         cols.t[:, C_GFFN:C_GFFN + 8].unsqueeze(2).to_broadcast([128, 8, 128]), ALU.mult,
                         r=[st_, cols], w=[wup_b[cc]])
                fl(wdn_b[c], Wdn.t[:, c, :], w_dn_v[:, c, :], 1024, None, "pool")

            NX = 4
            xr = [K.sb(es, "fx%d" % i, [128, D], F32) for i in range(NX)]
            xr_sem = [K.dsem("fxs%d" % i) for i in range(NX)]
            xo_sem = [K.dsem("fxo%d" % i) for i in range(NX)]
            xi = [0]
            junk = K.sb(es, "fjunk", [128, D], BF16)
            hf = K.sb(es, "hf", [128, D], BF16)
            hfT = [K.sb(es, "hfT%d" % i, [128, 8, WIN + 2], BF16) for i in range(2)]
            ssf = K.sb(es, "ssf", [128, 4], F32)
            cg = [K.sb(es, "cg%d" % i, [128, WIN], F32) for i in range(2)]
            cv = [K.sb(es, "cv%d" % i, [128, WIN], F32) for i in range(2)]
            G = [K.sb(es, "G%d" % i, [128, NCH, WIN], BF16) for i in range(2)]
            upr = Ring(banks[0:4])
            dnr = Ring(banks[4:8])
            nwin = TOK // WIN
            wx = {}

            def FA(wdw):
                tok0 = wdw * WIN
                ht = hfT[wdw % 2]
                hprev = hfT[(wdw + 1) % 2]
                if wdw % (S // WIN) == 0:
                    K.memset("pool", ht.t[:, :, 0:2], 0.0, w=[ht])
                else:
                    K.copy("pool", ht.t[:, :, 0:2], hprev.t[:, :, WIN:WIN + 2], r=[hprev], w=[ht])
                xs_ = []
                for j in range(2):
                    i = xi[0] % NX
                    xi[0] += 1
                    xt = xr[i]
                    r0 = tok0 + j * 128
                    K.dma("sp", xt.t[:], x1_d[r0:r0 + 128, :], xr_sem[i], w=[xt])
                    if first[0]:
                        K.q["sp"].ops[-1][0].append((x1_done.sem, x1_done.val))
                        first[0] = False
                    xs_.append((xt, i))
                    cc_ = (wdw % 2) * 2 + j
                    K.act(junk.t[:], xt.t[:], AF.Square, r=[xt], w=[junk, ssf], accum=ssf.t[:, cc_:cc_ + 1])
                    rstd(ssf, ssf.t[:, cc_:cc_ + 1], ssf, ssf.t[:, cc_:cc_ + 1], D, 1)
                    K.act(hf.t[:], xt.t[:], AF.Copy, r=[xt, ssf], w=[hf], scale=ssf.t[:, cc_:cc_ + 1])
                    bk = dnr.next()
                    bv = bk.t[:].bitcast(BF16)
                    for k in range(8):
                        K.tr(bv[:, k * 128:(k + 1) * 128], hf.t[:, k * 128:(k + 1) * 128], ident.t[:],
                             r=[hf, ident], w=[bk], sig=(k == 7))
                    K.copy("dve", ht.t[:, :, 2 + j * 128:2 + (j + 1) * 128],
                           bv.rearrange("p (k t) -> p k t", k=8), r=[bk], w=[ht])
                wx[wdw] = xs_

            FA(0)
            for wdw in range(nwin):
                tok0 = wdw * WIN
                ht = hfT[wdw % 2]
                xs_ = wx.pop(wdw)
                Gw = G[wdw % 2]
                for c in range(NCH):
                    if c == 8 and wdw + 1 < nwin:
                        FA(wdw + 1)
                    bg_, bv_ = upr.next(), upr.next()
                    for half, bkk in ((0, bg_), (1, bv_)):
                        c0 = half * DFF + c * 128
                        for k in range(8):
                            K.mm(bkk.t[:, 0:WIN + 2], Wup.t[:, k, c0:c0 + 128], ht.t[:, k, :], k == 0, k == 7,
                                 r=[wup_b[half * NCH + c], ht], w=[bkk], sig=(k == 7))
                    cgt, cvt = cg[c % 2], cv[c % 2]
                    for half, bkk, dst in ((0, bg_, cgt), (1, bv_, cvt)):
                        ci = half * NCH + c
                        K.act(dst.t[:], bkk.t[:, 2:WIN + 2], AF.Identity, r=[bkk, cols], w=[dst],
                              scale=col(C_FCW + ci * 3 + 2), bias=col(C_FCB + ci))
                        K.stt(dst.t[:], bkk.t[:, 1:WIN + 1], col(C_FCW + ci * 3 + 1), dst.t[:], ALU.mult, ALU.add,
                              r=[bkk, cols, dst], w=[dst])
                        K.stt(dst.t[:], bkk.t[:, 0:WIN], col(C_FCW + ci * 3 + 0), dst.t[:], ALU.mult, ALU.add,
                              r=[bkk, cols, dst], w=[dst])
                    K.act(cgt.t[:], cgt.t[:], AF.Gelu_apprx_tanh, r=[cgt], w=[cgt])
                    K.tt("pool", Gw.t[:, c, :], cgt.t[:], cvt.t[:], ALU.mult, r=[cgt, cvt], w=[Gw])
                for j in range(2):
                    xt, i = xs_[j]
                    r0 = tok0 + j * 128
                    for hh in range(2):
                        bd = dnr.next()
                        for c in range(NCH):
                            K.mm(bd.t[:, :], Gw.t[:, c, j * 128:(j + 1) * 128], Wdn.t[:, c, hh * 512:(hh + 1) * 512],
                                 c == 0, c == NCH - 1, r=[Gw, wdn_b[c]], w=[bd], sig=(c == NCH - 1))
                        K.tt("dve", xt.t[:, hh * 512:(hh + 1) * 512], bd.t[:, :], xt.t[:, hh * 512:(hh + 1) * 512], ALU.add,
                             r=[bd, xt], w=[xt])
                    K.dma("sp", x2_d[r0:r0 + 128, :], xt.t[:], xo_sem[i], r=[xt])
            x2_toks = [Tok(None, sm.h, sm.val) for sm in xo_sem]
            K.barrier(x2_toks)
            K.flush()

        with ExitStack() as es:
            Wpg = K.sb(es, "Wpg", [128, 8, D], BF16)
            Wpl = K.sb(es, "Wpl", [128, 2, D], BF16)
            brow = K.sb(es, "brow", [128, D], BF16)
            one0 = K.sb(es, "one0", [128, 128], BF16)
            browf = K.sb(es, "browf", [128, D], F32)
            stage = [K.sb(es, "pstg%d" % i, [128, 1024], F32) for i in range(2)]
            stage_sem = [K.dsem("pstg%d" % i) for i in range(2)]
            si = [0]
            w_pg_v = w_pg_d.rearrange("(k p) n -> p k n", p=128)
            w_pl_v = w_ple_d.rearrange("(k p) n -> p k n", p=128)
            for k in range(8):
                i = si[0] % 2
                si[0] += 1
                K.dma("sp", stage[i].t[:], w_pg_v[:, k, :], stage_sem[i], w=[stage[i]])
                K.act(Wpg.t[:, k, :], stage[i].t[:], AF.Copy, r=[stage[i], cols], w=[Wpg], scale=col(C_GPLE + k))
            for k in range(2):
                i = si[0] % 2
                si[0] += 1
                K.dma("sp", stage[i].t[:], w_pl_v[:, k, :], stage_sem[i], w=[stage[i]])
                K.copy("act", Wpl.t[:, k, :], stage[i].t[:], r=[stage[i]], w=[Wpl])
            K.memset("pool", browf.t[:], 0.0, w=[browf])
            K.memset("pool", one0.t[:], 0.0, w=[one0])
            K.memset("pool", one0.t[0:1, :], 1.0, w=[one0])
            bsem = K.dsem("brow")
            K.dma("sp", browf.t[0:1, :], bple_d[0:1, :], bsem, w=[browf])
            K.copy("pool", brow.t[:], browf.t[:], r=[browf], w=[brow])

            NX = 4
            xr = [K.sb(es, "px%d" % i, [128, D], F32) for i in range(NX)]
            xr_sem = [K.dsem("pxs%d" % i) for i in range(NX)]
            xo_sem = [K.dsem("pxo%d" % i) for i in range(NX)]
            pr = [K.sb(es, "pp%d" % i, [128, 256], F32) for i in range(2)]
            pr_sem = [K.dsem("pps%d" % i) for i in range(2)]
            junk = K.sb(es, "pjunk", [128, D], BF16)
            hp = [K.sb(es, "hp%d" % i, [128, D], BF16) for i in range(2)]
            pb = [K.sb(es, "pb%d" % i, [128, 256], BF16) for i in range(2)]
            hpT = [K.sb(es, "hpT%d" % i, [128, 10, 128], BF16) for i in range(3)]
            ssp = K.sb(es, "ssp", [128, 4], F32)
            sg = [K.sb(es, "sg%d" % i, [128, 512], F32) for i in range(4)]
            pring = Ring(banks)
            out_toks = []
            NTI = TOK // 128

            def PA(ti):
                r0 = ti * 128
                i = ti % NX
                xt = xr[i]
                K.dma("sp", xt.t[:], x2_d[r0:r0 + 128, :], xr_sem[i], w=[xt])
                if ti == 0:
                    for tk in x2_toks:
                        K.q["sp"].ops[-1][0].append((tk.sem, tk.val))
                pt_ = pr[ti % 2]
                K.dma("sp", pt_.t[:], p_d[r0:r0 + 128, :], pr_sem[ti % 2], w=[pt_])
                cc_ = ti % 4
                hp_ = hp[ti % 2]
                pb_ = pb[ti % 2]
                K.act(junk.t[:], xt.t[:], AF.Square, r=[xt], w=[junk, ssp], accum=ssp.t[:, cc_:cc_ + 1])
                rstd(ssp, ssp.t[:, cc_:cc_ + 1], ssp, ssp.t[:, cc_:cc_ + 1], D, 1)
                K.act(hp_.t[:], xt.t[:], AF.Copy, r=[xt, ssp], w=[hp_], scale=ssp.t[:, cc_:cc_ + 1])
                K.copy("pool", pb_.t[:], pt_.t[:], r=[pt_], w=[pb_])
                hT = hpT[ti % 3]
                bk = pring.next()
                bv = bk.t[:].bitcast(BF16)
                for k in range(8):
                    K.tr(bv[:, k * 128:(k + 1) * 128], hp_.t[:, k * 128:(k + 1) * 128], ident.t[:],
                         r=[hp_, ident], w=[bk], sig=(k == 7))
                K.copy("dve", hT.t[:, 0:8, :], bv.rearrange("p (k t) -> p k t", k=8), r=[bk], w=[hT])
                bk = pring.next()
                bv = bk.t[:].bitcast(BF16)
                for k in range(2):
                    K.tr(bv[:, k * 128:(k + 1) * 128], pb_.t[:, k * 128:(k + 1) * 128], ident.t[:],
                         r=[pb_, ident], w=[bk], sig=(k == 1))
                K.copy("dve", hT.t[:, 8:10, :], bv[:, 0:256].rearrange("p (k t) -> p k t", k=2), r=[bk], w=[hT])

            def PB(ti):
                r0 = ti * 128
                i = ti % NX
                xt = xr[i]
                hT = hpT[ti % 3]
                for hh in range(2):
                    bgt, be = pring.next(), pring.next()
                    for k in range(8):
                        K.mm(bgt.t[:, :], hT.t[:, k, :], Wpg.t[:, k, hh * 512:(hh + 1) * 512], k == 0, False,
                             r=[hT, Wpg], w=[bgt], sig=False)
                    K.mm(bgt.t[:, :], one0.t[:], brow.t[:, hh * 512:(hh + 1) * 512], False, True,
                         r=[one0, brow], w=[bgt], sig=True)
                    for k in range(2):
                        K.mm(be.t[:, :], hT.t[:, 8 + k, :], Wpl.t[:, k, hh * 512:(hh + 1) * 512], k == 0, k == 1,
                             r=[hT, Wpl], w=[be], sig=(k == 1))
                    sgt = sg[(ti * 2 + hh) % 4]
                    K.act(sgt.t[:], bgt.t[:, :], AF.Sigmoid, r=[bgt], w=[sgt])
                    K.tt("dve", sgt.t[:], sgt.t[:], be.t[:, :], ALU.mult, r=[sgt, be], w=[sgt])
                    K.tt("pool", xt.t[:, hh * 512:(hh + 1) * 512], xt.t[:, hh * 512:(hh + 1) * 512], sgt.t[:], ALU.add,
                         r=[xt, sgt], w=[xt])
                out_toks.append(K.dma("sp", out_d[r0:r0 + 128, :], xt.t[:], xo_sem[i], r=[xt]))

            PA(0)
            PA(1)
            for ti in range(NTI):
                if ti + 2 < NTI:
                    PA(ti + 2)
                PB(ti)
            K.flush(final_toks=out_toks[-8:])
    return nc


def _host_inputs(inputs):
    f = lambda a: np.ascontiguousarray(np.asarray(a, dtype=np.float32))
    x = f(inputs["x"])
    p = f(inputs["p"])[0]
    pos = np.asarray(inputs["positions"]).astype(np.int32)

    def colz(v):
        v = f(v).reshape(-1)
        return v.reshape(-1, 128).T

    cols = np.zeros((128, NCOL), np.float32)
    cols[:, C_GMIX:C_GMIX + 8] = colz(inputs["g_mix"][0])
    cols[:, C_GFFN:C_GFFN + 8] = colz(inputs["g_ffn"][0])
    cols[:, C_GPLE:C_GPLE + 8] = colz(inputs["g_ple"][0])
    cols[:, C_GCQ:C_GCQ + 2] = colz(inputs["g_cq"][0])
    cols[:, C_GCKV:C_GCKV + 1] = colz(inputs["g_ckv"][0])
    cols[:, C_GOUT:C_GOUT + 4] = colz(inputs["g_attn_out"][0])
    cols[:, C_GOUT + 4:C_GOUT + 8] = colz(inputs["g_lru_out"][0])
    lcw = f(inputs["w_lru_conv"][0])
    cols[:, C_LCW:C_LCW + 16] = lcw.reshape(4, 4, 128).transpose(2, 1, 0).reshape(128, 16)
    cols[:, C_LCB:C_LCB + 4] = colz(inputs["b_lru_conv"][0])
    cols[:, C_BR:C_BR + 4] = colz(inputs["b_lru_r"][0])
    cols[:, C_BI:C_BI + 4] = colz(inputs["b_lru_i"][0])
    cols[:, C_LAM:C_LAM + 4] = colz(inputs["lru_lambda"][0])
    fcw = f(inputs["w_ffn_conv"][0])
    cols[:, C_FCW:C_FCW + 132] = fcw.reshape(3, 44, 128).transpose(2, 1, 0).reshape(128, 132)
    cols[:, C_FCB:C_FCB + 44] = colz(inputs["b_ffn_conv"][0])
    rows = np.zeros((1, NROW), np.float32)
    rows[0, R_GQ:R_GQ + 96] = f(inputs["g_qn"][0])
    rows[0, R_GK:R_GK + 96] = f(inputs["g_kn"][0])
    rows[0, R_IF:R_IF + 16] = (10000.0 ** (-np.arange(16, dtype=np.float32) / np.float32(16))).astype(np.float32)
    shared = {
        "w_in": f(inputs["w_in"][0]), "w_uq": f(inputs["w_uq"][0]), "w_ukv": f(inputs["w_ukv"][0]),
        "w_out": f(inputs["w_out"][0]), "w_up": f(inputs["w_up"][0]), "w_down": f(inputs["w_down"][0]),
        "w_pg": f(inputs["w_ple_gate"][0]), "w_ple": f(inputs["w_ple"][0]),
        "w_r": f(inputs["w_lru_r"][0]).reshape(512, 64), "w_i": f(inputs["w_lru_i"][0]).reshape(512, 64),
        "cols": cols, "rows": rows, "bple": f(inputs["b_ple_gate"][0]).reshape(1, D),
        "ident": np.eye(128, dtype=np.float32).astype(ml_dtypes.bfloat16),
    }
    maps = []
    for c in range(NCORES):
        m = dict(shared)
        m["x"] = np.ascontiguousarray(x[2 * c:2 * c + 2].reshape(TOK, D))
        m["p"] = np.ascontiguousarray(p[2 * c:2 * c + 2].reshape(TOK, 256))
        pc = pos[2 * c:2 * c + 2]
        m["posT"] = np.ascontiguousarray(pc.reshape(2, NT, 128).transpose(2, 0, 1).reshape(128, 2 * NT))
        maps.append(m)
    return maps


def kernel(**inputs):
    maps = _host_inputs(inputs)
    nc = build_nc()
    res = run_bass_kernel_spmd(nc, maps, core_ids=list(range(NCORES)))
    outs = [np.asarray(r["out"], dtype=np.float32).reshape(2, S, D) for r in res.results]
    return np.concatenate(outs, axis=0)
```

```python
import math
import numpy as np
import ml_dtypes
from contextlib import ExitStack
import concourse.bass as bass
import concourse.mybir as mybir
from concourse.bass_utils import run_bass_kernel_spmd

F32 = mybir.dt.float32
BF16 = mybir.dt.bfloat16
I32 = mybir.dt.int32
AF = mybir.ActivationFunctionType
ALU = mybir.AluOpType
AX = mybir.AxisListType

NCORES = 8
D = 1024
S = 2048
NT = S // 128
TOK = 2 * S
H = 8
EPS = 1e-6
DFF = 2816
NCH = DFF // 128
WIN = 256

C_GMIX, C_GFFN, C_GPLE, C_GCQ, C_GCKV, C_GOUT = 0, 8, 16, 24, 26, 27
C_LCW, C_LCB, C_BR, C_BI, C_LAM = 35, 51, 55, 59, 63
C_FCW, C_FCB = 67, 67 + 132
NCOL = C_FCB + 44
R_GQ, R_GK, R_IF = 0, 96, 192
NROW = 208


class Tok:
    __slots__ = ("q", "sem", "val")

    def __init__(self, q, sem, val):
        self.q, self.sem, self.val = q, sem, val


class Buf:
    def __init__(self, name=""):
        self.name = name
        self.w = None
        self.r = {}


class TB:
    def __init__(self, t, name=""):
        self.t = t
        self.b = Buf(name)


class Q:
    def __init__(self, name, sem):
        self.name, self.sem = name, sem
        self.n = 0
        self.ops = []
        self.waited = {}
        self.pend_r = []
        self.pend_w = []


class DSem:
    def __init__(self, h):
        self.h = h
        self.val = 0


def _b(x):
    return x.b if isinstance(x, TB) else x


class Kern:
    def __init__(self, nc, es):
        self.nc = nc
        self.es = es
        self.q = {}
        for n in ("pe", "dve", "act", "pool", "sp"):
            self.q[n] = Q(n, es.enter_context(nc.semaphore("q_" + n)))
        self.nsem = 5

    def dsem(self, name):
        self.nsem += 1
        return DSem(self.es.enter_context(self.nc.semaphore(name)))

    def sb(self, es, name, shape, dt):
        return TB(es.enter_context(self.nc.sbuf_tensor("s_" + name, list(shape), dt)), name)

    def _waits(self, q, r, w):
        needs = {}

        def need(t):
            if t is None:
                return
            k = id(t.sem)
            if k not in needs or needs[k][1] < t.val:
                needs[k] = (t.sem, t.val)

        for b in r:
            need(_b(b).w)
        for b in w:
            b = _b(b)
            if b.w is not None and b.w.q is not q:
                need(b.w)
            for t in b.r.values():
                if t.q is not q:
                    need(t)
        waits = []
        for k, (sem, val) in needs.items():
            if q.waited.get(k, 0) < val:
                q.waited[k] = val
                waits.append((sem, val))
        return waits

    @staticmethod
    def _stamp(tok, r, w):
        for b in r:
            b = _b(b)
            k = id(tok.sem)
            o = b.r.get(k)
            if o is None or o.val < tok.val:
                b.r[k] = tok
        for b in w:
            b = _b(b)
            b.w = tok
            b.r = {}

    def op(self, qn, f, r=(), w=(), sig=True):
        q = self.q[qn]
        waits = self._waits(q, r, w)
        if sig:
            q.n += 1
            tok = Tok(q, q.sem, q.n)
            q.ops.append((waits, f, (q.sem, 1)))
            self._stamp(tok, list(r) + q.pend_r, list(w) + q.pend_w)
            q.pend_r, q.pend_w = [], []
            return tok
        q.ops.append((waits, f, None))
        q.pend_r += list(r)
        q.pend_w += list(w)
        return None

    def dma(self, qn, out, in_, sem, r=(), w=()):
        q = self.q[qn]
        assert not q.pend_r and not q.pend_w
        waits = self._waits(q, r, w)
        sem.val += 16
        tok = Tok(None, sem.h, sem.val)
        q.ops.append((waits, lambda e: e.dma_start(out=out, in_=in_), (sem.h, 16)))
        self._stamp(tok, r, w)
        return tok

    def barrier(self, extra=()):
        for q in self.q.values():
            waits = []
            for oq in self.q.values():
                if oq is q or oq.n == 0:
                    continue
                k = id(oq.sem)
                if q.waited.get(k, 0) < oq.n:
                    q.waited[k] = oq.n
                    waits.append((oq.sem, oq.n))
            for t in extra:
                k = id(t.sem)
                if q.waited.get(k, 0) < t.val:
                    q.waited[k] = t.val
                    waits.append((t.sem, t.val))
            if waits:
                q.ops.append((waits, None, None))

    def flush(self, final_toks=()):
        nc = self.nc
        if final_toks:
            q = self.q["sp"]
            needs = {}
            for t in final_toks:
                k = id(t.sem)
                if k not in needs or needs[k][1] < t.val:
                    needs[k] = (t.sem, t.val)
            q.ops.append((list(needs.values()), None, None))
        with nc.Block() as blk:
            for qn, deco in (("sp", blk.sync), ("act", blk.scalar), ("pool", blk.gpsimd),
                             ("dve", blk.vector), ("pe", blk.tensor)):
                qq = self.q[qn]
                assert not qq.pend_r and not qq.pend_w, qn
                ops = qq.ops
                qq.ops = []

                def body(e, ops=ops):
                    for waits, f, inc in ops:
                        for sem, val in waits:
                            e.wait_ge(sem, val)
                        if f is None:
                            continue
                        ins = f(e)
                        if inc is not None:
                            ins.then_inc(inc[0], inc[1])

                deco(body)

    def mm(self, out, lhsT, rhs, start, stop, r=(), w=(), sig=False):
        return self.op("pe", lambda e: e.matmul(out, lhsT, rhs, start=start, stop=stop), r, w, sig)

    def tr(self, out, in_, ident, r=(), w=(), sig=False):
        return self.op("pe", lambda e: e.transpose(out, in_, ident), r, w, sig)

    def act(self, out, in_, func, r=(), w=(), bias=None, scale=None, accum=None, qn="act"):
        kw = {}
        if bias is not None:
            kw["bias"] = bias
        if scale is not None:
            kw["scale"] = scale
        if accum is not None:
            kw["accum_out"] = accum
        return self.op(qn, lambda e: e.activation(out=out, in_=in_, func=func, **kw), r, w)

    def tt(self, qn, out, in0, in1, op, r=(), w=()):
        return self.op(qn, lambda e: e.tensor_tensor(out=out, in0=in0, in1=in1, op=op), r, w)

    def ts(self, qn, out, in0, s1, s2, op0, op1, r=(), w=()):
        return self.op(qn, lambda e: e.tensor_scalar(out=out, in0=in0, scalar1=s1, scalar2=s2,
                                                     op0=op0, op1=op1), r, w)

    def stt(self, out, in0, scalar, in1, op0, op1, r=(), w=()):
        return self.op("dve", lambda e: e.scalar_tensor_tensor(out=out, in0=in0, scalar=scalar, in1=in1,
                                                               op0=op0, op1=op1), r, w)

    def copy(self, qn, out, in_, r=(), w=()):
        if qn == "act":
            return self.op(qn, lambda e: e.copy(out=out, in_=in_), r, w)
        return self.op(qn, lambda e: e.tensor_copy(out=out, in_=in_), r, w)

    def memset(self, qn, ap, val, w=()):
        return self.op(qn, lambda e: e.memset(ap, val), (), w)


def build_nc():
    nc = bass.Bass("TRN2", target_bir_lowering=False)

    def din(name, shape, dt=F32):
        return nc.dram_tensor(name, list(shape), dt, kind="ExternalInput").ap()

    x_d = din("x", [TOK, D])
    p_d = din("p", [TOK, 256])
    pos_d = din("posT", [128, 2 * NT], I32)
    w_in_d = din("w_in", [D, 1440])
    w_uq_d = din("w_uq", [256, 768])
    w_ukv_d = din("w_ukv", [128, 1024])
    w_out_d = din("w_out", [D, D])
    w_up_d = din("w_up", [D, 2 * DFF])
    w_down_d = din("w_down", [DFF, D])
    w_pg_d = din("w_pg", [D, D])
    w_ple_d = din("w_ple", [256, D])
    w_r_d = din("w_r", [512, 64])
    w_i_d = din("w_i", [512, 64])
    cols_d = din("cols", [128, NCOL])
    rows_d = din("rows", [1, NROW])
    bple_d = din("bple", [1, D])
    ident_d = din("ident", [128, 128], BF16)
    out_d = nc.dram_tensor("out", [TOK, D], F32, kind="ExternalOutput").ap()
    x1_d = nc.dram_tensor("x1s", [TOK, D], F32, kind="Internal").ap()
    x2_d = nc.dram_tensor("x2s", [TOK, D], F32, kind="Internal").ap()

    with ExitStack() as g:
        K = Kern(nc, g)
        banks = [TB(g.enter_context(nc.psum_tensor("ps%d" % i, [128, 512], F32)), "ps%d" % i) for i in range(8)]

        class Ring:
            def __init__(self, items):
                self.items = items
                self.i = 0

            def next(self):
                it = self.items[self.i % len(self.items)]
                self.i += 1
                return it

        cols = K.sb(g, "cols", [128, NCOL], F32)
        rows = K.sb(g, "rows", [128, NROW], F32)
        ident = K.sb(g, "ident", [128, 128], BF16)
        nhalf = K.sb(g, "nhalf", [128, 16], F32)
        cst_sem = K.dsem("cst")
        K.dma("sp", cols.t[:], cols_d[:, :], cst_sem, w=[cols])
        K.dma("sp", rows.t[:], rows_d.partition_broadcast(128)[:, 0, :], cst_sem, w=[rows])
        K.dma("sp", ident.t[:], ident_d[:, :], cst_sem, w=[ident])
        ftok = Tok(None, cst_sem.h, cst_sem.val)
        for tb in (cols, rows, ident):
            tb.b.w = ftok
        K.memset("pool", nhalf.t[:], -0.5, w=[nhalf])

        def col(c0, n=1):
            return cols.t[:, c0:c0 + n]

        def rstd(out_tb, out_ap, ss_tb, ss_ap, nfeat, n):
            K.ts("pool", ss_ap, ss_ap, 1.0 / nfeat, EPS, ALU.mult, ALU.add, r=[ss_tb], w=[ss_tb])
            K.tt("pool", out_ap, ss_ap, nhalf.t[:, 0:n], ALU.pow, r=[ss_tb, nhalf], w=[out_tb])

        with ExitStack() as es:
            Win = K.sb(es, "Win", [128, 8, 1440], BF16)
            Wuq = K.sb(es, "Wuq", [128, 2, 768], BF16)
            Wukv = K.sb(es, "Wukv", [128, 1024], BF16)
            Wout = K.sb(es, "Wout", [128, 8, 1024], BF16)
            Wr = K.sb(es, "Wr", [128, 4, 128], BF16)
            Wi = K.sb(es, "Wi", [128, 4, 128], BF16)
            stage = [K.sb(es, "stg%d" % i, [128, 512], F32) for i in range(2)]
            stage_sem = [K.dsem("stg%d" % i) for i in range(2)]
            stg_i = [0]
            lxc = K.sb(es, "lxc", [128, 512], F32)
            lxb = K.sb(es, "lxb", [128, 512], BF16)
            lr_ = K.sb(es, "lr", [128, 512], F32)
            li_ = K.sb(es, "li", [128, 512], F32)
            lm_ = K.sb(es, "lm", [128, 512], F32)

            def load_cast(dst_tb, dst_ap, src_ap, n, scale_col=None):
                i = stg_i[0] % 2
                stg_i[0] += 1
                st_ = stage[i]
                K.dma("sp", st_.t[:, 0:n], src_ap, stage_sem[i], w=[st_])
                if scale_col is None:
                    K.copy("act", dst_ap, st_.t[:, 0:n], r=[st_], w=[dst_tb])
                else:
                    K.act(dst_ap, st_.t[:, 0:n], AF.Copy, r=[st_, cols], w=[dst_tb], scale=scale_col)

            w_in_v = w_in_d.rearrange("(k p) n -> p k n", p=128)
            for k in range(8):
                for h0 in (0, 480, 960):
                    load_cast(Win, Win.t[:, k, h0:h0 + 480], w_in_v[:, k, h0:h0 + 480], 480, col(C_GMIX + k))
            w_uq_v = w_uq_d.rearrange("(k p) n -> p k n", p=128)
            for k in range(2):
                for h0 in (0, 384):
                    load_cast(Wuq, Wuq.t[:, k, h0:h0 + 384], w_uq_v[:, k, h0:h0 + 384], 384, col(C_GCQ + k))
            for h0 in (0, 512):
                load_cast(Wukv, Wukv.t[:, h0:h0 + 512], w_ukv_d[:, h0:h0 + 512], 512, col(C_GCKV))
            w_out_v = w_out_d.rearrange("(k p) n -> p k n", p=128)
            for k in range(8):
                for h0 in (0, 512):
                    load_cast(Wout, Wout.t[:, k, h0:h0 + 512], w_out_v[:, k, h0:h0 + 512], 512, col(C_GOUT + k))
            bd_sem = K.dsem("bd")
            for gi, (wd, stg_, dst) in enumerate(((w_r_d, lr_, Wr), (w_i_d, li_, Wi))):
                K.memset("pool", stg_.t[:], 0.0, w=[stg_])
                sv = stg_.t[:].rearrange("p (c n) -> p c n", c=4)
                for blk in range(8):
                    c, hf_ = blk // 2, blk % 2
                    K.dma("sp", sv[hf_ * 64:(hf_ + 1) * 64, c, hf_ * 64:(hf_ + 1) * 64],
                          wd[blk * 64:(blk + 1) * 64, :], bd_sem, w=[stg_])
            for stg_ in (lr_, li_):
                stg_.b.w = Tok(None, bd_sem.h, bd_sem.val)
            K.copy("pool", Wr.t[:], lr_.t[:].rearrange("p (c n) -> p c n", c=4), r=[lr_], w=[Wr])
            K.copy("pool", Wi.t[:], li_.t[:].rearrange("p (c n) -> p c n", c=4), r=[li_], w=[Wi])

            gq = K.sb(es, "gq", [128, 96], F32)
            K.ts("pool", gq.t[:], rows.t[:, R_GQ:R_GQ + 96], 96.0 ** -0.5, 0.0, ALU.mult, ALU.add, r=[rows], w=[gq])
            clam = K.sb(es, "clam", [128, 4], F32)
            K.act(clam.t[:], col(C_LAM, 4), AF.Exp, r=[cols], w=[clam], scale=-1.0)
            K.act(clam.t[:], clam.t[:], AF.Ln, r=[clam], w=[clam], bias=1.0)
            K.ts("pool", clam.t[:], clam.t[:], -8.0, 0.0, ALU.mult, ALU.add, r=[clam], w=[clam])
            ones2 = K.sb(es, "ones2", [128, 2], F32)
            K.memset("pool", ones2.t[:], 1.0, w=[ones2])

            KT = K.sb(es, "KT", [128, 8, S], BF16)
            VA = K.sb(es, "VA", [128, NT, 8, 65], BF16)
            KTb = [Buf("KT%d" % i) for i in range(NT)]
            VAb = [Buf("VA%d" % i) for i in range(NT)]
            K.memset("pool", VA.t[:], 1.0, w=VAb)
            posi = K.sb(es, "posi", [128, 2 * NT], I32)
            pos_sem = K.dsem("pos")
            K.dma("sp", posi.t[:], pos_d[:, :], pos_sem, w=[posi])
            posf = K.sb(es, "posf", [128, 2 * NT], F32)
            K.copy("dve", posf.t[:], posi.t[:], r=[posi], w=[posf])
            NA = 2 * NT
            angi = K.sb(es, "angi", [128, NA, 16], I32)
            rsin = K.sb(es, "rsin", [128, NA, 16], F32)
            rcos = K.sb(es, "rcos", [128, NA, 16], F32)
            x1o = K.sb(es, "x1o", [128, D], F32)
            ang_t, angn_t, angm_t = lxc, lm_, x1o
            angv = lxc.t[:].rearrange("p (a f) -> p a f", f=16)
            angnv = lm_.t[:].rearrange("p (a f) -> p a f", f=16)
            angmv = x1o.t[:, 0:512].rearrange("p (a f) -> p a f", f=16)
            TWO_PI = 2.0 * math.pi
            C1 = 6.28125
            C2 = TWO_PI - C1
            PI_IN = 3.1415925

            def wrap():
                K.ts("dve", angmv, angv, math.pi, -TWO_PI, ALU.is_gt, ALU.mult, r=[ang_t], w=[angm_t])
                K.tt("dve", angv, angv, angmv, ALU.add, r=[ang_t, angm_t], w=[ang_t])
                K.ts("dve", angmv, angv, -math.pi, TWO_PI, ALU.is_lt, ALU.mult, r=[ang_t], w=[angm_t])
                K.tt("dve", angv, angv, angmv, ALU.add, r=[ang_t, angm_t], w=[ang_t])
                K.ts("dve", angv, angv, PI_IN, -PI_IN, ALU.min, ALU.max, r=[ang_t], w=[ang_t])

            pf = posf.t[:, :].unsqueeze(2).to_broadcast([128, NA, 16])
            fr = rows.t[:, R_IF:R_IF + 16].unsqueeze(1).to_broadcast([128, NA, 16])
            K.tt("dve", angv, pf, fr, ALU.mult, r=[posf, rows], w=[ang_t])
            K.ts("dve", angnv, angv, 1.0 / TWO_PI, None, ALU.mult, ALU.bypass, r=[ang_t], w=[angn_t])
            K.copy("dve", angi.t[:], angnv, r=[angn_t], w=[angi])
            K.copy("dve", angnv, angi.t[:], r=[angi], w=[angn_t])
            K.stt(angv, angnv, -C1, angv, ALU.mult, ALU.add, r=[angn_t, ang_t], w=[ang_t])
            K.stt(angv, angnv, -C2, angv, ALU.mult, ALU.add, r=[angn_t, ang_t], w=[ang_t])
            wrap()
            K.act(rsin.t[:], angv, AF.Sin, r=[ang_t], w=[rsin])
            K.ts("dve", angv, angv, math.pi / 2, None, ALU.add, ALU.bypass, r=[ang_t], w=[ang_t])
            wrap()
            K.act(rcos.t[:], angv, AF.Sin, r=[ang_t], w=[rcos])

            xr = [K.sb(es, "x%d" % i, [128, D], F32) for i in range(3)]
            xr_sem = [K.dsem("xs%d" % i) for i in range(3)]
            xr_i = [0]

            def load_x(src_ap):
                i = xr_i[0] % 3
                xr_i[0] += 1
                tb = xr[i]
                K.dma("sp", tb.t[:], src_ap, xr_sem[i], w=[tb])
                return tb

            junk = K.sb(es, "junk", [128, D], BF16)
            hn = Ring([K.sb(es, "hn%d" % i, [128, D], BF16) for i in range(2)])
            hnT = [K.sb(es, "hnT%d" % i, [128, 8, 512], BF16) for i in range(2)]
            ss4 = K.sb(es, "ss4", [128, 8], F32)
            mla = K.sb(es, "mla", [128, 416], F32)
            ssm = K.sb(es, "ssm", [128, 4], F32)
            cqn = K.sb(es, "cqn", [128, 384], BF16)
            cqT = K.sb(es, "cqT", [128, 3, 128], BF16)
            qf = K.sb(es, "qf", [128, 8, 96], F32)
            sq = K.sb(es, "sq", [128, 8, 96], F32)
            ssh = K.sb(es, "ssh", [128, 16], F32)
            Qb = K.sb(es, "Qb", [128, 8, 96], BF16)
            Kb = K.sb(es, "Kb", [128, 8, 96], BF16)
            QT = [K.sb(es, "QT%d" % i, [128, 8, 128], BF16) for i in range(2)]
            rk = K.sb(es, "rk", [128, 32], F32)
            rk2 = K.sb(es, "rk2", [128, 32], F32)
            rt = [K.sb(es, "rt%d" % i, [128, 8, 16], F32) for i in range(2)]
            PT = Ring([K.sb(es, "PT%d" % i, [128, 4, 128], BF16) for i in range(4)])
            rec = K.sb(es, "rec", [128, 8], F32)
            attn = K.sb(es, "attn", [128, 8, 64], F32)
            ssa = K.sb(es, "ssa", [128, 2], F32)
            attb = K.sb(es, "attb", [128, 512], BF16)
            attT = [K.sb(es, "attT%d" % i, [128, 4, 128], BF16) for i in range(8)]
            xlb = K.sb(es, "xlb", [128, 4, 515], F32)
            hst = K.sb(es, "hst", [128, 4], F32)
            ybf = [K.sb(es, "ybf%d" % i, [128, 4, 512], BF16) for i in range(2)]
            ssl = [K.sb(es, "ssl%d" % i, [128, 4, 2], F32) for i in range(2)]
            x1_sem = K.dsem("x1o")

            gen = Ring(banks[4:8])
            sbk = Ring(banks[0:2])
            obk = banks[2:4]
            NG = 2 * NT
            NSG = NG // 4

            def N_unit(G, jj):
                r0 = G * 512 + jj * 128
                col_ = (G % 2) * 4 + jj
                xt = load_x(x_d[r0:r0 + 128, :])
                h_ = hn.next()
                K.act(junk.t[:], xt.t[:], AF.Square, r=[xt], w=[junk, ss4], accum=ss4.t[:, col_:col_ + 1])
                rstd(ss4, ss4.t[:, col_:col_ + 1], ss4, ss4.t[:, col_:col_ + 1], D, 1)
                K.act(h_.t[:], xt.t[:], AF.Copy, r=[xt, ss4], w=[h_], scale=ss4.t[:, col_:col_ + 1])
                bk = gen.next()
                bv = bk.t[:].bitcast(BF16)
                for k in range(8):
                    K.tr(bv[:, k * 128:(k + 1) * 128], h_.t[:, k * 128:(k + 1) * 128], ident.t[:],
                         r=[h_, ident], w=[bk], sig=(k == 7))
                hT = hnT[G % 2]
                K.copy("dve", hT.t[:, :, jj * 128:(jj + 1) * 128],
                       bv.rearrange("p (k t) -> p k t", k=8), r=[bk], w=[hT])

            pstate = {}

            def P_stage(g, i):
                G, j = g // 4, g % 4
                t = g % NT
                hT = hnT[G % 2]
                if i == 0:
                    bk = gen.next()
                    for k in range(8):
                        K.mm(bk.t[:, 0:416], hT.t[:, k, j * 128:(j + 1) * 128], Win.t[:, k, 0:416],
                             k == 0, k == 7, r=[hT, Win], w=[bk], sig=(k == 7))
                    K.copy("act", mla.t[:], bk.t[:, 0:416], r=[bk], w=[mla])
                    K.act(junk.t[:, 0:256], mla.t[:, 0:256], AF.Square, r=[mla], w=[junk, ssm], accum=ssm.t[:, 0:1])
                    K.act(junk.t[:, 0:128], mla.t[:, 256:384], AF.Square, r=[mla], w=[junk, ssm], accum=ssm.t[:, 1:2])
                    K.act(junk.t[:, 0:32], mla.t[:, 384:416], AF.Square, r=[mla], w=[junk, ssm], accum=ssm.t[:, 2:3])
                    rstd(ssm, ssm.t[:, 0:1], ssm, ssm.t[:, 0:1], 256, 1)
                    rstd(ssm, ssm.t[:, 1:2], ssm, ssm.t[:, 1:2], 128, 1)
                    K.act(cqn.t[:, 0:256], mla.t[:, 0:256], AF.Copy, r=[mla, ssm], w=[cqn], scale=ssm.t[:, 0:1])
                    K.act(cqn.t[:, 256:384], mla.t[:, 256:384], AF.Copy, r=[mla, ssm], w=[cqn], scale=ssm.t[:, 1:2])
                elif i == 1:
                    bk = gen.next()
                    bv = bk.t[:].bitcast(BF16)
                    for k in range(3):
                        K.tr(bv[:, k * 128:(k + 1) * 128], cqn.t[:, k * 128:(k + 1) * 128], ident.t[:],
                             r=[cqn, ident], w=[bk], sig=(k == 2))
                    K.copy("dve", cqT.t[:], bv[:, 0:384].rearrange("p (k t) -> p k t", k=3), r=[bk], w=[cqT])
                elif i == 2:
                    bq0, bq1 = gen.next(), gen.next()
                    for k in range(2):
                        K.mm(bq0.t[:, 0:384], cqT.t[:, k, :], Wuq.t[:, k, 0:384], k == 0, k == 1,
                             r=[cqT, Wuq], w=[bq0], sig=(k == 1))
                    for k in range(2):
                        K.mm(bq1.t[:, 0:384], cqT.t[:, k, :], Wuq.t[:, k, 384:768], k == 0, k == 1,
                             r=[cqT, Wuq], w=[bq1], sig=(k == 1))
                    qf2 = qf.t[:].rearrange("p h d -> p (h d)")
                    K.copy("act", qf2[:, 0:384], bq0.t[:, 0:384], r=[bq0], w=[qf])
                    K.copy("act", qf2[:, 384:768], bq1.t[:, 0:384], r=[bq1], w=[qf])
                    K.tt("pool", sq.t[:], qf.t[:], qf.t[:], ALU.mult, r=[qf], w=[sq])
                    K.op("dve", lambda e: e.tensor_reduce(out=ssh.t[:, 0:8], in_=sq.t[:], axis=AX.X, op=ALU.add),
                         r=[sq], w=[ssh])
                    bk0, bk1 = gen.next(), gen.next()
                    K.mm(bk0.t[:, :], cqT.t[:, 2, :], Wukv.t[:, 0:512], True, True, r=[cqT, Wukv], w=[bk0], sig=True)
                    K.mm(bk1.t[:, :], cqT.t[:, 2, :], Wukv.t[:, 512:1024], True, True, r=[cqT, Wukv], w=[bk1], sig=True)
                    pstate["kv"] = (bk0, bk1)
                    for hb, bkk in ((0, bk0), (1, bk1)):
                        kvv = bkk.t[:].rearrange("p (h d) -> p h d", h=4)
                        K.act(sq.t[:, hb * 4:(hb + 1) * 4, 0:64], kvv[:, :, 0:64], AF.Square, r=[bkk], w=[sq])
                        K.copy("act", VA.t[:, t, hb * 4:(hb + 1) * 4, 0:64], kvv[:, :, 64:128], r=[bkk], w=[VAb[t]])
                    K.op("dve", lambda e: e.tensor_reduce(out=ssh.t[:, 8:16], in_=sq.t[:, :, 0:64], axis=AX.X, op=ALU.add),
                         r=[sq], w=[ssh])
                    K.ts("dve", ssh.t[:, 8:16], ssh.t[:, 8:16], ssm.t[:, 2:3], None, ALU.add, ALU.bypass,
                         r=[ssh, ssm], w=[ssh])
                    rstd(ssh, ssh.t[:], ssh, ssh.t[:], 96, 16)
                elif i == 3:
                    bk0, bk1 = pstate["kv"]
                    K.tt("dve", qf.t[:], qf.t[:], ssh.t[:, 0:8].unsqueeze(2).to_broadcast([128, 8, 96]), ALU.mult,
                         r=[qf, ssh], w=[qf])
                    K.tt("dve", qf.t[:], qf.t[:], gq.t[:, :].unsqueeze(1).to_broadcast([128, 8, 96]), ALU.mult,
                         r=[qf, gq], w=[qf])
                    cosb = rcos.t[:, g, :].unsqueeze(1).to_broadcast([128, 8, 16])
                    sinb = rsin.t[:, g, :].unsqueeze(1).to_broadcast([128, 8, 16])
                    K.copy("act", Qb.t[:, :, 0:64], qf.t[:, :, 0:64], r=[qf], w=[Qb])
                    K.tt("pool", rt[0].t[:], qf.t[:, :, 64:80], cosb, ALU.mult, r=[qf, rcos], w=[rt[0]])
                    K.tt("pool", rt[1].t[:], qf.t[:, :, 80:96], sinb, ALU.mult, r=[qf, rsin], w=[rt[1]])
                    K.tt("pool", Qb.t[:, :, 64:80], rt[0].t[:], rt[1].t[:], ALU.subtract, r=[rt[0], rt[1]], w=[Qb])
                    K.tt("pool", rt[0].t[:], qf.t[:, :, 80:96], cosb, ALU.mult, r=[qf, rcos], w=[rt[0]])
                    K.tt("pool", rt[1].t[:], qf.t[:, :, 64:80], sinb, ALU.mult, r=[qf, rsin], w=[rt[1]])
                    K.tt("pool", Qb.t[:, :, 80:96], rt[0].t[:], rt[1].t[:], ALU.add, r=[rt[0], rt[1]], w=[Qb])
                    for hb, bkk in ((0, bk0), (1, bk1)):
                        kvv = bkk.t[:].rearrange("p (h d) -> p h d", h=4)
                        K.tt("dve", sq.t[:, hb * 4:(hb + 1) * 4, 0:64], kvv[:, :, 0:64],
                             ssh.t[:, 8 + hb * 4:12 + hb * 4].unsqueeze(2).to_broadcast([128, 4, 64]), ALU.mult,
                             r=[bkk, ssh], w=[sq])
                    K.tt("dve", Kb.t[:, :, 0:64], sq.t[:, :, 0:64],
                         rows.t[:, R_GK:R_GK + 64].unsqueeze(1).to_broadcast([128, 8, 64]), ALU.mult,
                         r=[sq, rows], w=[Kb])
                    K.tt("pool", rk.t[:], mla.t[:, 384:416], rows.t[:, R_GK + 64:R_GK + 96], ALU.mult, r=[mla, rows], w=[rk])
                    c16, s16 = rcos.t[:, g, :], rsin.t[:, g, :]
                    K.tt("pool", rt[0].t[:, 0, :], rk.t[:, 0:16], c16, ALU.mult, r=[rk, rcos], w=[rt[0]])
                    K.tt("pool", rt[1].t[:, 0, :], rk.t[:, 16:32], s16, ALU.mult, r=[rk, rsin], w=[rt[1]])
                    K.tt("pool", rk2.t[:, 0:16], rt[0].t[:, 0, :], rt[1].t[:, 0, :], ALU.subtract, r=[rt[0], rt[1]], w=[rk2])
                    K.tt("pool", rt[0].t[:, 0, :], rk.t[:, 16:32], c16, ALU.mult, r=[rk, rcos], w=[rt[0]])
                    K.tt("pool", rt[1].t[:, 0, :], rk.t[:, 0:16], s16, ALU.mult, r=[rk, rsin], w=[rt[1]])
                    K.tt("pool", rk2.t[:, 16:32], rt[0].t[:, 0, :], rt[1].t[:, 0, :], ALU.add, r=[rt[0], rt[1]], w=[rk2])
                    K.tt("dve", Kb.t[:, :, 64:96], rk2.t[:, :].unsqueeze(1).to_broadcast([128, 8, 32]),
                         ssh.t[:, 8:16].unsqueeze(2).to_broadcast([128, 8, 32]), ALU.mult, r=[rk2, ssh], w=[Kb])
                elif i == 4:
                    qt = QT[g % 2]
                    for src, dst_b, dst_ap in ((Qb, qt, qt.t[0:96, :, :]),
                                               (Kb, KTb[t], KT.t[0:96, :, t * 128:(t + 1) * 128])):
                        bk = gen.next()
                        bv = bk.t[:].bitcast(BF16)
                        for h in range(8):
                            K.tr(bv[0:96, h * 128:(h + 1) * 128], src.t[:, h, :], ident.t[:],
                                 r=[src, ident], w=[bk], sig=(h == 7))
                        K.copy("dve", dst_ap, bv[0:96, :].rearrange("p (h t) -> p h t", h=8), r=[bk], w=[dst_b])

            def A_head(g, h):
                t = g % NT
                qt = QT[g % 2]
                ob = obk[h // 4]
                oap = ob.t[:, (h % 4) * 65:(h % 4) * 65 + 65]
                for g0 in range(0, t + 1, 4):
                    n = min(4, t + 1 - g0)
                    sb_ = sbk.next()
                    for i in range(n):
                        kt = g0 + i
                        K.mm(sb_.t[:, i * 128:(i + 1) * 128], KT.t[0:96, h, kt * 128:(kt + 1) * 128],
                             qt.t[0:96, h, :], True, True, r=[KTb[kt], qt], w=[sb_], sig=(i == n - 1))
                    pt = PT.next()
                    K.act(pt.t[:, 0:n, :].rearrange("p a b -> p (a b)"), sb_.t[:, 0:n * 128], AF.Exp,
                          r=[sb_], w=[pt])
                    if g0 + n - 1 == t:
                        K.memset("pool", pt.t[64:128, n - 1, 0:64], 0.0, w=[pt])
                    for i in range(n):
                        kt = g0 + i
                        K.mm(oap, pt.t[:, i, :], VA.t[:, kt, h, :], kt == 0, kt == t,
                             r=[pt, VAb[kt]], w=[ob], sig=(i == n - 1))

            def F_tile(g):
                for hb in range(2):
                    ov = obk[hb].t[:, 0:260].rearrange("p (h d) -> p h d", h=4)
                    K.op("dve", lambda e, ov=ov, hb=hb: e.reciprocal(out=rec.t[:, hb * 4:(hb + 1) * 4], in_=ov[:, :, 64]),
                         r=[obk[hb]], w=[rec])
                    K.tt("dve", attn.t[:, hb * 4:(hb + 1) * 4, :], ov[:, :, 0:64],
                         rec.t[:, hb * 4:(hb + 1) * 4].unsqueeze(2).to_broadcast([128, 4, 64]), ALU.mult,
                         r=[obk[hb], rec], w=[attn])
                a2 = attn.t[:].rearrange("p h d -> p (h d)")
                K.act(junk.t[:, 0:512], a2, AF.Square, r=[attn], w=[junk, ssa], accum=ssa.t[:, 0:1])
                rstd(ssa, ssa.t[:, 0:1], ssa, ssa.t[:, 0:1], 512, 1)
                K.act(attb.t[:], a2, AF.Copy, r=[attn, ssa], w=[attb], scale=ssa.t[:, 0:1])
                at = attT[g % 8]
                bk = gen.next()
                bv = bk.t[:].bitcast(BF16)
                for k in range(4):
                    K.tr(bv[:, k * 128:(k + 1) * 128], attb.t[:, k * 128:(k + 1) * 128], ident.t[:],
                         r=[attb, ident], w=[bk], sig=(k == 3))
                K.copy("dve", at.t[:], bv[:, 0:512].rearrange("p (k t) -> p k t", k=4), r=[bk], w=[at])

            def L_chunk(G, c):
                hT = hnT[G % 2]
                yb = ybf[G % 2]
                sl = ssl[G % 2]
                if G % (NT // 4) == 0 and c == 0:
                    K.memset("pool", xlb.t[:, :, 0:3], 0.0, w=[xlb])
                    K.memset("pool", hst.t[:], 0.0, w=[hst])
                bx, bg = gen.next(), gen.next()
                for k in range(8):
                    K.mm(bx.t[:, :], Win.t[:, k, 416 + c * 128:416 + (c + 1) * 128], hT.t[:, k, :],
                         k == 0, k == 7, r=[Win, hT], w=[bx], sig=(k == 7))
                K.copy("act", xlb.t[:, c, 3:515], bx.t[:, :], r=[bx], w=[xlb])
                for k in range(8):
                    K.mm(bg.t[:, :], Win.t[:, k, 928 + c * 128:928 + (c + 1) * 128], hT.t[:, k, :],
                         k == 0, k == 7, r=[Win, hT], w=[bg], sig=(k == 7))
                K.act(li_.t[:], bg.t[:, :], AF.Gelu_apprx_tanh, r=[bg], w=[li_])
                K.ts("dve", lxc.t[:], xlb.t[:, c, 3:515], col(C_LCW + c * 4 + 3), col(C_LCB + c),
                     ALU.mult, ALU.add, r=[xlb, cols], w=[lxc])
                for jj in range(3):
                    K.stt(lxc.t[:], xlb.t[:, c, jj:jj + 512], col(C_LCW + c * 4 + jj), lxc.t[:],
                          ALU.mult, ALU.add, r=[xlb, cols, lxc], w=[lxc])
                K.copy("pool", xlb.t[:, c, 0:3], xlb.t[:, c, 512:515], r=[xlb], w=[xlb])
                K.copy("pool", lxb.t[:], lxc.t[:], r=[lxc], w=[lxb])
                br, bi = gen.next(), gen.next()
                K.mm(br.t[:, :], Wr.t[:, c, :], lxb.t[:], True, True, r=[Wr, lxb], w=[br], sig=True)
                K.mm(bi.t[:, :], Wi.t[:, c, :], lxb.t[:], True, True, r=[Wi, lxb], w=[bi], sig=True)
                K.act(lr_.t[:], br.t[:, :], AF.Sigmoid, r=[br, cols], w=[lr_], bias=col(C_BR + c))
                K.act(lm_.t[:], bi.t[:, :], AF.Sigmoid, r=[bi, cols], w=[lm_], bias=col(C_BI + c))
                K.act(lr_.t[:], lr_.t[:], AF.Exp, r=[lr_, clam], w=[lr_], scale=clam.t[:, c:c + 1])
                K.tt("dve", lm_.t[:], lm_.t[:], lxc.t[:], ALU.mult, r=[lm_, lxc], w=[lm_])
                K.act(lxc.t[:], lr_.t[:], AF.Square, r=[lr_], w=[lxc])
                K.act(lxc.t[:], lxc.t[:], AF.Sqrt, r=[lxc], w=[lxc], scale=-1.0, bias=1.0)
                K.tt("dve", lm_.t[:], lm_.t[:], lxc.t[:], ALU.mult, r=[lm_, lxc], w=[lm_])
                K.op("dve", lambda e, c=c: e.tensor_tensor_scan(out=lxc.t[:], data0=lr_.t[:], data1=lm_.t[:],
                                                                 initial=hst.t[:, c:c + 1], op0=ALU.mult, op1=ALU.add),
                     r=[lr_, lm_, hst], w=[lxc])
                K.copy("pool", hst.t[:, c:c + 1], lxc.t[:, 511:512], r=[lxc], w=[hst])
                K.tt("dve", lxc.t[:], lxc.t[:], li_.t[:], ALU.mult, r=[lxc, li_], w=[lxc])
                K.copy("pool", yb.t[:, c, :], lxc.t[:], r=[lxc], w=[yb])
                K.act(lm_.t[:], lxc.t[:], AF.Square, r=[lxc], w=[lm_])
                bs_ = gen.next()
                for j in range(4):
                    K.mm(bs_.t[:, (c * 4 + j) * 2:(c * 4 + j) * 2 + 2], lm_.t[:, j * 128:(j + 1) * 128], ones2.t[:],
                         True, True, r=[lm_, ones2], w=[bs_], sig=(j == 3))
                slv = sl.t[:].rearrange("p j o -> p (j o)")
                if c == 0:
                    K.copy("dve", slv, bs_.t[:, 0:8], r=[bs_], w=[sl])
                else:
                    K.tt("dve", slv, slv, bs_.t[:, c * 8:c * 8 + 8], ALU.add, r=[bs_, sl], w=[sl])
                if c == 3:
                    rstd(sl, slv, sl, slv, 512, 8)

            def O_tile(G, j):
                g = G * 4 + j
                r0 = g * 128
                at = attT[g % 8]
                yb = ybf[G % 2]
                sl = ssl[G % 2]
                xt = load_x(x_d[r0:r0 + 128, :])
                for hh in range(2):
                    ba = gen.next()
                    for k in range(4):
                        K.mm(ba.t[:, :], at.t[:, k, :], Wout.t[:, k, hh * 512:(hh + 1) * 512], k == 0, k == 3,
                             r=[at, Wout], w=[ba], sig=(k == 3))
                    K.tt("dve", x1o.t[:, hh * 512:(hh + 1) * 512], ba.t[:, :], xt.t[:, hh * 512:(hh + 1) * 512], ALU.add,
                         r=[ba, xt], w=[x1o])
                    bl = gen.next()
                    for k in range(4):
                        K.mm(bl.t[:, :], yb.t[:, k, j * 128:(j + 1) * 128], Wout.t[:, 4 + k, hh * 512:(hh + 1) * 512],
                             k == 0, k == 3, r=[yb, Wout], w=[bl], sig=(k == 3))
                    K.stt(x1o.t[:, hh * 512:(hh + 1) * 512], bl.t[:, :], sl.t[:, j, 0:1], x1o.t[:, hh * 512:(hh + 1) * 512],
                          ALU.mult, ALU.add, r=[bl, sl, x1o], w=[x1o])
                K.dma("sp", x1_d[r0:r0 + 128, :], x1o.t[:], x1_sem, r=[x1o])


            for jj in range(4):
                N_unit(0, jj)
            for i in range(5):
                P_stage(0, i)
            for g in range(NG):
                G, j = g // 4, g % 4
                for h in range(8):
                    A_head(g, h)
                    if g + 1 < NG and (g + 1) % NT != 0:
                        if h == 0:
                            P_stage(g + 1, 0)
                        elif h == 1:
                            P_stage(g + 1, 1)
                        elif h == 2:
                            P_stage(g + 1, 2)
                        elif h == 3:
                            P_stage(g + 1, 3)
                        elif h == 6:
                            P_stage(g + 1, 4)
                    if h == 4 and G + 1 < NSG:
                        if j == 1:
                            N_unit(G + 1, 0)
                            N_unit(G + 1, 1)
                        elif j == 2:
                            N_unit(G + 1, 2)
                            N_unit(G + 1, 3)
                    if h == 5:
                        L_chunk(G, j)
                    if h == 7 and G >= 1:
                        O_tile(G - 1, j)
                F_tile(g)
                if g + 1 < NG and (g + 1) % NT == 0:
                    for i in range(5):
                        P_stage(g + 1, i)
            for j in range(4):
                O_tile(NSG - 1, j)
            x1_done = Tok(None, x1_sem.h, x1_sem.val)
            K.barrier([x1_done])
            K.flush()

        with ExitStack() as es:
            Wup = K.sb(es, "Wup", [128, 8, 2 * DFF], BF16)
            Wdn = K.sb(es, "Wdn", [128, NCH, D], BF16)
            stage = [K.sb(es, "fstg%d" % i, [128, 1024], F32) for i in range(2)]
            stage_sem = [K.dsem("fstg%d" % i) for i in range(2)]
            wup_b = [Buf("wup%d" % c) for c in range(2 * NCH)]
            wdn_b = [Buf("wdn%d" % c) for c in range(NCH)]
            w_up_v = w_up_d.rearrange("(k p) n -> p k n", p=128)
            w_dn_v = w_down_d.rearrange("(c p) n -> p c n", p=128)
            si = [0]
            first = [True]

            def fl(dst_b, dst_ap, src_ap, n, scale_col, eng):
                i = si[0] % 2
                si[0] += 1
                st_ = stage[i]
                K.dma("sp", st_.t[:, 0:n], src_ap, stage_sem[i], w=[st_])
                if scale_col is not None:
                    K.ts("pool", dst_ap, st_.t[:, 0:n], scale_col, 0.0, ALU.mult, ALU.add, r=[st_, cols], w=[dst_b])
                else:
                    K.copy(eng, dst_ap, st_.t[:, 0:n], r=[st_], w=[dst_b])

            for c in range(NCH):
                for half in range(2):
                    cc = half * NCH + c
                    c0 = half * DFF + c * 128
                    i = si[0] % 2
                    si[0] += 1
                    st_ = stage[i]
                    K.dma("sp", st_.t[:, :].rearrange("p (k n) -> p k n", k=8), w_up_v[:, :, c0:c0 + 128], stage_sem[i], w=[st_])
                    K.tt("pool", Wup.t[:, :, c0:c0 + 128], st_.t[:, :].rearrange("p (k n) -> p k n", k=8),
                         cols.t[:, C_GFFN:C_GFFN + 8].unsqueeze(2).to_broadcast([128, 8, 128]), ALU.mult,
                         r=[st_, cols], w=[wup_b[cc]])
                fl(wdn_b[c], Wdn.t[:, c, :], w_dn_v[:, c, :], 1024, None, "pool")

            NX = 4
            xr = [K.sb(es, "fx%d" % i, [128, D], F32) for i in range(NX)]
            xr_sem = [K.dsem("fxs%d" % i) for i in range(NX)]
            xo_sem = [K.dsem("fxo%d" % i) for i in range(NX)]
            xi = [0]
            junk = K.sb(es, "fjunk", [128, D], BF16)
            hf = K.sb(es, "hf", [128, D], BF16)
            hfT = [K.sb(es, "hfT%d" % i, [128, 8, WIN + 2], BF16) for i in range(2)]
            ssf = K.sb(es, "ssf", [128, 4], F32)
            cg = [K.sb(es, "cg%d" % i, [128, WIN], F32) for i in range(2)]
            cv = [K.sb(es, "cv%d" % i, [128, WIN], F32) for i in range(2)]
            G = [K.sb(es, "G%d" % i, [128, NCH, WIN], BF16) for i in range(2)]
            upr = Ring(banks[0:4])
            dnr = Ring(banks[4:8])
            nwin = TOK // WIN
            wx = {}

            def FA(wdw):
                tok0 = wdw * WIN
                ht = hfT[wdw % 2]
                hprev = hfT[(wdw + 1) % 2]
                if wdw % (S // WIN) == 0:
                    K.memset("pool", ht.t[:, :, 0:2], 0.0, w=[ht])
                else:
                    K.copy("pool", ht.t[:, :, 0:2], hprev.t[:, :, WIN:WIN + 2], r=[hprev], w=[ht])
                xs_ = []
                for j in range(2):
                    i = xi[0] % NX
                    xi[0] += 1
                    xt = xr[i]
                    r0 = tok0 + j * 128
                    K.dma("sp", xt.t[:], x1_d[r0:r0 + 128, :], xr_sem[i], w=[xt])
                    if first[0]:
                        K.q["sp"].ops[-1][0].append((x1_done.sem, x1_done.val))
                        first[0] = False
                    xs_.append((xt, i))
                    cc_ = (wdw % 2) * 2 + j
                    K.act(junk.t[:], xt.t[:], AF.Square, r=[xt], w=[junk, ssf], accum=ssf.t[:, cc_:cc_ + 1])
                    rstd(ssf, ssf.t[:, cc_:cc_ + 1], ssf, ssf.t[:, cc_:cc_ + 1], D, 1)
                    K.act(hf.t[:], xt.t[:], AF.Copy, r=[xt, ssf], w=[hf], scale=ssf.t[:, cc_:cc_ + 1])
                    bk = dnr.next()
                    bv = bk.t[:].bitcast(BF16)
                    for k in range(8):
                        K.tr(bv[:, k * 128:(k + 1) * 128], hf.t[:, k * 128:(k + 1) * 128], ident.t[:],
                             r=[hf, ident], w=[bk], sig=(k == 7))
                    K.copy("dve", ht.t[:, :, 2 + j * 128:2 + (j + 1) * 128],
                           bv.rearrange("p (k t) -> p k t", k=8), r=[bk], w=[ht])
                wx[wdw] = xs_

            FA(0)
            for wdw in range(nwin):
                tok0 = wdw * WIN
                ht = hfT[wdw % 2]
                xs_ = wx.pop(wdw)
                Gw = G[wdw % 2]
                for c in range(NCH):
                    if c == 8 and wdw + 1 < nwin:
                        FA(wdw + 1)
                    bg_, bv_ = upr.next(), upr.next()
                    for half, bkk in ((0, bg_), (1, bv_)):
                        c0 = half * DFF + c * 128
                        for k in range(8):
                            K.mm(bkk.t[:, 0:WIN + 2], Wup.t[:, k, c0:c0 + 128], ht.t[:, k, :], k == 0, k == 7,
                                 r=[wup_b[half * NCH + c], ht], w=[bkk], sig=(k == 7))
                    cgt, cvt = cg[c % 2], cv[c % 2]
                    for half, bkk, dst in ((0, bg_, cgt), (1, bv_, cvt)):
                        ci = half * NCH + c
                        K.act(dst.t[:], bkk.t[:, 2:WIN + 2], AF.Identity, r=[bkk, cols], w=[dst],
                              scale=col(C_FCW + ci * 3 + 2), bias=col(C_FCB + ci))
                        K.stt(dst.t[:], bkk.t[:, 1:WIN + 1], col(C_FCW + ci * 3 + 1), dst.t[:], ALU.mult, ALU.add,
                              r=[bkk, cols, dst], w=[dst])
                        K.stt(dst.t[:], bkk.t[:, 0:WIN], col(C_FCW + ci * 3 + 0), dst.t[:], ALU.mult, ALU.add,
                              r=[bkk, cols, dst], w=[dst])
                    K.act(cgt.t[:], cgt.t[:], AF.Gelu_apprx_tanh, r=[cgt], w=[cgt])
                    K.tt("pool", Gw.t[:, c, :], cgt.t[:], cvt.t[:], ALU.mult, r=[cgt, cvt], w=[Gw])
                for j in range(2):
                    xt, i = xs_[j]
                    r0 = tok0 + j * 128
                    for hh in range(2):
                        bd = dnr.next()
                        for c in range(NCH):
                            K.mm(bd.t[:, :], Gw.t[:, c, j * 128:(j + 1) * 128], Wdn.t[:, c, hh * 512:(hh + 1) * 512],
                                 c == 0, c == NCH - 1, r=[Gw, wdn_b[c]], w=[bd], sig=(c == NCH - 1))
                        K.tt("dve", xt.t[:, hh * 512:(hh + 1) * 512], bd.t[:, :], xt.t[:, hh * 512:(hh + 1) * 512], ALU.add,
                             r=[bd, xt], w=[xt])
                    K.dma("sp", x2_d[r0:r0 + 128, :], xt.t[:], xo_sem[i], r=[xt])
            x2_toks = [Tok(None, sm.h, sm.val) for sm in xo_sem]
            K.barrier(x2_toks)
            K.flush()

        with ExitStack() as es:
            Wpg = K.sb(es, "Wpg", [128, 8, D], BF16)
            Wpl = K.sb(es, "Wpl", [128, 2, D], BF16)
            brow = K.sb(es, "brow", [128, D], BF16)
            one0 = K.sb(es, "one0", [128, 128], BF16)
            browf = K.sb(es, "browf", [128, D], F32)
            stage = [K.sb(es, "pstg%d" % i, [128, 1024], F32) for i in range(2)]
            stage_sem = [K.dsem("pstg%d" % i) for i in range(2)]
            si = [0]
            w_pg_v = w_pg_d.rearrange("(k p) n -> p k n", p=128)
            w_pl_v = w_ple_d.rearrange("(k p) n -> p k n", p=128)
            for k in range(8):
                i = si[0] % 2
                si[0] += 1
                K.dma("sp", stage[i].t[:], w_pg_v[:, k, :], stage_sem[i], w=[stage[i]])
                K.act(Wpg.t[:, k, :], stage[i].t[:], AF.Copy, r=[stage[i], cols], w=[Wpg], scale=col(C_GPLE + k))
            for k in range(2):
                i = si[0] % 2
                si[0] += 1
                K.dma("sp", stage[i].t[:], w_pl_v[:, k, :], stage_sem[i], w=[stage[i]])
                K.copy("act", Wpl.t[:, k, :], stage[i].t[:], r=[stage[i]], w=[Wpl])
            K.memset("pool", browf.t[:], 0.0, w=[browf])
            K.memset("pool", one0.t[:], 0.0, w=[one0])
            K.memset("pool", one0.t[0:1, :], 1.0, w=[one0])
            bsem = K.dsem("brow")
            K.dma("sp", browf.t[0:1, :], bple_d[0:1, :], bsem, w=[browf])
            K.copy("pool", brow.t[:], browf.t[:], r=[browf], w=[brow])

            NX = 4
            xr = [K.sb(es, "px%d" % i, [128, D], F32) for i in range(NX)]
            xr_sem = [K.dsem("pxs%d" % i) for i in range(NX)]
            xo_sem = [K.dsem("pxo%d" % i) for i in range(NX)]
            pr = [K.sb(es, "pp%d" % i, [128, 256], F32) for i in range(2)]
            pr_sem = [K.dsem("pps%d" % i) for i in range(2)]
            junk = K.sb(es, "pjunk", [128, D], BF16)
            hp = [K.sb(es, "hp%d" % i, [128, D], BF16) for i in range(2)]
            pb = [K.sb(es, "pb%d" % i, [128, 256], BF16) for i in range(2)]
            hpT = [K.sb(es, "hpT%d" % i, [128, 10, 128], BF16) for i in range(3)]
            ssp = K.sb(es, "ssp", [128, 4], F32)
            sg = [K.sb(es, "sg%d" % i, [128, 512], F32) for i in range(4)]
            pring = Ring(banks)
            out_toks = []
            NTI = TOK // 128

            def PA(ti):
                r0 = ti * 128
                i = ti % NX
                xt = xr[i]
                K.dma("sp", xt.t[:], x2_d[r0:r0 + 128, :], xr_sem[i], w=[xt])
                if ti == 0:
                    for tk in x2_toks:
                        K.q["sp"].ops[-1][0].append((tk.sem, tk.val))
                pt_ = pr[ti % 2]
                K.dma("sp", pt_.t[:], p_d[r0:r0 + 128, :], pr_sem[ti % 2], w=[pt_])
                cc_ = ti % 4
                hp_ = hp[ti % 2]
                pb_ = pb[ti % 2]
                K.act(junk.t[:], xt.t[:], AF.Square, r=[xt], w=[junk, ssp], accum=ssp.t[:, cc_:cc_ + 1])
                rstd(ssp, ssp.t[:, cc_:cc_ + 1], ssp, ssp.t[:, cc_:cc_ + 1], D, 1)
                K.act(hp_.t[:], xt.t[:], AF.Copy, r=[xt, ssp], w=[hp_], scale=ssp.t[:, cc_:cc_ + 1])
                K.copy("pool", pb_.t[:], pt_.t[:], r=[pt_], w=[pb_])
                hT = hpT[ti % 3]
                bk = pring.next()
                bv = bk.t[:].bitcast(BF16)
                for k in range(8):
                    K.tr(bv[:, k * 128:(k + 1) * 128], hp_.t[:, k * 128:(k + 1) * 128], ident.t[:],
                         r=[hp_, ident], w=[bk], sig=(k == 7))
                K.copy("dve", hT.t[:, 0:8, :], bv.rearrange("p (k t) -> p k t", k=8), r=[bk], w=[hT])
                bk = pring.next()
                bv = bk.t[:].bitcast(BF16)
                for k in range(2):
                    K.tr(bv[:, k * 128:(k + 1) * 128], pb_.t[:, k * 128:(k + 1) * 128], ident.t[:],
                         r=[pb_, ident], w=[bk], sig=(k == 1))
                K.copy("dve", hT.t[:, 8:10, :], bv[:, 0:256].rearrange("p (k t) -> p k t", k=2), r=[bk], w=[hT])

            def PB(ti):
                r0 = ti * 128
                i = ti % NX
                xt = xr[i]
                hT = hpT[ti % 3]
                for hh in range(2):
                    bgt, be = pring.next(), pring.next()
                    for k in range(8):
                        K.mm(bgt.t[:, :], hT.t[:, k, :], Wpg.t[:, k, hh * 512:(hh + 1) * 512], k == 0, False,
                             r=[hT, Wpg], w=[bgt], sig=False)
                    K.mm(bgt.t[:, :], one0.t[:], brow.t[:, hh * 512:(hh + 1) * 512], False, True,
                         r=[one0, brow], w=[bgt], sig=True)
                    for k in range(2):
                        K.mm(be.t[:, :], hT.t[:, 8 + k, :], Wpl.t[:, k, hh * 512:(hh + 1) * 512], k == 0, k == 1,
                             r=[hT, Wpl], w=[be], sig=(k == 1))
                    sgt = sg[(ti * 2 + hh) % 4]
                    K.act(sgt.t[:], bgt.t[:, :], AF.Sigmoid, r=[bgt], w=[sgt])
                    K.tt("dve", sgt.t[:], sgt.t[:], be.t[:, :], ALU.mult, r=[sgt, be], w=[sgt])
                    K.tt("pool", xt.t[:, hh * 512:(hh + 1) * 512], xt.t[:, hh * 512:(hh + 1) * 512], sgt.t[:], ALU.add,
                         r=[xt, sgt], w=[xt])
                out_toks.append(K.dma("sp", out_d[r0:r0 + 128, :], xt.t[:], xo_sem[i], r=[xt]))

            PA(0)
            PA(1)
            for ti in range(NTI):
                if ti + 2 < NTI:
                    PA(ti + 2)
                PB(ti)
            K.flush(final_toks=out_toks[-8:])
    return nc


def _host_inputs(inputs):
    f = lambda a: np.ascontiguousarray(np.asarray(a, dtype=np.float32))
    x = f(inputs["x"])
    p = f(inputs["p"])[0]
    pos = np.asarray(inputs["positions"]).astype(np.int32)

    def colz(v):
        v = f(v).reshape(-1)
        return v.reshape(-1, 128).T

    cols = np.zeros((128, NCOL), np.float32)
    cols[:, C_GMIX:C_GMIX + 8] = colz(inputs["g_mix"][0])
    cols[:, C_GFFN:C_GFFN + 8] = colz(inputs["g_ffn"][0])
    cols[:, C_GPLE:C_GPLE + 8] = colz(inputs["g_ple"][0])
    cols[:, C_GCQ:C_GCQ + 2] = colz(inputs["g_cq"][0])
    cols[:, C_GCKV:C_GCKV + 1] = colz(inputs["g_ckv"][0])
    cols[:, C_GOUT:C_GOUT + 4] = colz(inputs["g_attn_out"][0])
    cols[:, C_GOUT + 4:C_GOUT + 8] = colz(inputs["g_lru_out"][0])
    lcw = f(inputs["w_lru_conv"][0])
    cols[:, C_LCW:C_LCW + 16] = lcw.reshape(4, 4, 128).transpose(2, 1, 0).reshape(128, 16)
    cols[:, C_LCB:C_LCB + 4] = colz(inputs["b_lru_conv"][0])
    cols[:, C_BR:C_BR + 4] = colz(inputs["b_lru_r"][0])
    cols[:, C_BI:C_BI + 4] = colz(inputs["b_lru_i"][0])
    cols[:, C_LAM:C_LAM + 4] = colz(inputs["lru_lambda"][0])
    fcw = f(inputs["w_ffn_conv"][0])
    cols[:, C_FCW:C_FCW + 132] = fcw.reshape(3, 44, 128).transpose(2, 1, 0).reshape(128, 132)
    cols[:, C_FCB:C_FCB + 44] = colz(inputs["b_ffn_conv"][0])
    rows = np.zeros((1, NROW), np.float32)
    rows[0, R_GQ:R_GQ + 96] = f(inputs["g_qn"][0])
    rows[0, R_GK:R_GK + 96] = f(inputs["g_kn"][0])
    rows[0, R_IF:R_IF + 16] = (10000.0 ** (-np.arange(16, dtype=np.float32) / np.float32(16))).astype(np.float32)
    shared = {
        "w_in": f(inputs["w_in"][0]), "w_uq": f(inputs["w_uq"][0]), "w_ukv": f(inputs["w_ukv"][0]),
        "w_out": f(inputs["w_out"][0]), "w_up": f(inputs["w_up"][0]), "w_down": f(inputs["w_down"][0]),
        "w_pg": f(inputs["w_ple_gate"][0]), "w_ple": f(inputs["w_ple"][0]),
        "w_r": f(inputs["w_lru_r"][0]).reshape(512, 64), "w_i": f(inputs["w_lru_i"][0]).reshape(512, 64),
        "cols": cols, "rows": rows, "bple": f(inputs["b_ple_gate"][0]).reshape(1, D),
        "ident": np.eye(128, dtype=np.float32).astype(ml_dtypes.bfloat16),
    }
    maps = []
    for c in range(NCORES):
        m = dict(shared)
        m["x"] = np.ascontiguousarray(x[2 * c:2 * c + 2].reshape(TOK, D))
        m["p"] = np.ascontiguousarray(p[2 * c:2 * c + 2].reshape(TOK, 256))
        pc = pos[2 * c:2 * c + 2]
        m["posT"] = np.ascontiguousarray(pc.reshape(2, NT, 128).transpose(2, 0, 1).reshape(128, 2 * NT))
        maps.append(m)
    return maps


def kernel(**inputs):
    maps = _host_inputs(inputs)
    nc = build_nc()
    res = run_bass_kernel_spmd(nc, maps, core_ids=list(range(NCORES)))
    outs = [np.asarray(r["out"], dtype=np.float32).reshape(2, S, D) for r in res.results]
    return np.concatenate(outs, axis=0)
```

```python
import math
import numpy as np
import ml_dtypes
from contextlib import ExitStack
import concourse.bass as bass
import concourse.mybir as mybir
from concourse.bass_utils import run_bass_kernel_spmd

F32 = mybir.dt.float32
BF16 = mybir.dt.bfloat16
I32 = mybir.dt.int32
AF = mybir.ActivationFunctionType
ALU = mybir.AluOpType
AX = mybir.AxisListType

NCORES = 8
D = 1024
S = 2048
NT = S // 128
TOK = 2 * S
H = 8
EPS = 1e-6
DFF = 2816
NCH = DFF // 128
WIN = 256

C_GMIX, C_GFFN, C_GPLE, C_GCQ, C_GCKV, C_GOUT = 0, 8, 16, 24, 26, 27
C_LCW, C_LCB, C_BR, C_BI, C_LAM = 35, 51, 55, 59, 63
C_FCW, C_FCB = 67, 67 + 132
NCOL = C_FCB + 44
R_GQ, R_GK, R_IF = 0, 96, 192
NROW = 208


class Tok:
    __slots__ = ("q", "sem", "val")

    def __init__(self, q, sem, val):
        self.q, self.sem, self.val = q, sem, val


class Buf:
    def __init__(self, name=""):
        self.name = name
        self.w = None
        self.r = {}


class TB:
    def __init__(self, t, name=""):
        self.t = t
        self.b = Buf(name)


class Q:
    def __init__(self, name, sem):
        self.name, self.sem = name, sem
        self.n = 0
        self.ops = []
        self.waited = {}
        self.pend_r = []
        self.pend_w = []


class DSem:
    def __init__(self, h):
        self.h = h
        self.val = 0


def _b(x):
    return x.b if isinstance(x, TB) else x


class Kern:
    def __init__(self, nc, es):
        self.nc = nc
        self.es = es
        self.q = {}
        for n in ("pe", "dve", "act", "pool", "sp"):
            self.q[n] = Q(n, es.enter_context(nc.semaphore("q_" + n)))
        self.nsem = 5

    def dsem(self, name):
        self.nsem += 1
        return DSem(self.es.enter_context(self.nc.semaphore(name)))

    def sb(self, es, name, shape, dt):
        return TB(es.enter_context(self.nc.sbuf_tensor("s_" + name, list(shape), dt)), name)

    def _waits(self, q, r, w):
        needs = {}

        def need(t):
            if t is None:
                return
            k = id(t.sem)
            if k not in needs or needs[k][1] < t.val:
                needs[k] = (t.sem, t.val)

        for b in r:
            need(_b(b).w)
        for b in w:
            b = _b(b)
            if b.w is not None and b.w.q is not q:
                need(b.w)
            for t in b.r.values():
                if t.q is not q:
                    need(t)
        waits = []
        for k, (sem, val) in needs.items():
            if q.waited.get(k, 0) < val:
                q.waited[k] = val
                waits.append((sem, val))
        return waits

    @staticmethod
    def _stamp(tok, r, w):
        for b in r:
            b = _b(b)
            k = id(tok.sem)
            o = b.r.get(k)
            if o is None or o.val < tok.val:
                b.r[k] = tok
        for b in w:
            b = _b(b)
            b.w = tok
            b.r = {}

    def op(self, qn, f, r=(), w=(), sig=True):
        q = self.q[qn]
        waits = self._waits(q, r, w)
        if sig:
            q.n += 1
            tok = Tok(q, q.sem, q.n)
            q.ops.append((waits, f, (q.sem, 1)))
            self._stamp(tok, list(r) + q.pend_r, list(w) + q.pend_w)
            q.pend_r, q.pend_w = [], []
            return tok
        q.ops.append((waits, f, None))
        q.pend_r += list(r)
        q.pend_w += list(w)
        return None

    def dma(self, qn, out, in_, sem, r=(), w=()):
        q = self.q[qn]
        assert not q.pend_r and not q.pend_w
        waits = self._waits(q, r, w)
        sem.val += 16
        tok = Tok(None, sem.h, sem.val)
        q.ops.append((waits, lambda e: e.dma_start(out=out, in_=in_), (sem.h, 16)))
        self._stamp(tok, r, w)
        return tok

    def barrier(self, extra=()):
        for q in self.q.values():
            waits = []
            for oq in self.q.values():
                if oq is q or oq.n == 0:
                    continue
                k = id(oq.sem)
                if q.waited.get(k, 0) < oq.n:
                    q.waited[k] = oq.n
                    waits.append((oq.sem, oq.n))
            for t in extra:
                k = id(t.sem)
                if q.waited.get(k, 0) < t.val:
                    q.waited[k] = t.val
                    waits.append((t.sem, t.val))
            if waits:
                q.ops.append((waits, None, None))

    def flush(self, final_toks=()):
        nc = self.nc
        if final_toks:
            q = self.q["sp"]
            needs = {}
            for t in final_toks:
                k = id(t.sem)
                if k not in needs or needs[k][1] < t.val:
                    needs[k] = (t.sem, t.val)
            q.ops.append((list(needs.values()), None, None))
        with nc.Block() as blk:
            for qn, deco in (("sp", blk.sync), ("act", blk.scalar), ("pool", blk.gpsimd),
                             ("dve", blk.vector), ("pe", blk.tensor)):
                qq = self.q[qn]
                assert not qq.pend_r and not qq.pend_w, qn
                ops = qq.ops
                qq.ops = []

                def body(e, ops=ops):
                    for waits, f, inc in ops:
                        for sem, val in waits:
                            e.wait_ge(sem, val)
                        if f is None:
                            continue
                        ins = f(e)
                        if inc is not None:
                            ins.then_inc(inc[0], inc[1])

                deco(body)

    def mm(self, out, lhsT, rhs, start, stop, r=(), w=(), sig=False):
        return self.op("pe", lambda e: e.matmul(out, lhsT, rhs, start=start, stop=stop), r, w, sig)

    def tr(self, out, in_, ident, r=(), w=(), sig=False):
        return self.op("pe", lambda e: e.transpose(out, in_, ident), r, w, sig)

    def act(self, out, in_, func, r=(), w=(), bias=None, scale=None, accum=None, qn="act"):
        kw = {}
        if bias is not None:
            kw["bias"] = bias
        if scale is not None:
            kw["scale"] = scale
        if accum is not None:
            kw["accum_out"] = accum
        return self.op(qn, lambda e: e.activation(out=out, in_=in_, func=func, **kw), r, w)

    def tt(self, qn, out, in0, in1, op, r=(), w=()):
        return self.op(qn, lambda e: e.tensor_tensor(out=out, in0=in0, in1=in1, op=op), r, w)

    def ts(self, qn, out, in0, s1, s2, op0, op1, r=(), w=()):
        return self.op(qn, lambda e: e.tensor_scalar(out=out, in0=in0, scalar1=s1, scalar2=s2,
                                                     op0=op0, op1=op1), r, w)

    def stt(self, out, in0, scalar, in1, op0, op1, r=(), w=()):
        return self.op("dve", lambda e: e.scalar_tensor_tensor(out=out, in0=in0, scalar=scalar, in1=in1,
                                                               op0=op0, op1=op1), r, w)

    def copy(self, qn, out, in_, r=(), w=()):
        if qn == "act":
            return self.op(qn, lambda e: e.copy(out=out, in_=in_), r, w)
        return self.op(qn, lambda e: e.tensor_copy(out=out, in_=in_), r, w)

    def memset(self, qn, ap, val, w=()):
        return self.op(qn, lambda e: e.memset(ap, val), (), w)


def build_nc():
    nc = bass.Bass("TRN2", target_bir_lowering=False)

    def din(name, shape, dt=F32):
        return nc.dram_tensor(name, list(shape), dt, kind="ExternalInput").ap()

    x_d = din("x", [TOK, D])
    p_d = din("p", [TOK, 256])
    pos_d = din("posT", [128, 2 * NT], I32)
    w_in_d = din("w_in", [D, 1440])
    w_uq_d = din("w_uq", [256, 768])
    w_ukv_d = din("w_ukv", [128, 1024])
    w_out_d = din("w_out", [D, D])
    w_up_d = din("w_up", [D, 2 * DFF])
    w_down_d = din("w_down", [DFF, D])
    w_pg_d = din("w_pg", [D, D])
    w_ple_d = din("w_ple", [256, D])
    w_r_d = din("w_r", [512, 64])
    w_i_d = din("w_i", [512, 64])
    cols_d = din("cols", [128, NCOL])
    rows_d = din("rows", [1, NROW])
    bple_d = din("bple", [1, D])
    ident_d = din("ident", [128, 128], BF16)
    out_d = nc.dram_tensor("out", [TOK, D], F32, kind="ExternalOutput").ap()
    x1_d = nc.dram_tensor("x1s", [TOK, D], F32, kind="Internal").ap()
    x2_d = nc.dram_tensor("x2s", [TOK, D], F32, kind="Internal").ap()

    with ExitStack() as g:
        K = Kern(nc, g)
        banks = [TB(g.enter_context(nc.psum_tensor("ps%d" % i, [128, 512], F32)), "ps%d" % i) for i in range(8)]

        class Ring:
            def __init__(self, items):
                self.items = items
                self.i = 0

            def next(self):
                it = self.items[self.i % len(self.items)]
                self.i += 1
                return it

        cols = K.sb(g, "cols", [128, NCOL], F32)
        rows = K.sb(g, "rows", [128, NROW], F32)
        ident = K.sb(g, "ident", [128, 128], BF16)
        nhalf = K.sb(g, "nhalf", [128, 16], F32)
        cst_sem = K.dsem("cst")
        K.dma("sp", cols.t[:], cols_d[:, :], cst_sem, w=[cols])
        K.dma("sp", rows.t[:], rows_d.partition_broadcast(128)[:, 0, :], cst_sem, w=[rows])
        K.dma("sp", ident.t[:], ident_d[:, :], cst_sem, w=[ident])
        ftok = Tok(None, cst_sem.h, cst_sem.val)
        for tb in (cols, rows, ident):
            tb.b.w = ftok
        K.memset("pool", nhalf.t[:], -0.5, w=[nhalf])

        def col(c0, n=1):
            return cols.t[:, c0:c0 + n]

        def rstd(out_tb, out_ap, ss_tb, ss_ap, nfeat, n):
            K.ts("pool", ss_ap, ss_ap, 1.0 / nfeat, EPS, ALU.mult, ALU.add, r=[ss_tb], w=[ss_tb])
            K.tt("pool", out_ap, ss_ap, nhalf.t[:, 0:n], ALU.pow, r=[ss_tb, nhalf], w=[out_tb])

        with ExitStack() as es:
            Win = K.sb(es, "Win", [128, 8, 1440], BF16)
            Wuq = K.sb(es, "Wuq", [128, 2, 768], BF16)
            Wukv = K.sb(es, "Wukv", [128, 1024], BF16)
            Wout = K.sb(es, "Wout", [128, 8, 1024], BF16)
            Wr = K.sb(es, "Wr", [128, 4, 128], BF16)
            Wi = K.sb(es, "Wi", [128, 4, 128], BF16)
            stage = [K.sb(es, "stg%d" % i, [128, 512], F32) for i in range(2)]
            stage_sem = [K.dsem("stg%d" % i) for i in range(2)]
            stg_i = [0]
            lxc = K.sb(es, "lxc", [128, 512], F32)
            lxb = K.sb(es, "lxb", [128, 512], BF16)
            lr_ = K.sb(es, "lr", [128, 512], F32)
            li_ = K.sb(es, "li", [128, 512], F32)
            lm_ = K.sb(es, "lm", [128, 512], F32)

            def load_cast(dst_tb, dst_ap, src_ap, n, scale_col=None):
                i = stg_i[0] % 2
                stg_i[0] += 1
                st_ = stage[i]
                K.dma("sp", st_.t[:, 0:n], src_ap, stage_sem[i], w=[st_])
                if scale_col is None:
                    K.copy("act", dst_ap, st_.t[:, 0:n], r=[st_], w=[dst_tb])
                else:
                    K.act(dst_ap, st_.t[:, 0:n], AF.Copy, r=[st_, cols], w=[dst_tb], scale=scale_col)

            w_in_v = w_in_d.rearrange("(k p) n -> p k n", p=128)
            for k in range(8):
                for h0 in (0, 480, 960):
                    load_cast(Win, Win.t[:, k, h0:h0 + 480], w_in_v[:, k, h0:h0 + 480], 480, col(C_GMIX + k))
            w_uq_v = w_uq_d.rearrange("(k p) n -> p k n", p=128)
            for k in range(2):
                for h0 in (0, 384):
                    load_cast(Wuq, Wuq.t[:, k, h0:h0 + 384], w_uq_v[:, k, h0:h0 + 384], 384, col(C_GCQ + k))
            for h0 in (0, 512):
                load_cast(Wukv, Wukv.t[:, h0:h0 + 512], w_ukv_d[:, h0:h0 + 512], 512, col(C_GCKV))
            w_out_v = w_out_d.rearrange("(k p) n -> p k n", p=128)
            for k in range(8):
                for h0 in (0, 512):
                    load_cast(Wout, Wout.t[:, k, h0:h0 + 512], w_out_v[:, k, h0:h0 + 512], 512, col(C_GOUT + k))
            bd_sem = K.dsem("bd")
            for gi, (wd, stg_, dst) in enumerate(((w_r_d, lr_, Wr), (w_i_d, li_, Wi))):
                K.memset("pool", stg_.t[:], 0.0, w=[stg_])
                sv = stg_.t[:].rearrange("p (c n) -> p c n", c=4)
                for blk in range(8):
                    c, hf_ = blk // 2, blk % 2
                    K.dma("sp", sv[hf_ * 64:(hf_ + 1) * 64, c, hf_ * 64:(hf_ + 1) * 64],
                          wd[blk * 64:(blk + 1) * 64, :], bd_sem, w=[stg_])
            for stg_ in (lr_, li_):
                stg_.b.w = Tok(None, bd_sem.h, bd_sem.val)
            K.copy("pool", Wr.t[:], lr_.t[:].rearrange("p (c n) -> p c n", c=4), r=[lr_], w=[Wr])
            K.copy("pool", Wi.t[:], li_.t[:].rearrange("p (c n) -> p c n", c=4), r=[li_], w=[Wi])

            gq = K.sb(es, "gq", [128, 96], F32)
            K.ts("pool", gq.t[:], rows.t[:, R_GQ:R_GQ + 96], 96.0 ** -0.5, 0.0, ALU.mult, ALU.add, r=[rows], w=[gq])
            clam = K.sb(es, "clam", [128, 4], F32)
            K.act(clam.t[:], col(C_LAM, 4), AF.Exp, r=[cols], w=[clam], scale=-1.0)
            K.act(clam.t[:], clam.t[:], AF.Ln, r=[clam], w=[clam], bias=1.0)
            K.ts("pool", clam.t[:], clam.t[:], -8.0, 0.0, ALU.mult, ALU.add, r=[clam], w=[clam])
            ones2 = K.sb(es, "ones2", [128, 2], F32)
            K.memset("pool", ones2.t[:], 1.0, w=[ones2])

            KT = K.sb(es, "KT", [128, 8, S], BF16)
            VA = K.sb(es, "VA", [128, NT, 8, 65], BF16)
            KTb = [Buf("KT%d" % i) for i in range(NT)]
            VAb = [Buf("VA%d" % i) for i in range(NT)]
            K.memset("pool", VA.t[:], 1.0, w=VAb)
            posi = K.sb(es, "posi", [128, 2 * NT], I32)
            pos_sem = K.dsem("pos")
            K.dma("sp", posi.t[:], pos_d[:, :], pos_sem, w=[posi])
            posf = K.sb(es, "posf", [128, 2 * NT], F32)
            K.copy("dve", posf.t[:], posi.t[:], r=[posi], w=[posf])
            NA = 2 * NT
            angi = K.sb(es, "angi", [128, NA, 16], I32)
            rsin = K.sb(es, "rsin", [128, NA, 16], F32)
            rcos = K.sb(es, "rcos", [128, NA, 16], F32)
            x1o = K.sb(es, "x1o", [128, D], F32)
            ang_t, angn_t, angm_t = lxc, lm_, x1o
            angv = lxc.t[:].rearrange("p (a f) -> p a f", f=16)
            angnv = lm_.t[:].rearrange("p (a f) -> p a f", f=16)
            angmv = x1o.t[:, 0:512].rearrange("p (a f) -> p a f", f=16)
            TWO_PI = 2.0 * math.pi
            C1 = 6.28125
            C2 = TWO_PI - C1
            PI_IN = 3.1415925

            def wrap():
                K.ts("dve", angmv, angv, math.pi, -TWO_PI, ALU.is_gt, ALU.mult, r=[ang_t], w=[angm_t])
                K.tt("dve", angv, angv, angmv, ALU.add, r=[ang_t, angm_t], w=[ang_t])
                K.ts("dve", angmv, angv, -math.pi, TWO_PI, ALU.is_lt, ALU.mult, r=[ang_t], w=[angm_t])
                K.tt("dve", angv, angv, angmv, ALU.add, r=[ang_t, angm_t], w=[ang_t])
                K.ts("dve", angv, angv, PI_IN, -PI_IN, ALU.min, ALU.max, r=[ang_t], w=[ang_t])

            pf = posf.t[:, :].unsqueeze(2).to_broadcast([128, NA, 16])
            fr = rows.t[:, R_IF:R_IF + 16].unsqueeze(1).to_broadcast([128, NA, 16])
            K.tt("dve", angv, pf, fr, ALU.mult, r=[posf, rows], w=[ang_t])
            K.ts("dve", angnv, angv, 1.0 / TWO_PI, None, ALU.mult, ALU.bypass, r=[ang_t], w=[angn_t])
            K.copy("dve", angi.t[:], angnv, r=[angn_t], w=[angi])
            K.copy("dve", angnv, angi.t[:], r=[angi], w=[angn_t])
            K.stt(angv, angnv, -C1, angv, ALU.mult, ALU.add, r=[angn_t, ang_t], w=[ang_t])
            K.stt(angv, angnv, -C2, angv, ALU.mult, ALU.add, r=[angn_t, ang_t], w=[ang_t])
            wrap()
            K.act(rsin.t[:], angv, AF.Sin, r=[ang_t], w=[rsin])
            K.ts("dve", angv, angv, math.pi / 2, None, ALU.add, ALU.bypass, r=[ang_t], w=[ang_t])
            wrap()
            K.act(rcos.t[:], angv, AF.Sin, r=[ang_t], w=[rcos])

            xr = [K.sb(es, "x%d" % i, [128, D], F32) for i in range(3)]
            xr_sem = [K.dsem("xs%d" % i) for i in range(3)]
            xr_i = [0]

            def load_x(src_ap):
                i = xr_i[0] % 3
                xr_i[0] += 1
                tb = xr[i]
                K.dma("sp", tb.t[:], src_ap, xr_sem[i], w=[tb])
                return tb

            junk = K.sb(es, "junk", [128, D], BF16)
            hn = Ring([K.sb(es, "hn%d" % i, [128, D], BF16) for i in range(2)])
            hnT = [K.sb(es, "hnT%d" % i, [128, 8, 512], BF16) for i in range(2)]
            ss4 = K.sb(es, "ss4", [128, 8], F32)
            mla = K.sb(es, "mla", [128, 416], F32)
            ssm = K.sb(es, "ssm", [128, 4], F32)
            cqn = K.sb(es, "cqn", [128, 384], BF16)
            cqT = K.sb(es, "cqT", [128, 3, 128], BF16)
            qf = K.sb(es, "qf", [128, 8, 96], F32)
            sq = K.sb(es, "sq", [128, 8, 96], F32)
            ssh = K.sb(es, "ssh", [128, 16], F32)
            Qb = K.sb(es, "Qb", [128, 8, 96], BF16)
            Kb = K.sb(es, "Kb", [128, 8, 96], BF16)
            QT = [K.sb(es, "QT%d" % i, [128, 8, 128], BF16) for i in range(2)]
            rk = K.sb(es, "rk", [128, 32], F32)
            rk2 = K.sb(es, "rk2", [128, 32], F32)
            rt = [K.sb(es, "rt%d" % i, [128, 8, 16], F32) for i in range(2)]
            PT = Ring([K.sb(es, "PT%d" % i, [128, 4, 128], BF16) for i in range(4)])
            rec = K.sb(es, "rec", [128, 8], F32)
            attn = K.sb(es, "attn", [128, 8, 64], F32)
            ssa = K.sb(es, "ssa", [128, 2], F32)
            attb = K.sb(es, "attb", [128, 512], BF16)
            attT = [K.sb(es, "attT%d" % i, [128, 4, 128], BF16) for i in range(8)]
            xlb = K.sb(es, "xlb", [128, 4, 515], F32)
            hst = K.sb(es, "hst", [128, 4], F32)
            ybf = [K.sb(es, "ybf%d" % i, [128, 4, 512], BF16) for i in range(2)]
            ssl = [K.sb(es, "ssl%d" % i, [128, 4, 2], F32) for i in range(2)]
            x1_sem = K.dsem("x1o")

            gen = Ring(banks[4:8])
            sbk = Ring(banks[0:2])
            obk = banks[2:4]
            NG = 2 * NT
            NSG = NG // 4

            def N_unit(G, jj):
                r0 = G * 512 + jj * 128
                col_ = (G % 2) * 4 + jj
                xt = load_x(x_d[r0:r0 + 128, :])
                h_ = hn.next()
                K.act(junk.t[:], xt.t[:], AF.Square, r=[xt], w=[junk, ss4], accum=ss4.t[:, col_:col_ + 1])
                rstd(ss4, ss4.t[:, col_:col_ + 1], ss4, ss4.t[:, col_:col_ + 1], D, 1)
                K.act(h_.t[:], xt.t[:], AF.Copy, r=[xt, ss4], w=[h_], scale=ss4.t[:, col_:col_ + 1])
                bk = gen.next()
                bv = bk.t[:].bitcast(BF16)
                for k in range(8):
                    K.tr(bv[:, k * 128:(k + 1) * 128], h_.t[:, k * 128:(k + 1) * 128], ident.t[:],
                         r=[h_, ident], w=[bk], sig=(k == 7))
                hT = hnT[G % 2]
                K.copy("dve", hT.t[:, :, jj * 128:(jj + 1) * 128],
                       bv.rearrange("p (k t) -> p k t", k=8), r=[bk], w=[hT])

            pstate = {}

            def P_stage(g, i):
                G, j = g // 4, g % 4
                t = g % NT
                hT = hnT[G % 2]
                if i == 0:
                    bk = gen.next()
                    for k in range(8):
                        K.mm(bk.t[:, 0:416], hT.t[:, k, j * 128:(j + 1) * 128], Win.t[:, k, 0:416],
                             k == 0, k == 7, r=[hT, Win], w=[bk], sig=(k == 7))
                    K.copy("act", mla.t[:], bk.t[:, 0:416], r=[bk], w=[mla])
                    K.act(junk.t[:, 0:256], mla.t[:, 0:256], AF.Square, r=[mla], w=[junk, ssm], accum=ssm.t[:, 0:1])
                    K.act(junk.t[:, 0:128], mla.t[:, 256:384], AF.Square, r=[mla], w=[junk, ssm], accum=ssm.t[:, 1:2])
                    K.act(junk.t[:, 0:32], mla.t[:, 384:416], AF.Square, r=[mla], w=[junk, ssm], accum=ssm.t[:, 2:3])
                    rstd(ssm, ssm.t[:, 0:1], ssm, ssm.t[:, 0:1], 256, 1)
                    rstd(ssm, ssm.t[:, 1:2], ssm, ssm.t[:, 1:2], 128, 1)
                    K.act(cqn.t[:, 0:256], mla.t[:, 0:256], AF.Copy, r=[mla, ssm], w=[cqn], scale=ssm.t[:, 0:1])
                    K.act(cqn.t[:, 256:384], mla.t[:, 256:384], AF.Copy, r=[mla, ssm], w=[cqn], scale=ssm.t[:, 1:2])
                elif i == 1:
                    bk = gen.next()
                    bv = bk.t[:].bitcast(BF16)
                    for k in range(3):
                        K.tr(bv[:, k * 128:(k + 1) * 128], cqn.t[:, k * 128:(k + 1) * 128], ident.t[:],
                             r=[cqn, ident], w=[bk], sig=(k == 2))
                    K.copy("dve", cqT.t[:], bv[:, 0:384].rearrange("p (k t) -> p k t", k=3), r=[bk], w=[cqT])
                elif i == 2:
                    bq0, bq1 = gen.next(), gen.next()
                    for k in range(2):
                        K.mm(bq0.t[:, 0:384], cqT.t[:, k, :], Wuq.t[:, k, 0:384], k == 0, k == 1,
                             r=[cqT, Wuq], w=[bq0], sig=(k == 1))
                    for k in range(2):
                        K.mm(bq1.t[:, 0:384], cqT.t[:, k, :], Wuq.t[:, k, 384:768], k == 0, k == 1,
                             r=[cqT, Wuq], w=[bq1], sig=(k == 1))
                    qf2 = qf.t[:].rearrange("p h d -> p (h d)")
                    K.copy("act", qf2[:, 0:384], bq0.t[:, 0:384], r=[bq0], w=[qf])
                    K.copy("act", qf2[:, 384:768], bq1.t[:, 0:384], r=[bq1], w=[qf])
                    K.tt("pool", sq.t[:], qf.t[:], qf.t[:], ALU.mult, r=[qf], w=[sq])
                    K.op("dve", lambda e: e.tensor_reduce(out=ssh.t[:, 0:8], in_=sq.t[:], axis=AX.X, op=ALU.add),
                         r=[sq], w=[ssh])
                    bk0, bk1 = gen.next(), gen.next()
                    K.mm(bk0.t[:, :], cqT.t[:, 2, :], Wukv.t[:, 0:512], True, True, r=[cqT, Wukv], w=[bk0], sig=True)
                    K.mm(bk1.t[:, :], cqT.t[:, 2, :], Wukv.t[:, 512:1024], True, True, r=[cqT, Wukv], w=[bk1], sig=True)
                    pstate["kv"] = (bk0, bk1)
                    for hb, bkk in ((0, bk0), (1, bk1)):
                        kvv = bkk.t[:].rearrange("p (h d) -> p h d", h=4)
                        K.act(sq.t[:, hb * 4:(hb + 1) * 4, 0:64], kvv[:, :, 0:64], AF.Square, r=[bkk], w=[sq])
                        K.copy("act", VA.t[:, t, hb * 4:(hb + 1) * 4, 0:64], kvv[:, :, 64:128], r=[bkk], w=[VAb[t]])
                    K.op("dve", lambda e: e.tensor_reduce(out=ssh.t[:, 8:16], in_=sq.t[:, :, 0:64], axis=AX.X, op=ALU.add),
                         r=[sq], w=[ssh])
                    K.ts("dve", ssh.t[:, 8:16], ssh.t[:, 8:16], ssm.t[:, 2:3], None, ALU.add, ALU.bypass,
                         r=[ssh, ssm], w=[ssh])
                    rstd(ssh, ssh.t[:], ssh, ssh.t[:], 96, 16)
                elif i == 3:
                    bk0, bk1 = pstate["kv"]
                    K.tt("dve", qf.t[:], qf.t[:], ssh.t[:, 0:8].unsqueeze(2).to_broadcast([128, 8, 96]), ALU.mult,
                         r=[qf, ssh], w=[qf])
                    K.tt("dve", qf.t[:], qf.t[:], gq.t[:, :].unsqueeze(1).to_broadcast([128, 8, 96]), ALU.mult,
                         r=[qf, gq], w=[qf])
                    cosb = rcos.t[:, g, :].unsqueeze(1).to_broadcast([128, 8, 16])
                    sinb = rsin.t[:, g, :].unsqueeze(1).to_broadcast([128, 8, 16])
                    K.copy("act", Qb.t[:, :, 0:64], qf.t[:, :, 0:64], r=[qf], w=[Qb])
                    K.tt("pool", rt[0].t[:], qf.t[:, :, 64:80], cosb, ALU.mult, r=[qf, rcos], w=[rt[0]])
                    K.tt("pool", rt[1].t[:], qf.t[:, :, 80:96], sinb, ALU.mult, r=[qf, rsin], w=[rt[1]])
                    K.tt("pool", Qb.t[:, :, 64:80], rt[0].t[:], rt[1].t[:], ALU.subtract, r=[rt[0], rt[1]], w=[Qb])
                    K.tt("pool", rt[0].t[:], qf.t[:, :, 80:96], cosb, ALU.mult, r=[qf, rcos], w=[rt[0]])
                    K.tt("pool", rt[1].t[:], qf.t[:, :, 64:80], sinb, ALU.mult, r=[qf, rsin], w=[rt[1]])
                    K.tt("pool", Qb.t[:, :, 80:96], rt[0].t[:], rt[1].t[:], ALU.add, r=[rt[0], rt[1]], w=[Qb])
                    for hb, bkk in ((0, bk0), (1, bk1)):
                        kvv = bkk.t[:].rearrange("p (h d) -> p h d", h=4)
                        K.tt("dve", sq.t[:, hb * 4:(hb + 1) * 4, 0:64], kvv[:, :, 0:64],
                             ssh.t[:, 8 + hb * 4:12 + hb * 4].unsqueeze(2).to_broadcast([128, 4, 64]), ALU.mult,
                             r=[bkk, ssh], w=[sq])
                    K.tt("dve", Kb.t[:, :, 0:64], sq.t[:, :, 0:64],
                         rows.t[:, R_GK:R_GK + 64].unsqueeze(1).to_broadcast([128, 8, 64]), ALU.mult,
                         r=[sq, rows], w=[Kb])
                    K.tt("pool", rk.t[:], mla.t[:, 384:416], rows.t[:, R_GK + 64:R_GK + 96], ALU.mult, r=[mla, rows], w=[rk])
                    c16, s16 = rcos.t[:, g, :], rsin.t[:, g, :]
                    K.tt("pool", rt[0].t[:, 0, :], rk.t[:, 0:16], c16, ALU.mult, r=[rk, rcos], w=[rt[0]])
                    K.tt("pool", rt[1].t[:, 0, :], rk.t[:, 16:32], s16, ALU.mult, r=[rk, rsin], w=[rt[1]])
                    K.tt("pool", rk2.t[:, 0:16], rt[0].t[:, 0, :], rt[1].t[:, 0, :], ALU.subtract, r=[rt[0], rt[1]], w=[rk2])
                    K.tt("pool", rt[0].t[:, 0, :], rk.t[:, 16:32], c16, ALU.mult, r=[rk, rcos], w=[rt[0]])
                    K.tt("pool", rt[1].t[:, 0, :], rk.t[:, 0:16], s16, ALU.mult, r=[rk, rsin], w=[rt[1]])
                    K.tt("pool", rk2.t[:, 16:32], rt[0].t[:, 0, :], rt[1].t[:, 0, :], ALU.add, r=[rt[0], rt[1]], w=[rk2])
                    K.tt("dve", Kb.t[:, :, 64:96], rk2.t[:, :].unsqueeze(1).to_broadcast([128, 8, 32]),
                         ssh.t[:, 8:16].unsqueeze(2).to_broadcast([128, 8, 32]), ALU.mult, r=[rk2, ssh], w=[Kb])
                elif i == 4:
                    qt = QT[g % 2]
                    for src, dst_b, dst_ap in ((Qb, qt, qt.t[0:96, :, :]),
                                               (Kb, KTb[t], KT.t[0:96, :, t * 128:(t + 1) * 128])):
                        bk = gen.next()
                        bv = bk.t[:].bitcast(BF16)
                        for h in range(8):
                            K.tr(bv[0:96, h * 128:(h + 1) * 128], src.t[:, h, :], ident.t[:],
                                 r=[src, ident], w=[bk], sig=(h == 7))
                        K.copy("dve", dst_ap, bv[0:96, :].rearrange("p (h t) -> p h t", h=8), r=[bk], w=[dst_b])

            def A_tile(g, fill):
                t = g % NT
                qt = QT[g % 2]
                groups = [(h, g0, min(4, t + 1 - g0)) for h in range(8) for g0 in range(0, t + 1, 4)]
                sbs = {}

                def qk(i):
                    h, g0, n = groups[i]
                    sb_ = sbk.next()
                    sbs[i] = sb_
                    for ii in range(n):
                        kt = g0 + ii
                        K.mm(sb_.t[:, ii * 128:(ii + 1) * 128], KT.t[0:96, h, kt * 128:(kt + 1) * 128],
                             qt.t[0:96, h, :], True, True, r=[KTb[kt], qt], w=[sb_], sig=(ii == n - 1))

                qk(0)
                for i, (h, g0, n) in enumerate(groups):
                    if i + 1 < len(groups):
                        qk(i + 1)
                    sb_ = sbs.pop(i)
                    ob = obk[h // 4]
                    oap = ob.t[:, (h % 4) * 65:(h % 4) * 65 + 65]
                    pt = PT.next()
                    K.act(pt.t[:, 0:n, :].rearrange("p a b -> p (a b)"), sb_.t[:, 0:n * 128], AF.Exp,
                          r=[sb_], w=[pt])
                    if g0 + n - 1 == t:
                        K.memset("pool", pt.t[64:128, n - 1, 0:64], 0.0, w=[pt])
                    for ii in range(n):
                        kt = g0 + ii
                        K.mm(oap, pt.t[:, ii, :], VA.t[:, kt, h, :], kt == 0, kt == t,
                             r=[pt, VAb[kt]], w=[ob], sig=(ii == n - 1))
                    if g0 + n - 1 == t:
                        fill(h)

            def F_tile(g):
                for hb in range(2):
                    ov = obk[hb].t[:, 0:260].rearrange("p (h d) -> p h d", h=4)
                    K.op("dve", lambda e, ov=ov, hb=hb: e.reciprocal(out=rec.t[:, hb * 4:(hb + 1) * 4], in_=ov[:, :, 64]),
                         r=[obk[hb]], w=[rec])
                    K.tt("dve", attn.t[:, hb * 4:(hb + 1) * 4, :], ov[:, :, 0:64],
                         rec.t[:, hb * 4:(hb + 1) * 4].unsqueeze(2).to_broadcast([128, 4, 64]), ALU.mult,
                         r=[obk[hb], rec], w=[attn])
                a2 = attn.t[:].rearrange("p h d -> p (h d)")
                K.act(junk.t[:, 0:512], a2, AF.Square, r=[attn], w=[junk, ssa], accum=ssa.t[:, 0:1])
                rstd(ssa, ssa.t[:, 0:1], ssa, ssa.t[:, 0:1], 512, 1)
                K.act(attb.t[:], a2, AF.Copy, r=[attn, ssa], w=[attb], scale=ssa.t[:, 0:1])
                at = attT[g % 8]
                bk = gen.next()
                bv = bk.t[:].bitcast(BF16)
                for k in range(4):
                    K.tr(bv[:, k * 128:(k + 1) * 128], attb.t[:, k * 128:(k + 1) * 128], ident.t[:],
                         r=[attb, ident], w=[bk], sig=(k == 3))
                K.copy("dve", at.t[:], bv[:, 0:512].rearrange("p (k t) -> p k t", k=4), r=[bk], w=[at])

            def L_chunk(G, c):
                hT = hnT[G % 2]
                yb = ybf[G % 2]
                sl = ssl[G % 2]
                if G % (NT // 4) == 0 and c == 0:
                    K.memset("pool", xlb.t[:, :, 0:3], 0.0, w=[xlb])
                    K.memset("pool", hst.t[:], 0.0, w=[hst])
                bx, bg = gen.next(), gen.next()
                for k in range(8):
                    K.mm(bx.t[:, :], Win.t[:, k, 416 + c * 128:416 + (c + 1) * 128], hT.t[:, k, :],
                         k == 0, k == 7, r=[Win, hT], w=[bx], sig=(k == 7))
                K.copy("act", xlb.t[:, c, 3:515], bx.t[:, :], r=[bx], w=[xlb])
                for k in range(8):
                    K.mm(bg.t[:, :], Win.t[:, k, 928 + c * 128:928 + (c + 1) * 128], hT.t[:, k, :],
                         k == 0, k == 7, r=[Win, hT], w=[bg], sig=(k == 7))
                K.act(li_.t[:], bg.t[:, :], AF.Gelu_apprx_tanh, r=[bg], w=[li_])
                K.ts("dve", lxc.t[:], xlb.t[:, c, 3:515], col(C_LCW + c * 4 + 3), col(C_LCB + c),
                     ALU.mult, ALU.add, r=[xlb, cols], w=[lxc])
                for jj in range(3):
                    K.stt(lxc.t[:], xlb.t[:, c, jj:jj + 512], col(C_LCW + c * 4 + jj), lxc.t[:],
                          ALU.mult, ALU.add, r=[xlb, cols, lxc], w=[lxc])
                K.copy("pool", xlb.t[:, c, 0:3], xlb.t[:, c, 512:515], r=[xlb], w=[xlb])
                K.copy("pool", lxb.t[:], lxc.t[:], r=[lxc], w=[lxb])
                br, bi = gen.next(), gen.next()
                K.mm(br.t[:, :], Wr.t[:, c, :], lxb.t[:], True, True, r=[Wr, lxb], w=[br], sig=True)
                K.mm(bi.t[:, :], Wi.t[:, c, :], lxb.t[:], True, True, r=[Wi, lxb], w=[bi], sig=True)
                K.act(lr_.t[:], br.t[:, :], AF.Sigmoid, r=[br, cols], w=[lr_], bias=col(C_BR + c))
                K.act(lm_.t[:], bi.t[:, :], AF.Sigmoid, r=[bi, cols], w=[lm_], bias=col(C_BI + c))
                K.act(lr_.t[:], lr_.t[:], AF.Exp, r=[lr_, clam], w=[lr_], scale=clam.t[:, c:c + 1])
                K.tt("dve", lm_.t[:], lm_.t[:], lxc.t[:], ALU.mult, r=[lm_, lxc], w=[lm_])
                K.act(lxc.t[:], lr_.t[:], AF.Square, r=[lr_], w=[lxc])
                K.act(lxc.t[:], lxc.t[:], AF.Sqrt, r=[lxc], w=[lxc], scale=-1.0, bias=1.0)
                K.tt("dve", lm_.t[:], lm_.t[:], lxc.t[:], ALU.mult, r=[lm_, lxc], w=[lm_])
                K.op("dve", lambda e, c=c: e.tensor_tensor_scan(out=lxc.t[:], data0=lr_.t[:], data1=lm_.t[:],
                                                                 initial=hst.t[:, c:c + 1], op0=ALU.mult, op1=ALU.add),
                     r=[lr_, lm_, hst], w=[lxc])
                K.copy("pool", hst.t[:, c:c + 1], lxc.t[:, 511:512], r=[lxc], w=[hst])
                K.tt("dve", lxc.t[:], lxc.t[:], li_.t[:], ALU.mult, r=[lxc, li_], w=[lxc])
                K.copy("pool", yb.t[:, c, :], lxc.t[:], r=[lxc], w=[yb])
                K.act(lm_.t[:], lxc.t[:], AF.Square, r=[lxc], w=[lm_])
                bs_ = gen.next()
                for j in range(4):
                    K.mm(bs_.t[:, (c * 4 + j) * 2:(c * 4 + j) * 2 + 2], lm_.t[:, j * 128:(j + 1) * 128], ones2.t[:],
                         True, True, r=[lm_, ones2], w=[bs_], sig=(j == 3))
                slv = sl.t[:].rearrange("p j o -> p (j o)")
                if c == 0:
                    K.copy("dve", slv, bs_.t[:, 0:8], r=[bs_], w=[sl])
                else:
                    K.tt("dve", slv, slv, bs_.t[:, c * 8:c * 8 + 8], ALU.add, r=[bs_, sl], w=[sl])
                if c == 3:
                    rstd(sl, slv, sl, slv, 512, 8)

            def O_tile(G, j):
                g = G * 4 + j
                r0 = g * 128
                at = attT[g % 8]
                yb = ybf[G % 2]
                sl = ssl[G % 2]
                xt = load_x(x_d[r0:r0 + 128, :])
                for hh in range(2):
                    ba = gen.next()
                    for k in range(4):
                        K.mm(ba.t[:, :], at.t[:, k, :], Wout.t[:, k, hh * 512:(hh + 1) * 512], k == 0, k == 3,
                             r=[at, Wout], w=[ba], sig=(k == 3))
                    K.tt("dve", x1o.t[:, hh * 512:(hh + 1) * 512], ba.t[:, :], xt.t[:, hh * 512:(hh + 1) * 512], ALU.add,
                         r=[ba, xt], w=[x1o])
                    bl = gen.next()
                    for k in range(4):
                        K.mm(bl.t[:, :], yb.t[:, k, j * 128:(j + 1) * 128], Wout.t[:, 4 + k, hh * 512:(hh + 1) * 512],
                             k == 0, k == 3, r=[yb, Wout], w=[bl], sig=(k == 3))
                    K.stt(x1o.t[:, hh * 512:(hh + 1) * 512], bl.t[:, :], sl.t[:, j, 0:1], x1o.t[:, hh * 512:(hh + 1) * 512],
                          ALU.mult, ALU.add, r=[bl, sl, x1o], w=[x1o])
                K.dma("pool", x1_d[r0:r0 + 128, :], x1o.t[:], x1_sem, r=[x1o])


            for jj in range(4):
                N_unit(0, jj)
            for i in range(5):
                P_stage(0, i)
            for g in range(NG):
                G, j = g // 4, g % 4

                def fill(h, g=g, G=G, j=j):
                    if g + 1 < NG and (g + 1) % NT != 0:
                        if h == 0:
                            P_stage(g + 1, 0)
                        elif h == 1:
                            P_stage(g + 1, 1)
                        elif h == 2:
                            P_stage(g + 1, 2)
                        elif h == 3:
                            P_stage(g + 1, 3)
                        elif h == 6:
                            P_stage(g + 1, 4)
                    if h == 4 and G + 1 < NSG:
                        if j == 1:
                            N_unit(G + 1, 0)
                            N_unit(G + 1, 1)
                        elif j == 2:
                            N_unit(G + 1, 2)
                            N_unit(G + 1, 3)
                    if h == 5:
                        L_chunk(G, j)
                    if h == 7 and G >= 1:
                        O_tile(G - 1, j)

                A_tile(g, fill)
                F_tile(g)
                if g + 1 < NG and (g + 1) % NT == 0:
                    for i in range(5):
                        P_stage(g + 1, i)
            for j in range(4):
                O_tile(NSG - 1, j)
            x1_done = Tok(None, x1_sem.h, x1_sem.val)
            K.barrier([x1_done])
            K.flush()

        with ExitStack() as es:
            Wup = K.sb(es, "Wup", [128, 8, 2 * DFF], BF16)
            Wdn = K.sb(es, "Wdn", [128, NCH, D], BF16)
            stage = [K.sb(es, "fstg%d" % i, [128, 1024], F32) for i in range(2)]
            stage_sem = [K.dsem("fstg%d" % i) for i in range(2)]
            wup_b = [Buf("wup%d" % c) for c in range(2 * NCH)]
            wdn_b = [Buf("wdn%d" % c) for c in range(NCH)]
            w_up_v = w_up_d.rearrange("(k p) n -> p k n", p=128)
            w_dn_v = w_down_d.rearrange("(c p) n -> p c n", p=128)
            si = [0]
            first = [True]

            def fl(dst_b, dst_ap, src_ap, n, scale_col, eng):
                i = si[0] % 2
                si[0] += 1
                st_ = stage[i]
                K.dma("sp", st_.t[:, 0:n], src_ap, stage_sem[i], w=[st_])
                if scale_col is not None:
                    K.ts("pool", dst_ap, st_.t[:, 0:n], scale_col, 0.0, ALU.mult, ALU.add, r=[st_, cols], w=[dst_b])
                else:
                    K.copy(eng, dst_ap, st_.t[:, 0:n], r=[st_], w=[dst_b])

            for c in range(NCH):
                for half in range(2):
                    cc = half * NCH + c
                    c0 = half * DFF + c * 128
                    i = si[0] % 2
                    si[0] += 1
                    st_ = stage[i]
                    K.dma("sp", st_.t[:, :].rearrange("p (k n) -> p k n", k=8), w_up_v[:, :, c0:c0 + 128], stage_sem[i], w=[st_])
                    K.tt("pool", Wup.t[:, :, c0:c0 + 128], st_.t[:, :].rearrange("p (k n) -> p k n", k=8),
                         cols.t[:, C_GFFN:C_GFFN + 8].unsqueeze(2).to_broadcast([128, 8, 128]), ALU.mult,
                         r=[st_, cols], w=[wup_b[cc]])
                fl(wdn_b[c], Wdn.t[:, c, :], w_dn_v[:, c, :], 1024, None, "pool")

            NX = 4
            xr = [K.sb(es, "fx%d" % i, [128, D], F32) for i in range(NX)]
            xr_sem = [K.dsem("fxs%d" % i) for i in range(NX)]
            xo_sem = [K.dsem("fxo%d" % i) for i in range(NX)]
            xi = [0]
            junk = K.sb(es, "fjunk", [128, D], BF16)
            hf = K.sb(es, "hf", [128, D], BF16)
            hfT = [K.sb(es, "hfT%d" % i, [128, 8, WIN + 2], BF16) for i in range(2)]
            ssf = K.sb(es, "ssf", [128, 4], F32)
            cg = [K.sb(es, "cg%d" % i, [128, WIN], F32) for i in range(2)]
            cv = [K.sb(es, "cv%d" % i, [128, WIN], F32) for i in range(2)]
            G = [K.sb(es, "G%d" % i, [128, NCH, WIN], BF16) for i in range(2)]
            upr = Ring(banks[0:4])
            dnr = Ring(banks[4:8])
            nwin = TOK // WIN
            wx = {}

            def FA(wdw):
                tok0 = wdw * WIN
                ht = hfT[wdw % 2]
                hprev = hfT[(wdw + 1) % 2]
                if wdw % (S // WIN) == 0:
                    K.memset("pool", ht.t[:, :, 0:2], 0.0, w=[ht])
                else:
                    K.copy("pool", ht.t[:, :, 0:2], hprev.t[:, :, WIN:WIN + 2], r=[hprev], w=[ht])
                xs_ = []
                for j in range(2):
                    i = xi[0] % NX
                    xi[0] += 1
                    xt = xr[i]
                    r0 = tok0 + j * 128
                    K.dma("sp", xt.t[:], x1_d[r0:r0 + 128, :], xr_sem[i], w=[xt])
                    if first[0]:
                        K.q["sp"].ops[-1][0].append((x1_done.sem, x1_done.val))
                        first[0] = False
                    xs_.append((xt, i))
                    cc_ = (wdw % 2) * 2 + j
                    K.act(junk.t[:], xt.t[:], AF.Square, r=[xt], w=[junk, ssf], accum=ssf.t[:, cc_:cc_ + 1])
                    rstd(ssf, ssf.t[:, cc_:cc_ + 1], ssf, ssf.t[:, cc_:cc_ + 1], D, 1)
                    K.act(hf.t[:], xt.t[:], AF.Copy, r=[xt, ssf], w=[hf], scale=ssf.t[:, cc_:cc_ + 1])
                    bk = dnr.next()
                    bv = bk.t[:].bitcast(BF16)
                    for k in range(8):
                        K.tr(bv[:, k * 128:(k + 1) * 128], hf.t[:, k * 128:(k + 1) * 128], ident.t[:],
                             r=[hf, ident], w=[bk], sig=(k == 7))
                    K.copy("dve", ht.t[:, :, 2 + j * 128:2 + (j + 1) * 128],
                           bv.rearrange("p (k t) -> p k t", k=8), r=[bk], w=[ht])
                wx[wdw] = xs_

            FA(0)
            for wdw in range(nwin):
                tok0 = wdw * WIN
                ht = hfT[wdw % 2]
                xs_ = wx.pop(wdw)
                Gw = G[wdw % 2]
                for c in range(NCH):
                    if c == 8 and wdw + 1 < nwin:
                        FA(wdw + 1)
                    bg_, bv_ = upr.next(), upr.next()
                    for half, bkk in ((0, bg_), (1, bv_)):
                        c0 = half * DFF + c * 128
                        for k in range(8):
                            K.mm(bkk.t[:, 0:WIN + 2], Wup.t[:, k, c0:c0 + 128], ht.t[:, k, :], k == 0, k == 7,
                                 r=[wup_b[half * NCH + c], ht], w=[bkk], sig=(k == 7))
                    cgt, cvt = cg[c % 2], cv[c % 2]
                    for half, bkk, dst in ((0, bg_, cgt), (1, bv_, cvt)):
                        ci = half * NCH + c
                        K.act(dst.t[:], bkk.t[:, 2:WIN + 2], AF.Identity, r=[bkk, cols], w=[dst],
                              scale=col(C_FCW + ci * 3 + 2), bias=col(C_FCB + ci))
                        K.stt(dst.t[:], bkk.t[:, 1:WIN + 1], col(C_FCW + ci * 3 + 1), dst.t[:], ALU.mult, ALU.add,
                              r=[bkk, cols, dst], w=[dst])
                        K.stt(dst.t[:], bkk.t[:, 0:WIN], col(C_FCW + ci * 3 + 0), dst.t[:], ALU.mult, ALU.add,
                              r=[bkk, cols, dst], w=[dst])
                    K.act(cgt.t[:], cgt.t[:], AF.Gelu_apprx_tanh, r=[cgt], w=[cgt])
                    K.tt("pool", Gw.t[:, c, :], cgt.t[:], cvt.t[:], ALU.mult, r=[cgt, cvt], w=[Gw])
                for j in range(2):
                    xt, i = xs_[j]
                    r0 = tok0 + j * 128
                    for hh in range(2):
                        bd = dnr.next()
                        for c in range(NCH):
                            K.mm(bd.t[:, :], Gw.t[:, c, j * 128:(j + 1) * 128], Wdn.t[:, c, hh * 512:(hh + 1) * 512],
                                 c == 0, c == NCH - 1, r=[Gw, wdn_b[c]], w=[bd], sig=(c == NCH - 1))
                        K.tt("dve", xt.t[:, hh * 512:(hh + 1) * 512], bd.t[:, :], xt.t[:, hh * 512:(hh + 1) * 512], ALU.add,
                             r=[bd, xt], w=[xt])
                    K.dma("pool", x2_d[r0:r0 + 128, :], xt.t[:], xo_sem[i], r=[xt])
            x2_toks = [Tok(None, sm.h, sm.val) for sm in xo_sem]
            K.barrier(x2_toks)
            K.flush()

        with ExitStack() as es:
            Wpg = K.sb(es, "Wpg", [128, 8, D], BF16)
            Wpl = K.sb(es, "Wpl", [128, 2, D], BF16)
            brow = K.sb(es, "brow", [128, D], BF16)
            one0 = K.sb(es, "one0", [128, 128], BF16)
            browf = K.sb(es, "browf", [128, D], F32)
            stage = [K.sb(es, "pstg%d" % i, [128, 1024], F32) for i in range(2)]
            stage_sem = [K.dsem("pstg%d" % i) for i in range(2)]
            si = [0]
            w_pg_v = w_pg_d.rearrange("(k p) n -> p k n", p=128)
            w_pl_v = w_ple_d.rearrange("(k p) n -> p k n", p=128)
            for k in range(8):
                i = si[0] % 2
                si[0] += 1
                K.dma("sp", stage[i].t[:], w_pg_v[:, k, :], stage_sem[i], w=[stage[i]])
                K.act(Wpg.t[:, k, :], stage[i].t[:], AF.Copy, r=[stage[i], cols], w=[Wpg], scale=col(C_GPLE + k))
            for k in range(2):
                i = si[0] % 2
                si[0] += 1
                K.dma("sp", stage[i].t[:], w_pl_v[:, k, :], stage_sem[i], w=[stage[i]])
                K.copy("act", Wpl.t[:, k, :], stage[i].t[:], r=[stage[i]], w=[Wpl])
            K.memset("pool", browf.t[:], 0.0, w=[browf])
            K.memset("pool", one0.t[:], 0.0, w=[one0])
            K.memset("pool", one0.t[0:1, :], 1.0, w=[one0])
            bsem = K.dsem("brow")
            K.dma("sp", browf.t[0:1, :], bple_d[0:1, :], bsem, w=[browf])
            K.copy("pool", brow.t[:], browf.t[:], r=[browf], w=[brow])

            NX = 8
            xr = [K.sb(es, "px%d" % i, [128, D], F32) for i in range(NX)]
            xr_sem = [K.dsem("pxs%d" % i) for i in range(NX)]
            xo_sem = [K.dsem("pxo%d" % i) for i in range(NX)]
            pr = [K.sb(es, "pp%d" % i, [128, 256], F32) for i in range(NX)]
            pr_sem = [K.dsem("pps%d" % i) for i in range(NX)]
            junk = K.sb(es, "pjunk", [128, D], BF16)
            hp = [K.sb(es, "hp%d" % i, [128, D], BF16) for i in range(2)]
            pb = [K.sb(es, "pb%d" % i, [128, 256], BF16) for i in range(2)]
            hpT = [K.sb(es, "hpT%d" % i, [128, 10, 128], BF16) for i in range(3)]
            ssp = K.sb(es, "ssp", [128, 4], F32)
            sg = [K.sb(es, "sg%d" % i, [128, 512], F32) for i in range(4)]
            pring = Ring(banks)
            out_toks = []
            NTI = TOK // 128

            def PL(ti):
                r0 = ti * 128
                i = ti % NX
                xt = xr[i]
                K.dma("sp", xt.t[:], x2_d[r0:r0 + 128, :], xr_sem[i], w=[xt])
                if ti == 0:
                    for tk in x2_toks:
                        K.q["sp"].ops[-1][0].append((tk.sem, tk.val))
                pt_ = pr[i]
                K.dma("sp", pt_.t[:], p_d[r0:r0 + 128, :], pr_sem[i], w=[pt_])

            def PA(ti):
                r0 = ti * 128
                i = ti % NX
                xt = xr[i]
                pt_ = pr[i]
                cc_ = ti % 4
                hp_ = hp[ti % 2]
                pb_ = pb[ti % 2]
                K.act(junk.t[:], xt.t[:], AF.Square, r=[xt], w=[junk, ssp], accum=ssp.t[:, cc_:cc_ + 1])
                rstd(ssp, ssp.t[:, cc_:cc_ + 1], ssp, ssp.t[:, cc_:cc_ + 1], D, 1)
                K.act(hp_.t[:], xt.t[:], AF.Copy, r=[xt, ssp], w=[hp_], scale=ssp.t[:, cc_:cc_ + 1])
                K.copy("pool", pb_.t[:], pt_.t[:], r=[pt_], w=[pb_])
                hT = hpT[ti % 3]
                bk = pring.next()
                bv = bk.t[:].bitcast(BF16)
                for k in range(8):
                    K.tr(bv[:, k * 128:(k + 1) * 128], hp_.t[:, k * 128:(k + 1) * 128], ident.t[:],
                         r=[hp_, ident], w=[bk], sig=(k == 7))
                K.copy("dve", hT.t[:, 0:8, :], bv.rearrange("p (k t) -> p k t", k=8), r=[bk], w=[hT])
                bk = pring.next()
                bv = bk.t[:].bitcast(BF16)
                for k in range(2):
                    K.tr(bv[:, k * 128:(k + 1) * 128], pb_.t[:, k * 128:(k + 1) * 128], ident.t[:],
                         r=[pb_, ident], w=[bk], sig=(k == 1))
                K.copy("dve", hT.t[:, 8:10, :], bv[:, 0:256].rearrange("p (k t) -> p k t", k=2), r=[bk], w=[hT])

            def PB(ti):
                r0 = ti * 128
                i = ti % NX
                xt = xr[i]
                hT = hpT[ti % 3]
                for hh in range(2):
                    bgt, be = pring.next(), pring.next()
                    for k in range(8):
                        K.mm(bgt.t[:, :], hT.t[:, k, :], Wpg.t[:, k, hh * 512:(hh + 1) * 512], k == 0, False,
                             r=[hT, Wpg], w=[bgt], sig=False)
                    K.mm(bgt.t[:, :], one0.t[:], brow.t[:, hh * 512:(hh + 1) * 512], False, True,
                         r=[one0, brow], w=[bgt], sig=True)
                    for k in range(2):
                        K.mm(be.t[:, :], hT.t[:, 8 + k, :], Wpl.t[:, k, hh * 512:(hh + 1) * 512], k == 0, k == 1,
                             r=[hT, Wpl], w=[be], sig=(k == 1))
                    sgt = sg[(ti * 2 + hh) % 4]
                    K.act(sgt.t[:], bgt.t[:, :], AF.Sigmoid, r=[bgt], w=[sgt])
                    K.tt("dve", sgt.t[:], sgt.t[:], be.t[:, :], ALU.mult, r=[sgt, be], w=[sgt])
                    K.tt("pool", xt.t[:, hh * 512:(hh + 1) * 512], xt.t[:, hh * 512:(hh + 1) * 512], sgt.t[:], ALU.add,
                         r=[xt, sgt], w=[xt])
                out_toks.append(K.dma("pool", out_d[r0:r0 + 128, :], xt.t[:], xo_sem[i], r=[xt]))

            for ti in range(5):
                PL(ti)
            PA(0)
            PA(1)
            for ti in range(NTI):
                if ti + 5 < NTI:
                    PL(ti + 5)
                if ti + 2 < NTI:
                    PA(ti + 2)
                PB(ti)
            K.flush(final_toks=out_toks[-NX:])
    return nc


def _host_inputs(inputs):
    f = lambda a: np.ascontiguousarray(np.asarray(a, dtype=np.float32))
    x = f(inputs["x"])
    p = f(inputs["p"])[0]
    pos = np.asarray(inputs["positions"]).astype(np.int32)

    def colz(v):
        v = f(v).reshape(-1)
        return v.reshape(-1, 128).T

    cols = np.zeros((128, NCOL), np.float32)
    cols[:, C_GMIX:C_GMIX + 8] = colz(inputs["g_mix"][0])
    cols[:, C_GFFN:C_GFFN + 8] = colz(inputs["g_ffn"][0])
    cols[:, C_GPLE:C_GPLE + 8] = colz(inputs["g_ple"][0])
    cols[:, C_GCQ:C_GCQ + 2] = colz(inputs["g_cq"][0])
    cols[:, C_GCKV:C_GCKV + 1] = colz(inputs["g_ckv"][0])
    cols[:, C_GOUT:C_GOUT + 4] = colz(inputs["g_attn_out"][0])
    cols[:, C_GOUT + 4:C_GOUT + 8] = colz(inputs["g_lru_out"][0])
    lcw = f(inputs["w_lru_conv"][0])
    cols[:, C_LCW:C_LCW + 16] = lcw.reshape(4, 4, 128).transpose(2, 1, 0).reshape(128, 16)
    cols[:, C_LCB:C_LCB + 4] = colz(inputs["b_lru_conv"][0])
    cols[:, C_BR:C_BR + 4] = colz(inputs["b_lru_r"][0])
    cols[:, C_BI:C_BI + 4] = colz(inputs["b_lru_i"][0])
    cols[:, C_LAM:C_LAM + 4] = colz(inputs["lru_lambda"][0])
    fcw = f(inputs["w_ffn_conv"][0])
    cols[:, C_FCW:C_FCW + 132] = fcw.reshape(3, 44, 128).transpose(2, 1, 0).reshape(128, 132)
    cols[:, C_FCB:C_FCB + 44] = colz(inputs["b_ffn_conv"][0])
    rows = np.zeros((1, NROW), np.float32)
    rows[0, R_GQ:R_GQ + 96] = f(inputs["g_qn"][0])
    rows[0, R_GK:R_GK + 96] = f(inputs["g_kn"][0])
    rows[0, R_IF:R_IF + 16] = (10000.0 ** (-np.arange(16, dtype=np.float32) / np.float32(16))).astype(np.float32)
    shared = {
        "w_in": f(inputs["w_in"][0]), "w_uq": f(inputs["w_uq"][0]), "w_ukv": f(inputs["w_ukv"][0]),
        "w_out": f(inputs["w_out"][0]), "w_up": f(inputs["w_up"][0]), "w_down": f(inputs["w_down"][0]),
        "w_pg": f(inputs["w_ple_gate"][0]), "w_ple": f(inputs["w_ple"][0]),
        "w_r": f(inputs["w_lru_r"][0]).reshape(512, 64), "w_i": f(inputs["w_lru_i"][0]).reshape(512, 64),
        "cols": cols, "rows": rows, "bple": f(inputs["b_ple_gate"][0]).reshape(1, D),
        "ident": np.eye(128, dtype=np.float32).astype(ml_dtypes.bfloat16),
    }
    maps = []
    for c in range(NCORES):
        m = dict(shared)
        m["x"] = np.ascontiguousarray(x[2 * c:2 * c + 2].reshape(TOK, D))
        m["p"] = np.ascontiguousarray(p[2 * c:2 * c + 2].reshape(TOK, 256))
        pc = pos[2 * c:2 * c + 2]
        m["posT"] = np.ascontiguousarray(pc.reshape(2, NT, 128).transpose(2, 0, 1).reshape(128, 2 * NT))
        maps.append(m)
    return maps


def kernel(**inputs):
    maps = _host_inputs(inputs)
    nc = build_nc()
    res = run_bass_kernel_spmd(nc, maps, core_ids=list(range(NCORES)))
    outs = [np.asarray(r["out"], dtype=np.float32).reshape(2, S, D) for r in res.results]
    return np.concatenate(outs, axis=0)
```

```python
import math
import numpy as np
import ml_dtypes
from contextlib import ExitStack
import concourse.bass as bass
import concourse.mybir as mybir
from concourse.bass_utils import run_bass_kernel_spmd

F32 = mybir.dt.float32
BF16 = mybir.dt.bfloat16
I32 = mybir.dt.int32
AF = mybir.ActivationFunctionType
ALU = mybir.AluOpType
AX = mybir.AxisListType

NCORES = 8
D = 1024
S = 2048
NT = S // 128
TOK = 2 * S
H = 8
EPS = 1e-6
DFF = 2816
NCH = DFF // 128
WIN = 256

C_GMIX, C_GFFN, C_GPLE, C_GCQ, C_GCKV, C_GOUT = 0, 8, 16, 24, 26, 27
C_LCW, C_LCB, C_BR, C_BI, C_LAM = 35, 51, 55, 59, 63
C_FCW, C_FCB = 67, 67 + 132
NCOL = C_FCB + 44
R_GQ, R_GK, R_IF = 0, 96, 192
NROW = 208


class Tok:
    __slots__ = ("q", "sem", "val")

    def __init__(self, q, sem, val):
        self.q, self.sem, self.val = q, sem, val


class Buf:
    def __init__(self, name=""):
        self.name = name
        self.w = None
        self.r = {}


class TB:
    def __init__(self, t, name=""):
        self.t = t
        self.b = Buf(name)


class Q:
    def __init__(self, name, sem):
        self.name, self.sem = name, sem
        self.n = 0
        self.ops = []
        self.waited = {}
        self.pend_r = []
        self.pend_w = []


class DSem:
    def __init__(self, h):
        self.h = h
        self.val = 0


def _b(x):
    return x.b if isinstance(x, TB) else x


class Kern:
    def __init__(self, nc, es):
        self.nc = nc
        self.es = es
        self.q = {}
        for n in ("pe", "dve", "act", "pool", "sp"):
            self.q[n] = Q(n, es.enter_context(nc.semaphore("q_" + n)))
        self.nsem = 5

    def dsem(self, name):
        self.nsem += 1
        return DSem(self.es.enter_context(self.nc.semaphore(name)))

    def sb(self, es, name, shape, dt):
        return TB(es.enter_context(self.nc.sbuf_tensor("s_" + name, list(shape), dt)), name)

    def _waits(self, q, r, w):
        needs = {}

        def need(t):
            if t is None:
                return
            k = id(t.sem)
            if k not in needs or needs[k][1] < t.val:
                needs[k] = (t.sem, t.val)

        for b in r:
            need(_b(b).w)
        for b in w:
            b = _b(b)
            if b.w is not None and b.w.q is not q:
                need(b.w)
            for t in b.r.values():
                if t.q is not q:
                    need(t)
        waits = []
        for k, (sem, val) in needs.items():
            if q.waited.get(k, 0) < val:
                q.waited[k] = val
                waits.append((sem, val))
        return waits

    @staticmethod
    def _stamp(tok, r, w):
        for b in r:
            b = _b(b)
            k = id(tok.sem)
            o = b.r.get(k)
            if o is None or o.val < tok.val:
                b.r[k] = tok
        for b in w:
            b = _b(b)
            b.w = tok
            b.r = {}

    def op(self, qn, f, r=(), w=(), sig=True):
        q = self.q[qn]
        waits = self._waits(q, r, w)
        if sig:
            q.n += 1
            tok = Tok(q, q.sem, q.n)
            q.ops.append((waits, f, (q.sem, 1)))
            self._stamp(tok, list(r) + q.pend_r, list(w) + q.pend_w)
            q.pend_r, q.pend_w = [], []
            return tok
        q.ops.append((waits, f, None))
        q.pend_r += list(r)
        q.pend_w += list(w)
        return None

    def dma(self, qn, out, in_, sem, r=(), w=()):
        q = self.q[qn]
        assert not q.pend_r and not q.pend_w
        waits = self._waits(q, r, w)
        sem.val += 16
        tok = Tok(None, sem.h, sem.val)
        q.ops.append((waits, lambda e: e.dma_start(out=out, in_=in_), (sem.h, 16)))
        self._stamp(tok, r, w)
        return tok

    def barrier(self, extra=()):
        for q in self.q.values():
            waits = []
            for oq in self.q.values():
                if oq is q or oq.n == 0:
                    continue
                k = id(oq.sem)
                if q.waited.get(k, 0) < oq.n:
                    q.waited[k] = oq.n
                    waits.append((oq.sem, oq.n))
            for t in extra:
                k = id(t.sem)
                if q.waited.get(k, 0) < t.val:
                    q.waited[k] = t.val
                    waits.append((t.sem, t.val))
            if waits:
                q.ops.append((waits, None, None))

    def flush(self, final_toks=()):
        nc = self.nc
        if final_toks:
            q = self.q["sp"]
            needs = {}
            for t in final_toks:
                k = id(t.sem)
                if k not in needs or needs[k][1] < t.val:
                    needs[k] = (t.sem, t.val)
            q.ops.append((list(needs.values()), None, None))
        with nc.Block() as blk:
            for qn, deco in (("sp", blk.sync), ("act", blk.scalar), ("pool", blk.gpsimd),
                             ("dve", blk.vector), ("pe", blk.tensor)):
                qq = self.q[qn]
                assert not qq.pend_r and not qq.pend_w, qn
                ops = qq.ops
                qq.ops = []

                def body(e, ops=ops):
                    for waits, f, inc in ops:
                        for sem, val in waits:
                            e.wait_ge(sem, val)
                        if f is None:
                            continue
                        ins = f(e)
                        if inc is not None:
                            ins.then_inc(inc[0], inc[1])

                deco(body)

    def mm(self, out, lhsT, rhs, start, stop, r=(), w=(), sig=False):
        return self.op("pe", lambda e: e.matmul(out, lhsT, rhs, start=start, stop=stop), r, w, sig)

    def tr(self, out, in_, ident, r=(), w=(), sig=False):
        return self.op("pe", lambda e: e.transpose(out, in_, ident), r, w, sig)

    def act(self, out, in_, func, r=(), w=(), bias=None, scale=None, accum=None, qn="act"):
        kw = {}
        if bias is not None:
            kw["bias"] = bias
        if scale is not None:
            kw["scale"] = scale
        if accum is not None:
            kw["accum_out"] = accum
        return self.op(qn, lambda e: e.activation(out=out, in_=in_, func=func, **kw), r, w)

    def tt(self, qn, out, in0, in1, op, r=(), w=()):
        return self.op(qn, lambda e: e.tensor_tensor(out=out, in0=in0, in1=in1, op=op), r, w)

    def ts(self, qn, out, in0, s1, s2, op0, op1, r=(), w=()):
        return self.op(qn, lambda e: e.tensor_scalar(out=out, in0=in0, scalar1=s1, scalar2=s2,
                                                     op0=op0, op1=op1), r, w)

    def stt(self, out, in0, scalar, in1, op0, op1, r=(), w=()):
        return self.op("dve", lambda e: e.scalar_tensor_tensor(out=out, in0=in0, scalar=scalar, in1=in1,
                                                               op0=op0, op1=op1), r, w)

    def copy(self, qn, out, in_, r=(), w=()):
        if qn == "act":
            return self.op(qn, lambda e: e.copy(out=out, in_=in_), r, w)
        return self.op(qn, lambda e: e.tensor_copy(out=out, in_=in_), r, w)

    def memset(self, qn, ap, val, w=()):
        return self.op(qn, lambda e: e.memset(ap, val), (), w)


def build_nc():
    nc = bass.Bass("TRN2", target_bir_lowering=False)

    def din(name, shape, dt=F32):
        return nc.dram_tensor(name, list(shape), dt, kind="ExternalInput").ap()

    x_d = din("x", [TOK, D])
    p_d = din("p", [TOK, 256])
    pos_d = din("posT", [128, 2 * NT], I32)
    w_in_d = din("w_in", [D, 1440])
    w_uq_d = din("w_uq", [256, 768])
    w_ukv_d = din("w_ukv", [128, 1024])
    w_out_d = din("w_out", [D, D])
    w_up_d = din("w_up", [D, 2 * DFF])
    w_down_d = din("w_down", [DFF, D])
    w_pg_d = din("w_pg", [D, D])
    w_ple_d = din("w_ple", [256, D])
    w_r_d = din("w_r", [512, 64])
    w_i_d = din("w_i", [512, 64])
    cols_d = din("cols", [128, NCOL])
    rows_d = din("rows", [1, NROW])
    bple_d = din("bple", [1, D])
    ident_d = din("ident", [128, 128], BF16)
    out_d = nc.dram_tensor("out", [TOK, D], F32, kind="ExternalOutput").ap()
    x1_d = nc.dram_tensor("x1s", [TOK, D], F32, kind="Internal").ap()
    x2_d = nc.dram_tensor("x2s", [TOK, D], F32, kind="Internal").ap()

    with ExitStack() as g:
        K = Kern(nc, g)
        banks = [TB(g.enter_context(nc.psum_tensor("ps%d" % i, [128, 512], F32)), "ps%d" % i) for i in range(8)]

        class Ring:
            def __init__(self, items):
                self.items = items
                self.i = 0

            def next(self):
                it = self.items[self.i % len(self.items)]
                self.i += 1
                return it

        cols = K.sb(g, "cols", [128, NCOL], F32)
        rows = K.sb(g, "rows", [128, NROW], F32)
        ident = K.sb(g, "ident", [128, 128], BF16)
        nhalf = K.sb(g, "nhalf", [128, 16], F32)
        cst_sem = K.dsem("cst")
        K.dma("sp", cols.t[:], cols_d[:, :], cst_sem, w=[cols])
        K.dma("sp", rows.t[:], rows_d.partition_broadcast(128)[:, 0, :], cst_sem, w=[rows])
        K.dma("sp", ident.t[:], ident_d[:, :], cst_sem, w=[ident])
        ftok = Tok(None, cst_sem.h, cst_sem.val)
        for tb in (cols, rows, ident):
            tb.b.w = ftok
        K.memset("pool", nhalf.t[:], -0.5, w=[nhalf])

        def col(c0, n=1):
            return cols.t[:, c0:c0 + n]

        def rstd(out_tb, out_ap, ss_tb, ss_ap, nfeat, n):
            K.ts("pool", ss_ap, ss_ap, 1.0 / nfeat, EPS, ALU.mult, ALU.add, r=[ss_tb], w=[ss_tb])
            K.tt("pool", out_ap, ss_ap, nhalf.t[:, 0:n], ALU.pow, r=[ss_tb, nhalf], w=[out_tb])

        with ExitStack() as es:
            Win = K.sb(es, "Win", [128, 8, 1440], BF16)
            Wuq = K.sb(es, "Wuq", [128, 2, 768], BF16)
            Wukv = K.sb(es, "Wukv", [128, 1024], BF16)
            Wout = K.sb(es, "Wout", [128, 8, 1024], BF16)
            Wr = K.sb(es, "Wr", [128, 4, 128], BF16)
            Wi = K.sb(es, "Wi", [128, 4, 128], BF16)
            stage = [K.sb(es, "stg%d" % i, [128, 512], F32) for i in range(2)]
            stage_sem = [K.dsem("stg%d" % i) for i in range(2)]
            stg_i = [0]
            lxc = K.sb(es, "lxc", [128, 512], F32)
            lxb = K.sb(es, "lxb", [128, 512], BF16)
            lr_ = K.sb(es, "lr", [128, 512], F32)
            li_ = K.sb(es, "li", [128, 512], F32)
            lm_ = K.sb(es, "lm", [128, 512], F32)

            def load_cast(dst_tb, dst_ap, src_ap, n, scale_col=None):
                i = stg_i[0] % 2
                stg_i[0] += 1
                st_ = stage[i]
                K.dma("sp", st_.t[:, 0:n], src_ap, stage_sem[i], w=[st_])
                if scale_col is None:
                    K.copy("act", dst_ap, st_.t[:, 0:n], r=[st_], w=[dst_tb])
                else:
                    K.act(dst_ap, st_.t[:, 0:n], AF.Copy, r=[st_, cols], w=[dst_tb], scale=scale_col)

            w_in_v = w_in_d.rearrange("(k p) n -> p k n", p=128)
            for k in range(8):
                for h0 in (0, 480, 960):
                    load_cast(Win, Win.t[:, k, h0:h0 + 480], w_in_v[:, k, h0:h0 + 480], 480, col(C_GMIX + k))
            w_uq_v = w_uq_d.rearrange("(k p) n -> p k n", p=128)
            for k in range(2):
                for h0 in (0, 384):
                    load_cast(Wuq, Wuq.t[:, k, h0:h0 + 384], w_uq_v[:, k, h0:h0 + 384], 384, col(C_GCQ + k))
            for h0 in (0, 512):
                load_cast(Wukv, Wukv.t[:, h0:h0 + 512], w_ukv_d[:, h0:h0 + 512], 512, col(C_GCKV))
            w_out_v = w_out_d.rearrange("(k p) n -> p k n", p=128)
            for k in range(8):
                for h0 in (0, 512):
                    load_cast(Wout, Wout.t[:, k, h0:h0 + 512], w_out_v[:, k, h0:h0 + 512], 512, col(C_GOUT + k))
            bd_sem = K.dsem("bd")
            for gi, (wd, stg_, dst) in enumerate(((w_r_d, lr_, Wr), (w_i_d, li_, Wi))):
                K.memset("pool", stg_.t[:], 0.0, w=[stg_])
                sv = stg_.t[:].rearrange("p (c n) -> p c n", c=4)
                for blk in range(8):
                    c, hf_ = blk // 2, blk % 2
                    K.dma("sp", sv[hf_ * 64:(hf_ + 1) * 64, c, hf_ * 64:(hf_ + 1) * 64],
                          wd[blk * 64:(blk + 1) * 64, :], bd_sem, w=[stg_])
            for stg_ in (lr_, li_):
                stg_.b.w = Tok(None, bd_sem.h, bd_sem.val)
            K.copy("pool", Wr.t[:], lr_.t[:].rearrange("p (c n) -> p c n", c=4), r=[lr_], w=[Wr])
            K.copy("pool", Wi.t[:], li_.t[:].rearrange("p (c n) -> p c n", c=4), r=[li_], w=[Wi])

            gq = K.sb(es, "gq", [128, 96], F32)
            K.ts("pool", gq.t[:], rows.t[:, R_GQ:R_GQ + 96], 96.0 ** -0.5, 0.0, ALU.mult, ALU.add, r=[rows], w=[gq])
            clam = K.sb(es, "clam", [128, 4], F32)
            K.act(clam.t[:], col(C_LAM, 4), AF.Exp, r=[cols], w=[clam], scale=-1.0)
            K.act(clam.t[:], clam.t[:], AF.Ln, r=[clam], w=[clam], bias=1.0)
            K.ts("pool", clam.t[:], clam.t[:], -8.0, 0.0, ALU.mult, ALU.add, r=[clam], w=[clam])
            ones2 = K.sb(es, "ones2", [128, 2], F32)
            K.memset("pool", ones2.t[:], 1.0, w=[ones2])

            KT = K.sb(es, "KT", [128, 8, S], BF16)
            VA = K.sb(es, "VA", [128, NT, 8, 65], BF16)
            KTb = [Buf("KT%d" % i) for i in range(NT)]
            VAb = [Buf("VA%d" % i) for i in range(NT)]
            K.memset("pool", VA.t[:], 1.0, w=VAb)
            posi = K.sb(es, "posi", [128, 2 * NT], I32)
            pos_sem = K.dsem("pos")
            K.dma("sp", posi.t[:], pos_d[:, :], pos_sem, w=[posi])
            posf = K.sb(es, "posf", [128, 2 * NT], F32)
            K.copy("dve", posf.t[:], posi.t[:], r=[posi], w=[posf])
            NA = 2 * NT
            angi = K.sb(es, "angi", [128, NA, 16], I32)
            rsin = K.sb(es, "rsin", [128, NA, 16], F32)
            rcos = K.sb(es, "rcos", [128, NA, 16], F32)
            x1o = K.sb(es, "x1o", [128, D], F32)
            ang_t, angn_t, angm_t = lxc, lm_, x1o
            angv = lxc.t[:].rearrange("p (a f) -> p a f", f=16)
            angnv = lm_.t[:].rearrange("p (a f) -> p a f", f=16)
            angmv = x1o.t[:, 0:512].rearrange("p (a f) -> p a f", f=16)
            TWO_PI = 2.0 * math.pi
            C1 = 6.28125
            C2 = TWO_PI - C1
            PI_IN = 3.1415925

            def wrap():
                K.ts("dve", angmv, angv, math.pi, -TWO_PI, ALU.is_gt, ALU.mult, r=[ang_t], w=[angm_t])
                K.tt("dve", angv, angv, angmv, ALU.add, r=[ang_t, angm_t], w=[ang_t])
                K.ts("dve", angmv, angv, -math.pi, TWO_PI, ALU.is_lt, ALU.mult, r=[ang_t], w=[angm_t])
                K.tt("dve", angv, angv, angmv, ALU.add, r=[ang_t, angm_t], w=[ang_t])
                K.ts("dve", angv, angv, PI_IN, -PI_IN, ALU.min, ALU.max, r=[ang_t], w=[ang_t])

            pf = posf.t[:, :].unsqueeze(2).to_broadcast([128, NA, 16])
            fr = rows.t[:, R_IF:R_IF + 16].unsqueeze(1).to_broadcast([128, NA, 16])
            K.tt("dve", angv, pf, fr, ALU.mult, r=[posf, rows], w=[ang_t])
            K.ts("dve", angnv, angv, 1.0 / TWO_PI, None, ALU.mult, ALU.bypass, r=[ang_t], w=[angn_t])
            K.copy("dve", angi.t[:], angnv, r=[angn_t], w=[angi])
            K.copy("dve", angnv, angi.t[:], r=[angi], w=[angn_t])
            K.stt(angv, angnv, -C1, angv, ALU.mult, ALU.add, r=[angn_t, ang_t], w=[ang_t])
            K.stt(angv, angnv, -C2, angv, ALU.mult, ALU.add, r=[angn_t, ang_t], w=[ang_t])
            wrap()
            K.act(rsin.t[:], angv, AF.Sin, r=[ang_t], w=[rsin])
            K.ts("dve", angv, angv, math.pi / 2, None, ALU.add, ALU.bypass, r=[ang_t], w=[ang_t])
            wrap()
            K.act(rcos.t[:], angv, AF.Sin, r=[ang_t], w=[rcos])

            xr = [K.sb(es, "x%d" % i, [128, D], F32) for i in range(3)]
            xr_sem = [K.dsem("xs%d" % i) for i in range(3)]
            xr_i = [0]

            def load_x(src_ap):
                i = xr_i[0] % 3
                xr_i[0] += 1
                tb = xr[i]
                K.dma("sp", tb.t[:], src_ap, xr_sem[i], w=[tb])
                return tb

            junk = K.sb(es, "junk", [128, D], BF16)
            hn = Ring([K.sb(es, "hn%d" % i, [128, D], BF16) for i in range(2)])
            hnT = [K.sb(es, "hnT%d" % i, [128, 8, 512], BF16) for i in range(2)]
            ss4 = K.sb(es, "ss4", [128, 8], F32)
            mla = K.sb(es, "mla", [128, 416], F32)
            ssm = K.sb(es, "ssm", [128, 4], F32)
            cqn = K.sb(es, "cqn", [128, 384], BF16)
            cqT = K.sb(es, "cqT", [128, 3, 128], BF16)
            qf = K.sb(es, "qf", [128, 8, 96], F32)
            sq = K.sb(es, "sq", [128, 8, 96], F32)
            ssh = K.sb(es, "ssh", [128, 16], F32)
            Qb = K.sb(es, "Qb", [128, 8, 96], BF16)
            Kb = K.sb(es, "Kb", [128, 8, 96], BF16)
            QT = [K.sb(es, "QT%d" % i, [128, 8, 128], BF16) for i in range(2)]
            rk = K.sb(es, "rk", [128, 32], F32)
            rk2 = K.sb(es, "rk2", [128, 32], F32)
            rt = [K.sb(es, "rt%d" % i, [128, 8, 16], F32) for i in range(2)]
            PT = Ring([K.sb(es, "PT%d" % i, [128, 4, 128], BF16) for i in range(4)])
            rec = K.sb(es, "rec", [128, 8], F32)
            attn = K.sb(es, "attn", [128, 8, 64], F32)
            ssa = K.sb(es, "ssa", [128, 2], F32)
            attb = K.sb(es, "attb", [128, 512], BF16)
            attT = [K.sb(es, "attT%d" % i, [128, 4, 128], BF16) for i in range(8)]
            xlb = K.sb(es, "xlb", [128, 4, 515], F32)
            hst = K.sb(es, "hst", [128, 4], F32)
            ybf = [K.sb(es, "ybf%d" % i, [128, 4, 512], BF16) for i in range(2)]
            ssl = [K.sb(es, "ssl%d" % i, [128, 4, 2], F32) for i in range(2)]
            x1_sem = K.dsem("x1o")

            bP = [banks[4], banks[5]]
            bL = banks[6]
            bM = banks[7]
            sbk = Ring(banks[0:2])
            obk = banks[2:4]
            NG = 2 * NT
            NSG = NG // 4

            def N_gen(G, jj):
                r0 = G * 512 + jj * 128
                col_ = (G % 2) * 4 + jj
                xt = load_x(x_d[r0:r0 + 128, :])
                h_ = hn.next()
                yield
                K.act(junk.t[:], xt.t[:], AF.Square, r=[xt], w=[junk, ss4], accum=ss4.t[:, col_:col_ + 1])
                yield
                rstd(ss4, ss4.t[:, col_:col_ + 1], ss4, ss4.t[:, col_:col_ + 1], D, 1)
                yield
                K.act(h_.t[:], xt.t[:], AF.Copy, r=[xt, ss4], w=[h_], scale=ss4.t[:, col_:col_ + 1])
                yield
                bk = bM
                bv = bk.t[:].bitcast(BF16)
                for k in range(8):
                    K.tr(bv[:, k * 128:(k + 1) * 128], h_.t[:, k * 128:(k + 1) * 128], ident.t[:],
                         r=[h_, ident], w=[bk], sig=(k == 7))
                hT = hnT[G % 2]
                K.copy("dve", hT.t[:, :, jj * 128:(jj + 1) * 128],
                       bv.rearrange("p (k t) -> p k t", k=8), r=[bk], w=[hT])

            def P_gen(g):
                G, j = g // 4, g % 4
                t = g % NT
                hT = hnT[G % 2]
                bk = bP[0]
                for k in range(8):
                    K.mm(bk.t[:, 0:416], hT.t[:, k, j * 128:(j + 1) * 128], Win.t[:, k, 0:416],
                         k == 0, k == 7, r=[hT, Win], w=[bk], sig=(k == 7))
                yield
                K.copy("act", mla.t[:], bk.t[:, 0:416], r=[bk], w=[mla])
                yield
                K.act(junk.t[:, 0:256], mla.t[:, 0:256], AF.Square, r=[mla], w=[junk, ssm], accum=ssm.t[:, 0:1])
                K.act(junk.t[:, 0:128], mla.t[:, 256:384], AF.Square, r=[mla], w=[junk, ssm], accum=ssm.t[:, 1:2])
                K.act(junk.t[:, 0:32], mla.t[:, 384:416], AF.Square, r=[mla], w=[junk, ssm], accum=ssm.t[:, 2:3])
                yield
                rstd(ssm, ssm.t[:, 0:1], ssm, ssm.t[:, 0:1], 256, 1)
                rstd(ssm, ssm.t[:, 1:2], ssm, ssm.t[:, 1:2], 128, 1)
                yield
                K.act(cqn.t[:, 0:256], mla.t[:, 0:256], AF.Copy, r=[mla, ssm], w=[cqn], scale=ssm.t[:, 0:1])
                K.act(cqn.t[:, 256:384], mla.t[:, 256:384], AF.Copy, r=[mla, ssm], w=[cqn], scale=ssm.t[:, 1:2])
                yield
                bk = bP[1]
                bv = bk.t[:].bitcast(BF16)
                for k in range(3):
                    K.tr(bv[:, k * 128:(k + 1) * 128], cqn.t[:, k * 128:(k + 1) * 128], ident.t[:],
                         r=[cqn, ident], w=[bk], sig=(k == 2))
                yield
                K.copy("dve", cqT.t[:], bv[:, 0:384].rearrange("p (k t) -> p k t", k=3), r=[bk], w=[cqT])
                yield
                bq0, bq1 = bP[0], bP[1]
                for k in range(2):
                    K.mm(bq0.t[:, 0:384], cqT.t[:, k, :], Wuq.t[:, k, 0:384], k == 0, k == 1,
                         r=[cqT, Wuq], w=[bq0], sig=(k == 1))
                for k in range(2):
                    K.mm(bq1.t[:, 0:384], cqT.t[:, k, :], Wuq.t[:, k, 384:768], k == 0, k == 1,
                         r=[cqT, Wuq], w=[bq1], sig=(k == 1))
                yield
                qf2 = qf.t[:].rearrange("p h d -> p (h d)")
                K.copy("act", qf2[:, 0:384], bq0.t[:, 0:384], r=[bq0], w=[qf])
                K.copy("act", qf2[:, 384:768], bq1.t[:, 0:384], r=[bq1], w=[qf])
                yield
                bk0, bk1 = bP[0], bP[1]
                K.mm(bk0.t[:, :], cqT.t[:, 2, :], Wukv.t[:, 0:512], True, True, r=[cqT, Wukv], w=[bk0], sig=True)
                K.mm(bk1.t[:, :], cqT.t[:, 2, :], Wukv.t[:, 512:1024], True, True, r=[cqT, Wukv], w=[bk1], sig=True)
                K.tt("pool", sq.t[:], qf.t[:], qf.t[:], ALU.mult, r=[qf], w=[sq])
                yield
                K.op("dve", lambda e: e.tensor_reduce(out=ssh.t[:, 0:8], in_=sq.t[:], axis=AX.X, op=ALU.add),
                     r=[sq], w=[ssh])
                yield
                for hb, bkk in ((0, bk0), (1, bk1)):
                    kvv = bkk.t[:].rearrange("p (h d) -> p h d", h=4)
                    K.act(sq.t[:, hb * 4:(hb + 1) * 4, 0:64], kvv[:, :, 0:64], AF.Square, r=[bkk], w=[sq])
                    K.copy("act", VA.t[:, t, hb * 4:(hb + 1) * 4, 0:64], kvv[:, :, 64:128], r=[bkk], w=[VAb[t]])
                yield
                K.op("dve", lambda e: e.tensor_reduce(out=ssh.t[:, 8:16], in_=sq.t[:, :, 0:64], axis=AX.X, op=ALU.add),
                     r=[sq], w=[ssh])
                yield
                K.ts("dve", ssh.t[:, 8:16], ssh.t[:, 8:16], ssm.t[:, 2:3], None, ALU.add, ALU.bypass,
                     r=[ssh, ssm], w=[ssh])
                yield
                rstd(ssh, ssh.t[:], ssh, ssh.t[:], 96, 16)
                K.tt("pool", rk.t[:], mla.t[:, 384:416], rows.t[:, R_GK + 64:R_GK + 96], ALU.mult, r=[mla, rows], w=[rk])
                yield
                K.tt("dve", qf.t[:], qf.t[:], ssh.t[:, 0:8].unsqueeze(2).to_broadcast([128, 8, 96]), ALU.mult,
                     r=[qf, ssh], w=[qf])
                c16, s16 = rcos.t[:, g, :], rsin.t[:, g, :]
                K.tt("pool", rt[0].t[:, 0, :], rk.t[:, 0:16], c16, ALU.mult, r=[rk, rcos], w=[rt[0]])
                K.tt("pool", rt[1].t[:, 0, :], rk.t[:, 16:32], s16, ALU.mult, r=[rk, rsin], w=[rt[1]])
                yield
                K.tt("dve", qf.t[:], qf.t[:], gq.t[:, :].unsqueeze(1).to_broadcast([128, 8, 96]), ALU.mult,
                     r=[qf, gq], w=[qf])
                K.tt("pool", rk2.t[:, 0:16], rt[0].t[:, 0, :], rt[1].t[:, 0, :], ALU.subtract, r=[rt[0], rt[1]], w=[rk2])
                yield
                K.tt("pool", rt[0].t[:, 0, :], rk.t[:, 16:32], c16, ALU.mult, r=[rk, rcos], w=[rt[0]])
                K.tt("pool", rt[1].t[:, 0, :], rk.t[:, 0:16], s16, ALU.mult, r=[rk, rsin], w=[rt[1]])
                for hb, bkk in ((0, bk0), (1, bk1)):
                    kvv = bkk.t[:].rearrange("p (h d) -> p h d", h=4)
                    K.tt("dve", sq.t[:, hb * 4:(hb + 1) * 4, 0:64], kvv[:, :, 0:64],
                         ssh.t[:, 8 + hb * 4:12 + hb * 4].unsqueeze(2).to_broadcast([128, 4, 64]), ALU.mult,
                         r=[bkk, ssh], w=[sq])
                yield
                K.tt("pool", rk2.t[:, 16:32], rt[0].t[:, 0, :], rt[1].t[:, 0, :], ALU.add, r=[rt[0], rt[1]], w=[rk2])
                K.copy("act", Qb.t[:, :, 0:64], qf.t[:, :, 0:64], r=[qf], w=[Qb])
                K.tt("dve", Kb.t[:, :, 0:64], sq.t[:, :, 0:64],
                     rows.t[:, R_GK:R_GK + 64].unsqueeze(1).to_broadcast([128, 8, 64]), ALU.mult,
                     r=[sq, rows], w=[Kb])
                yield
                cosb = rcos.t[:, g, :].unsqueeze(1).to_broadcast([128, 8, 16])
                sinb = rsin.t[:, g, :].unsqueeze(1).to_broadcast([128, 8, 16])
                K.tt("pool", rt[0].t[:], qf.t[:, :, 64:80], cosb, ALU.mult, r=[qf, rcos], w=[rt[0]])
                K.tt("pool", rt[1].t[:], qf.t[:, :, 80:96], sinb, ALU.mult, r=[qf, rsin], w=[rt[1]])
                K.tt("dve", Kb.t[:, :, 64:96], rk2.t[:, :].unsqueeze(1).to_broadcast([128, 8, 32]),
                     ssh.t[:, 8:16].unsqueeze(2).to_broadcast([128, 8, 32]), ALU.mult, r=[rk2, ssh], w=[Kb])
                yield
                K.tt("pool", Qb.t[:, :, 64:80], rt[0].t[:], rt[1].t[:], ALU.subtract, r=[rt[0], rt[1]], w=[Qb])
                bk = bP[1]
                bvk = bk.t[:].bitcast(BF16)
                for h in range(8):
                    K.tr(bvk[0:96, h * 128:(h + 1) * 128], Kb.t[:, h, :], ident.t[:],
                         r=[Kb, ident], w=[bk], sig=(h == 7))
                yield
                K.tt("pool", rt[0].t[:], qf.t[:, :, 80:96], cosb, ALU.mult, r=[qf, rcos], w=[rt[0]])
                K.tt("pool", rt[1].t[:], qf.t[:, :, 64:80], sinb, ALU.mult, r=[qf, rsin], w=[rt[1]])
                K.copy("dve", KT.t[0:96, :, t * 128:(t + 1) * 128], bvk[0:96, :].rearrange("p (h t) -> p h t", h=8),
                       r=[bk], w=[KTb[t]])
                yield
                K.tt("pool", Qb.t[:, :, 80:96], rt[0].t[:], rt[1].t[:], ALU.add, r=[rt[0], rt[1]], w=[Qb])
                yield
                qt = QT[g % 2]
                bk = bP[0]
                bvq = bk.t[:].bitcast(BF16)
                for h in range(8):
                    K.tr(bvq[0:96, h * 128:(h + 1) * 128], Qb.t[:, h, :], ident.t[:],
                         r=[Qb, ident], w=[bk], sig=(h == 7))
                yield
                K.copy("dve", qt.t[0:96, :, :], bvq[0:96, :].rearrange("p (h t) -> p h t", h=8), r=[bk], w=[qt])

            def F_gen(g):
                for hb in range(2):
                    ov = obk[hb].t[:, 0:260].rearrange("p (h d) -> p h d", h=4)
                    K.op("dve", lambda e, ov=ov, hb=hb: e.reciprocal(out=rec.t[:, hb * 4:(hb + 1) * 4], in_=ov[:, :, 64]),
                         r=[obk[hb]], w=[rec])
                    K.tt("dve", attn.t[:, hb * 4:(hb + 1) * 4, :], ov[:, :, 0:64],
                         rec.t[:, hb * 4:(hb + 1) * 4].unsqueeze(2).to_broadcast([128, 4, 64]), ALU.mult,
                         r=[obk[hb], rec], w=[attn])
                yield
                a2 = attn.t[:].rearrange("p h d -> p (h d)")
                K.act(junk.t[:, 0:512], a2, AF.Square, r=[attn], w=[junk, ssa], accum=ssa.t[:, 0:1])
                yield
                rstd(ssa, ssa.t[:, 0:1], ssa, ssa.t[:, 0:1], 512, 1)
                yield
                K.act(attb.t[:], a2, AF.Copy, r=[attn, ssa], w=[attb], scale=ssa.t[:, 0:1])
                yield
                at = attT[g % 8]
                bk = bM
                bv = bk.t[:].bitcast(BF16)
                for k in range(4):
                    K.tr(bv[:, k * 128:(k + 1) * 128], attb.t[:, k * 128:(k + 1) * 128], ident.t[:],
                         r=[attb, ident], w=[bk], sig=(k == 3))
                K.copy("dve", at.t[:], bv[:, 0:512].rearrange("p (k t) -> p k t", k=4), r=[bk], w=[at])

            def L_gen(G, c):
                hT = hnT[G % 2]
                yb = ybf[G % 2]
                sl = ssl[G % 2]
                if G % (NT // 4) == 0 and c == 0:
                    K.memset("pool", xlb.t[:, :, 0:3], 0.0, w=[xlb])
                    K.memset("pool", hst.t[:], 0.0, w=[hst])
                bx = bL
                for k in range(8):
                    K.mm(bx.t[:, :], Win.t[:, k, 416 + c * 128:416 + (c + 1) * 128], hT.t[:, k, :],
                         k == 0, k == 7, r=[Win, hT], w=[bx], sig=(k == 7))
                yield
                K.copy("act", xlb.t[:, c, 3:515], bx.t[:, :], r=[bx], w=[xlb])
                yield
                bg = bL
                for k in range(8):
                    K.mm(bg.t[:, :], Win.t[:, k, 928 + c * 128:928 + (c + 1) * 128], hT.t[:, k, :],
                         k == 0, k == 7, r=[Win, hT], w=[bg], sig=(k == 7))
                K.ts("dve", lxc.t[:], xlb.t[:, c, 3:515], col(C_LCW + c * 4 + 3), col(C_LCB + c),
                     ALU.mult, ALU.add, r=[xlb, cols], w=[lxc])
                yield
                K.act(li_.t[:], bg.t[:, :], AF.Gelu_apprx_tanh, r=[bg], w=[li_])
                for jj in range(3):
                    K.stt(lxc.t[:], xlb.t[:, c, jj:jj + 512], col(C_LCW + c * 4 + jj), lxc.t[:],
                          ALU.mult, ALU.add, r=[xlb, cols, lxc], w=[lxc])
                yield
                K.copy("pool", xlb.t[:, c, 0:3], xlb.t[:, c, 512:515], r=[xlb], w=[xlb])
                K.copy("pool", lxb.t[:], lxc.t[:], r=[lxc], w=[lxb])
                yield
                br = bL
                K.mm(br.t[:, :], Wr.t[:, c, :], lxb.t[:], True, True, r=[Wr, lxb], w=[br], sig=True)
                yield
                K.act(lr_.t[:], br.t[:, :], AF.Sigmoid, r=[br, cols], w=[lr_], bias=col(C_BR + c))
                yield
                bi = bL
                K.mm(bi.t[:, :], Wi.t[:, c, :], lxb.t[:], True, True, r=[Wi, lxb], w=[bi], sig=True)
                yield
                K.act(lm_.t[:], bi.t[:, :], AF.Sigmoid, r=[bi, cols], w=[lm_], bias=col(C_BI + c))
                K.act(lr_.t[:], lr_.t[:], AF.Exp, r=[lr_, clam], w=[lr_], scale=clam.t[:, c:c + 1])
                yield
                K.tt("dve", lm_.t[:], lm_.t[:], lxc.t[:], ALU.mult, r=[lm_, lxc], w=[lm_])
                yield
                K.act(lxc.t[:], lr_.t[:], AF.Square, r=[lr_], w=[lxc])
                K.act(lxc.t[:], lxc.t[:], AF.Sqrt, r=[lxc], w=[lxc], scale=-1.0, bias=1.0)
                yield
                K.tt("dve", lm_.t[:], lm_.t[:], lxc.t[:], ALU.mult, r=[lm_, lxc], w=[lm_])
                yield
                K.op("dve", lambda e, c=c: e.tensor_tensor_scan(out=lxc.t[:], data0=lr_.t[:], data1=lm_.t[:],
                                                                 initial=hst.t[:, c:c + 1], op0=ALU.mult, op1=ALU.add),
                     r=[lr_, lm_, hst], w=[lxc])
                yield
                K.copy("pool", hst.t[:, c:c + 1], lxc.t[:, 511:512], r=[lxc], w=[hst])
                K.tt("dve", lxc.t[:], lxc.t[:], li_.t[:], ALU.mult, r=[lxc, li_], w=[lxc])
                yield
                K.copy("pool", yb.t[:, c, :], lxc.t[:], r=[lxc], w=[yb])
                K.act(lm_.t[:], lxc.t[:], AF.Square, r=[lxc], w=[lm_])
                yield
                bs_ = bL
                for j in range(4):
                    K.mm(bs_.t[:, (c * 4 + j) * 2:(c * 4 + j) * 2 + 2], lm_.t[:, j * 128:(j + 1) * 128], ones2.t[:],
                         True, True, r=[lm_, ones2], w=[bs_], sig=(j == 3))
                yield
                slv = sl.t[:].rearrange("p j o -> p (j o)")
                if c == 0:
                    K.copy("dve", slv, bs_.t[:, 0:8], r=[bs_], w=[sl])
                else:
                    K.tt("dve", slv, slv, bs_.t[:, c * 8:c * 8 + 8], ALU.add, r=[bs_, sl], w=[sl])
                if c == 3:
                    yield
                    rstd(sl, slv, sl, slv, 512, 8)

            def O_gen(G, j):
                g = G * 4 + j
                r0 = g * 128
                at = attT[g % 8]
                yb = ybf[G % 2]
                sl = ssl[G % 2]
                xt = load_x(x_d[r0:r0 + 128, :])
                yield
                for hh in range(2):
                    ba = bM
                    for k in range(4):
                        K.mm(ba.t[:, :], at.t[:, k, :], Wout.t[:, k, hh * 512:(hh + 1) * 512], k == 0, k == 3,
                             r=[at, Wout], w=[ba], sig=(k == 3))
                    K.tt("dve", x1o.t[:, hh * 512:(hh + 1) * 512], ba.t[:, :], xt.t[:, hh * 512:(hh + 1) * 512], ALU.add,
                         r=[ba, xt], w=[x1o])
                    yield
                    bl = bM
                    for k in range(4):
                        K.mm(bl.t[:, :], yb.t[:, k, j * 128:(j + 1) * 128], Wout.t[:, 4 + k, hh * 512:(hh + 1) * 512],
                             k == 0, k == 3, r=[yb, Wout], w=[bl], sig=(k == 3))
                    K.stt(x1o.t[:, hh * 512:(hh + 1) * 512], bl.t[:, :], sl.t[:, j, 0:1], x1o.t[:, hh * 512:(hh + 1) * 512],
                          ALU.mult, ALU.add, r=[bl, sl, x1o], w=[x1o])
                    yield
                K.dma("pool", x1_d[r0:r0 + 128, :], x1o.t[:], x1_sem, r=[x1o])

            EST = {"N": 5, "P": 26, "F": 5, "L": 17, "O": 6}
            streams = []

            def add_stream(kind, gen_):
                streams.append([gen_, EST[kind]])

            def pump(n):
                k = 0
                while k < n and streams:
                    ent = streams.pop(0)
                    try:
                        next(ent[0])
                        ent[1] = max(ent[1] - 1, 1)
                        streams.append(ent)
                    except StopIteration:
                        pass
                    k += 1

            def drain():
                while streams:
                    pump(1)

            def run_all(gen_):
                for _ in gen_:
                    pass

            def A_tile(g):
                t = g % NT
                qt = QT[g % 2]
                groups = [(h, g0, min(4, t + 1 - g0)) for h in range(8) for g0 in range(0, t + 1, 4)]
                sbs = {}

                def qk(i):
                    h, g0, n = groups[i]
                    sb_ = sbk.next()
                    sbs[i] = sb_
                    for ii in range(n):
                        kt = g0 + ii
                        K.mm(sb_.t[:, ii * 128:(ii + 1) * 128], KT.t[0:96, h, kt * 128:(kt + 1) * 128],
                             qt.t[0:96, h, :], True, True, r=[KTb[kt], qt], w=[sb_], sig=(ii == n - 1))

                qk(0)
                ng = len(groups)
                for i, (h, g0, n) in enumerate(groups):
                    if i + 1 < ng:
                        qk(i + 1)
                    sb_ = sbs.pop(i)
                    ob = obk[h // 4]
                    oap = ob.t[:, (h % 4) * 65:(h % 4) * 65 + 65]
                    pt = PT.next()
                    K.act(pt.t[:, 0:n, :].rearrange("p a b -> p (a b)"), sb_.t[:, 0:n * 128], AF.Exp,
                          r=[sb_], w=[pt])
                    if g0 + n - 1 == t:
                        K.memset("pool", pt.t[64:128, n - 1, 0:64], 0.0, w=[pt])
                    for ii in range(n):
                        kt = g0 + ii
                        K.mm(oap, pt.t[:, ii, :], VA.t[:, kt, h, :], kt == 0, kt == t,
                             r=[pt, VAb[kt]], w=[ob], sig=(ii == n - 1))
                    rem = sum(e[1] for e in streams)
                    left = ng - i
                    pump((rem + left - 1) // left)

            for jj in range(4):
                run_all(N_gen(0, jj))
            run_all(P_gen(0))
            for g in range(NG):
                G, j = g // 4, g % 4
                if g + 1 < NG and (g + 1) % NT != 0:
                    add_stream("P", P_gen(g + 1))
                add_stream("L", L_gen(G, j))
                if G >= 1:
                    add_stream("O", O_gen(G - 1, j))
                if G + 1 < NSG and j in (1, 2):
                    add_stream("N", N_gen(G + 1, 2 * (j - 1)))
                    add_stream("N", N_gen(G + 1, 2 * (j - 1) + 1))
                A_tile(g)
                drain()
                fg = F_gen(g)
                next(fg)
                add_stream("F", fg)
                if g + 1 < NG and (g + 1) % NT == 0:
                    drain()
                    run_all(P_gen(g + 1))
            drain()
            for j in range(4):
                run_all(O_gen(NSG - 1, j))
            x1_done = Tok(None, x1_sem.h, x1_sem.val)
            K.barrier([x1_done])
            K.flush()

        with ExitStack() as es:
            Wup = K.sb(es, "Wup", [128, 8, 2 * DFF], BF16)
            Wdn = K.sb(es, "Wdn", [128, NCH, D], BF16)
            stage = [K.sb(es, "fstg%d" % i, [128, 1024], F32) for i in range(2)]
            stage_sem = [K.dsem("fstg%d" % i) for i in range(2)]
            wup_b = [Buf("wup%d" % c) for c in range(2 * NCH)]
            wdn_b = [Buf("wdn%d" % c) for c in range(NCH)]
            w_up_v = w_up_d.rearrange("(k p) n -> p k n", p=128)
            w_dn_v = w_down_d.rearrange("(c p) n -> p c n", p=128)
            si = [0]
            first = [True]

            def fl(dst_b, dst_ap, src_ap, n, scale_col, eng):
                i = si[0] % 2
                si[0] += 1
                st_ = stage[i]
                K.dma("sp", st_.t[:, 0:n], src_ap, stage_sem[i], w=[st_])
                if scale_col is not None:
                    K.ts("pool", dst_ap, st_.t[:, 0:n], scale_col, 0.0, ALU.mult, ALU.add, r=[st_, cols], w=[dst_b])
                else:
                    K.copy(eng, dst_ap, st_.t[:, 0:n], r=[st_], w=[dst_b])

            for c in range(NCH):
                for half in range(2):
                    cc = half * NCH + c
                    c0 = half * DFF + c * 128
                    i = si[0] % 2
                    si[0] += 1
                    st_ = stage[i]
                    K.dma("sp", st_.t[:, :].rearrange("p (k n) -> p k n", k=8), w_up_v[:, :, c0:c0 + 128], stage_sem[i], w=[st_])
                    K.tt("pool", Wup.t[:, :, c0:c0 + 128], st_.t[:, :].rearrange("p (k n) -> p k n", k=8),
                         cols.t[:, C_GFFN:C_GFFN + 8].unsqueeze(2).to_broadcast([128, 8, 128]), ALU.mult,
                         r=[st_, cols], w=[wup_b[cc]])
                fl(wdn_b[c], Wdn.t[:, c, :], w_dn_v[:, c, :], 1024, None, "pool")

            NX = 4
            xr = [K.sb(es, "fx%d" % i, [128, D], F32) for i in range(NX)]
            xr_sem = [K.dsem("fxs%d" % i) for i in range(NX)]
            xo_sem = [K.dsem("fxo%d" % i) for i in range(NX)]
            xi = [0]
            junk = K.sb(es, "fjunk", [128, D], BF16)
            hf = K.sb(es, "hf", [128, D], BF16)
            hfT = [K.sb(es, "hfT%d" % i, [128, 8, WIN + 2], BF16) for i in range(2)]
            ssf = K.sb(es, "ssf", [128, 4], F32)
            cg = [K.sb(es, "cg%d" % i, [128, WIN], F32) for i in range(2)]
            cv = [K.sb(es, "cv%d" % i, [128, WIN], F32) for i in range(2)]
            G = [K.sb(es, "G%d" % i, [128, NCH, WIN], BF16) for i in range(2)]
            upr = Ring(banks[0:4])
            dnr = Ring(banks[4:8])
            nwin = TOK // WIN
            wx = {}

            def FA(wdw):
                tok0 = wdw * WIN
                ht = hfT[wdw % 2]
                hprev = hfT[(wdw + 1) % 2]
                if wdw % (S // WIN) == 0:
                    K.memset("pool", ht.t[:, :, 0:2], 0.0, w=[ht])
                else:
                    K.copy("pool", ht.t[:, :, 0:2], hprev.t[:, :, WIN:WIN + 2], r=[hprev], w=[ht])
                xs_ = []
                for j in range(2):
                    i = xi[0] % NX
                    xi[0] += 1
                    xt = xr[i]
                    r0 = tok0 + j * 128
                    K.dma("sp", xt.t[:], x1_d[r0:r0 + 128, :], xr_sem[i], w=[xt])
                    if first[0]:
                        K.q["sp"].ops[-1][0].append((x1_done.sem, x1_done.val))
                        first[0] = False
                    xs_.append((xt, i))
                    cc_ = (wdw % 2) * 2 + j
                    K.act(junk.t[:], xt.t[:], AF.Square, r=[xt], w=[junk, ssf], accum=ssf.t[:, cc_:cc_ + 1])
                    rstd(ssf, ssf.t[:, cc_:cc_ + 1], ssf, ssf.t[:, cc_:cc_ + 1], D, 1)
                    K.act(hf.t[:], xt.t[:], AF.Copy, r=[xt, ssf], w=[hf], scale=ssf.t[:, cc_:cc_ + 1])
                    bk = dnr.next()
                    bv = bk.t[:].bitcast(BF16)
                    for k in range(8):
                        K.tr(bv[:, k * 128:(k + 1) * 128], hf.t[:, k * 128:(k + 1) * 128], ident.t[:],
                             r=[hf, ident], w=[bk], sig=(k == 7))
                    K.copy("dve", ht.t[:, :, 2 + j * 128:2 + (j + 1) * 128],
                           bv.rearrange("p (k t) -> p k t", k=8), r=[bk], w=[ht])
                wx[wdw] = xs_

            FA(0)
            for wdw in range(nwin):
                tok0 = wdw * WIN
                ht = hfT[wdw % 2]
                xs_ = wx.pop(wdw)
                Gw = G[wdw % 2]
                for c in range(NCH):
                    if c == 8 and wdw + 1 < nwin:
                        FA(wdw + 1)
                    bg_, bv_ = upr.next(), upr.next()
                    for half, bkk in ((0, bg_), (1, bv_)):
                        c0 = half * DFF + c * 128
                        for k in range(8):
                            K.mm(bkk.t[:, 0:WIN + 2], Wup.t[:, k, c0:c0 + 128], ht.t[:, k, :], k == 0, k == 7,
                                 r=[wup_b[half * NCH + c], ht], w=[bkk], sig=(k == 7))
                    cgt, cvt = cg[c % 2], cv[c % 2]
                    for half, bkk, dst in ((0, bg_, cgt), (1, bv_, cvt)):
                        ci = half * NCH + c
                        K.act(dst.t[:], bkk.t[:, 2:WIN + 2], AF.Identity, r=[bkk, cols], w=[dst],
                              scale=col(C_FCW + ci * 3 + 2), bias=col(C_FCB + ci))
                        K.stt(dst.t[:], bkk.t[:, 1:WIN + 1], col(C_FCW + ci * 3 + 1), dst.t[:], ALU.mult, ALU.add,
                              r=[bkk, cols, dst], w=[dst])
                        K.stt(dst.t[:], bkk.t[:, 0:WIN], col(C_FCW + ci * 3 + 0), dst.t[:], ALU.mult, ALU.add,
                              r=[bkk, cols, dst], w=[dst])
                    K.act(cgt.t[:], cgt.t[:], AF.Gelu_apprx_tanh, r=[cgt], w=[cgt])
                    K.tt("pool", Gw.t[:, c, :], cgt.t[:], cvt.t[:], ALU.mult, r=[cgt, cvt], w=[Gw])
                for j in range(2):
                    xt, i = xs_[j]
                    r0 = tok0 + j * 128
                    for hh in range(2):
                        bd = dnr.next()
                        for c in range(NCH):
                            K.mm(bd.t[:, :], Gw.t[:, c, j * 128:(j + 1) * 128], Wdn.t[:, c, hh * 512:(hh + 1) * 512],
                                 c == 0, c == NCH - 1, r=[Gw, wdn_b[c]], w=[bd], sig=(c == NCH - 1))
                        K.tt("dve", xt.t[:, hh * 512:(hh + 1) * 512], bd.t[:, :], xt.t[:, hh * 512:(hh + 1) * 512], ALU.add,
                             r=[bd, xt], w=[xt])
                    K.dma("pool", x2_d[r0:r0 + 128, :], xt.t[:], xo_sem[i], r=[xt])
            x2_toks = [Tok(None, sm.h, sm.val) for sm in xo_sem]
            K.barrier(x2_toks)
            K.flush()

        with ExitStack() as es:
            Wpg = K.sb(es, "Wpg", [128, 8, D], BF16)
            Wpl = K.sb(es, "Wpl", [128, 2, D], BF16)
            brow = K.sb(es, "brow", [128, D], BF16)
            one0 = K.sb(es, "one0", [128, 128], BF16)
            browf = K.sb(es, "browf", [128, D], F32)
            stage = [K.sb(es, "pstg%d" % i, [128, 1024], F32) for i in range(2)]
            stage_sem = [K.dsem("pstg%d" % i) for i in range(2)]
            si = [0]
            w_pg_v = w_pg_d.rearrange("(k p) n -> p k n", p=128)
            w_pl_v = w_ple_d.rearrange("(k p) n -> p k n", p=128)
            for k in range(8):
                i = si[0] % 2
                si[0] += 1
                K.dma("sp", stage[i].t[:], w_pg_v[:, k, :], stage_sem[i], w=[stage[i]])
                K.act(Wpg.t[:, k, :], stage[i].t[:], AF.Copy, r=[stage[i], cols], w=[Wpg], scale=col(C_GPLE + k))
            for k in range(2):
                i = si[0] % 2
                si[0] += 1
                K.dma("sp", stage[i].t[:], w_pl_v[:, k, :], stage_sem[i], w=[stage[i]])
                K.copy("act", Wpl.t[:, k, :], stage[i].t[:], r=[stage[i]], w=[Wpl])
            K.memset("pool", browf.t[:], 0.0, w=[browf])
            K.memset("pool", one0.t[:], 0.0, w=[one0])
            K.memset("pool", one0.t[0:1, :], 1.0, w=[one0])
            bsem = K.dsem("brow")
            K.dma("sp", browf.t[0:1, :], bple_d[0:1, :], bsem, w=[browf])
            K.copy("pool", brow.t[:], browf.t[:], r=[browf], w=[brow])

            NX = 8
            xr = [K.sb(es, "px%d" % i, [128, D], F32) for i in range(NX)]
            xr_sem = [K.dsem("pxs%d" % i) for i in range(NX)]
            xo_sem = [K.dsem("pxo%d" % i) for i in range(NX)]
            pr = [K.sb(es, "pp%d" % i, [128, 256], F32) for i in range(NX)]
            pr_sem = [K.dsem("pps%d" % i) for i in range(NX)]
            junk = K.sb(es, "pjunk", [128, D], BF16)
            hp = [K.sb(es, "hp%d" % i, [128, D], BF16) for i in range(2)]
            pb = [K.sb(es, "pb%d" % i, [128, 256], BF16) for i in range(2)]
            hpT = [K.sb(es, "hpT%d" % i, [128, 10, 128], BF16) for i in range(3)]
            ssp = K.sb(es, "ssp", [128, 4], F32)
            sg = [K.sb(es, "sg%d" % i, [128, 512], F32) for i in range(4)]
            pring = Ring(banks)
            out_toks = []
            NTI = TOK // 128

            def PL(ti):
                r0 = ti * 128
                i = ti % NX
                xt = xr[i]
                K.dma("sp", xt.t[:], x2_d[r0:r0 + 128, :], xr_sem[i], w=[xt])
                if ti == 0:
                    for tk in x2_toks:
                        K.q["sp"].ops[-1][0].append((tk.sem, tk.val))
                pt_ = pr[i]
                K.dma("sp", pt_.t[:], p_d[r0:r0 + 128, :], pr_sem[i], w=[pt_])

            def PA(ti):
                r0 = ti * 128
                i = ti % NX
                xt = xr[i]
                pt_ = pr[i]
                cc_ = ti % 4
                hp_ = hp[ti % 2]
                pb_ = pb[ti % 2]
                K.act(junk.t[:], xt.t[:], AF.Square, r=[xt], w=[junk, ssp], accum=ssp.t[:, cc_:cc_ + 1])
                rstd(ssp, ssp.t[:, cc_:cc_ + 1], ssp, ssp.t[:, cc_:cc_ + 1], D, 1)
                K.act(hp_.t[:], xt.t[:], AF.Copy, r=[xt, ssp], w=[hp_], scale=ssp.t[:, cc_:cc_ + 1])
                K.copy("pool", pb_.t[:], pt_.t[:], r=[pt_], w=[pb_])
                hT = hpT[ti % 3]
                bk = pring.next()
                bv = bk.t[:].bitcast(BF16)
                for k in range(8):
                    K.tr(bv[:, k * 128:(k + 1) * 128], hp_.t[:, k * 128:(k + 1) * 128], ident.t[:],
                         r=[hp_, ident], w=[bk], sig=(k == 7))
                K.copy("dve", hT.t[:, 0:8, :], bv.rearrange("p (k t) -> p k t", k=8), r=[bk], w=[hT])
                bk = pring.next()
                bv = bk.t[:].bitcast(BF16)
                for k in range(2):
                    K.tr(bv[:, k * 128:(k + 1) * 128], pb_.t[:, k * 128:(k + 1) * 128], ident.t[:],
                         r=[pb_, ident], w=[bk], sig=(k == 1))
                K.copy("dve", hT.t[:, 8:10, :], bv[:, 0:256].rearrange("p (k t) -> p k t", k=2), r=[bk], w=[hT])

            def PB(ti):
                r0 = ti * 128
                i = ti % NX
                xt = xr[i]
                hT = hpT[ti % 3]
                for hh in range(2):
                    bgt, be = pring.next(), pring.next()
                    for k in range(8):
                        K.mm(bgt.t[:, :], hT.t[:, k, :], Wpg.t[:, k, hh * 512:(hh + 1) * 512], k == 0, False,
                             r=[hT, Wpg], w=[bgt], sig=False)
                    K.mm(bgt.t[:, :], one0.t[:], brow.t[:, hh * 512:(hh + 1) * 512], False, True,
                         r=[one0, brow], w=[bgt], sig=True)
                    for k in range(2):
                        K.mm(be.t[:, :], hT.t[:, 8 + k, :], Wpl.t[:, k, hh * 512:(hh + 1) * 512], k == 0, k == 1,
                             r=[hT, Wpl], w=[be], sig=(k == 1))
                    sgt = sg[(ti * 2 + hh) % 4]
                    K.act(sgt.t[:], bgt.t[:, :], AF.Sigmoid, r=[bgt], w=[sgt])
                    K.tt("dve", sgt.t[:], sgt.t[:], be.t[:, :], ALU.mult, r=[sgt, be], w=[sgt])
                    K.tt("pool", xt.t[:, hh * 512:(hh + 1) * 512], xt.t[:, hh * 512:(hh + 1) * 512], sgt.t[:], ALU.add,
                         r=[xt, sgt], w=[xt])
                out_toks.append(K.dma("pool", out_d[r0:r0 + 128, :], xt.t[:], xo_sem[i], r=[xt]))

            for ti in range(5):
                PL(ti)
            PA(0)
            PA(1)
            for ti in range(NTI):
                if ti + 5 < NTI:
                    PL(ti + 5)
                if ti + 2 < NTI:
                    PA(ti + 2)
                PB(ti)
            K.flush(final_toks=out_toks[-NX:])
    return nc


def _host_inputs(inputs):
    f = lambda a: np.ascontiguousarray(np.asarray(a, dtype=np.float32))
    x = f(inputs["x"])
    p = f(inputs["p"])[0]
    pos = np.asarray(inputs["positions"]).astype(np.int32)

    def colz(v):
        v = f(v).reshape(-1)
        return v.reshape(-1, 128).T

    cols = np.zeros((128, NCOL), np.float32)
    cols[:, C_GMIX:C_GMIX + 8] = colz(inputs["g_mix"][0])
    cols[:, C_GFFN:C_GFFN + 8] = colz(inputs["g_ffn"][0])
    cols[:, C_GPLE:C_GPLE + 8] = colz(inputs["g_ple"][0])
    cols[:, C_GCQ:C_GCQ + 2] = colz(inputs["g_cq"][0])
    cols[:, C_GCKV:C_GCKV + 1] = colz(inputs["g_ckv"][0])
    cols[:, C_GOUT:C_GOUT + 4] = colz(inputs["g_attn_out"][0])
    cols[:, C_GOUT + 4:C_GOUT + 8] = colz(inputs["g_lru_out"][0])
    lcw = f(inputs["w_lru_conv"][0])
    cols[:, C_LCW:C_LCW + 16] = lcw.reshape(4, 4, 128).transpose(2, 1, 0).reshape(128, 16)
    cols[:, C_LCB:C_LCB + 4] = colz(inputs["b_lru_conv"][0])
    cols[:, C_BR:C_BR + 4] = colz(inputs["b_lru_r"][0])
    cols[:, C_BI:C_BI + 4] = colz(inputs["b_lru_i"][0])
    cols[:, C_LAM:C_LAM + 4] = colz(inputs["lru_lambda"][0])
    fcw = f(inputs["w_ffn_conv"][0])
    cols[:, C_FCW:C_FCW + 132] = fcw.reshape(3, 44, 128).transpose(2, 1, 0).reshape(128, 132)
    cols[:, C_FCB:C_FCB + 44] = colz(inputs["b_ffn_conv"][0])
    rows = np.zeros((1, NROW), np.float32)
    rows[0, R_GQ:R_GQ + 96] = f(inputs["g_qn"][0])
    rows[0, R_GK:R_GK + 96] = f(inputs["g_kn"][0])
    rows[0, R_IF:R_IF + 16] = (10000.0 ** (-np.arange(16, dtype=np.float32) / np.float32(16))).astype(np.float32)
    shared = {
        "w_in": f(inputs["w_in"][0]), "w_uq": f(inputs["w_uq"][0]), "w_ukv": f(inputs["w_ukv"][0]),
        "w_out": f(inputs["w_out"][0]), "w_up": f(inputs["w_up"][0]), "w_down": f(inputs["w_down"][0]),
        "w_pg": f(inputs["w_ple_gate"][0]), "w_ple": f(inputs["w_ple"][0]),
        "w_r": f(inputs["w_lru_r"][0]).reshape(512, 64), "w_i": f(inputs["w_lru_i"][0]).reshape(512, 64),
        "cols": cols, "rows": rows, "bple": f(inputs["b_ple_gate"][0]).reshape(1, D),
        "ident": np.eye(128, dtype=np.float32).astype(ml_dtypes.bfloat16),
    }
    maps = []
    for c in range(NCORES):
        m = dict(shared)
        m["x"] = np.ascontiguousarray(x[2 * c:2 * c + 2].reshape(TOK, D))
        m["p"] = np.ascontiguousarray(p[2 * c:2 * c + 2].reshape(TOK, 256))
        pc = pos[2 * c:2 * c + 2]
        m["posT"] = np.ascontiguousarray(pc.reshape(2, NT, 128).transpose(2, 0, 1).reshape(128, 2 * NT))
        maps.append(m)
    return maps


def kernel(**inputs):
    maps = _host_inputs(inputs)
    nc = build_nc()
    res = run_bass_kernel_spmd(nc, maps, core_ids=list(range(NCORES)))
    outs = [np.asarray(r["out"], dtype=np.float32).reshape(2, S, D) for r in res.results]
    return np.concatenate(outs, axis=0)
```
